# Optimizing a Trainium2 kernel written in Bass

```python
import math
import jax
import jax.numpy as jnp
from jax import lax
import numpy as np

D_MODEL = 1024
BATCH = 16
SEQ = 2048
DEPTH = 2

CTX_LEN = 256
GRID_W = 64
N_MIXERS = 2
D_FF = 4 * D_MODEL
S5_GROUP_CH = 16
S5_GROUPS = D_MODEL // S5_GROUP_CH
S5_STATE = 64
S5_DT_MIN = 1e-3
S5_DT_MAX = 1e-1
GDN_KEY_HEADS = 8
GDN_VALUE_HEADS = 16
GDN_HEAD_DIM = 128
GDN_QK_DIM = GDN_KEY_HEADS * GDN_HEAD_DIM
GDN_V_DIM = GDN_VALUE_HEADS * GDN_HEAD_DIM
GDN_CHUNK = 64
GDN_CONV = 5
GDN_IN_DIM = 2 * GDN_QK_DIM + 2 * GDN_V_DIM + 4 * GDN_VALUE_HEADS
N_S5_LAYERS = (DEPTH + 1) // 2
N_GDN_LAYERS = DEPTH // 2
NORM_EPS = 1e-6

kernel_name = 'hybrid_s5_gated_deltanet_dit'


def _rmsnorm(t, w):
    t32 = t.astype(jnp.float32)
    y = t32 * lax.rsqrt(jnp.mean(jnp.square(t32), axis=-1, keepdims=True) + NORM_EPS)
    return (y * w.astype(jnp.float32)).astype(t.dtype)


def _modulate(h, shift, scale):
    return h * (1 + scale) + shift


def _sqrelu_mlp(h, w1, w2):
    return jnp.square(jax.nn.relu(h @ w1)) @ w2


def _depthwise_conv(t, w):
    k = w.shape[0]
    return lax.conv_general_dilated(
        t, w[:, None, :].astype(t.dtype), window_strides=(1,),
        padding=[(k // 2, k // 2)], dimension_numbers=('NWC', 'WIO', 'NWC'),
        feature_group_count=t.shape[-1])


def _l2norm(t):
    t32 = t.astype(jnp.float32)
    return t32 * lax.rsqrt(jnp.sum(jnp.square(t32), axis=-1, keepdims=True) + NORM_EPS)


def _linear_recurrence(left, right):
    a_i, b_i = left
    a_j, b_j = right
    return a_j * a_i, a_j * b_i + b_j


def _s5_direction(u, a_bar, b_bar, c_mat, h0, reverse, need_out):
    bu = jnp.einsum('blgh,gph->lbgp', u.astype(jnp.complex64), b_bar)
    if reverse:
        bu = jnp.flip(bu, axis=0)
    if h0 is not None:
        bu = bu.at[0].add(a_bar * h0)
    a = jnp.broadcast_to(a_bar, (bu.shape[0], 1) + a_bar.shape)
    _, h = lax.associative_scan(_linear_recurrence, (a, bu), axis=0)
    final = h[-1]
    if not need_out:
        return None, final
    y = jnp.einsum('lbgp,ghp->blgh', h, c_mat).real
    if reverse:
        y = jnp.flip(y, axis=1)
    return y, final


def _s5_mixer(h_ctx, h_lat, lam_re, lam_im, log_dt, b_re, b_im, c_re, c_im, d_skip, w_glu, need_ctx_out):
    f32 = jnp.float32
    lam = lax.complex(lam_re.astype(f32), lam_im.astype(f32))
    dt = jnp.exp(log_dt.astype(f32))[..., None]
    a_bar = jnp.exp(lam * dt)
    b_bar = ((a_bar - 1) / lam)[..., None] * lax.complex(b_re.astype(f32), b_im.astype(f32))
    c_mat = lax.complex(c_re.astype(f32), c_im.astype(f32))

    def groups(t):
        return t.astype(f32).reshape(t.shape[0], t.shape[1], S5_GROUPS, S5_GROUP_CH)

    u_c, u_l = groups(h_ctx), groups(h_lat)
    yc_f, st_f = _s5_direction(u_c, a_bar[0], b_bar[0], c_mat[0], None, False, need_ctx_out)
    yc_b, st_b = _s5_direction(u_c, a_bar[1], b_bar[1], c_mat[1], None, True, need_ctx_out)
    yl_f, _ = _s5_direction(u_l, a_bar[0], b_bar[0], c_mat[0], st_f, False, True)
    yl_b, _ = _s5_direction(u_l, a_bar[1], b_bar[1], c_mat[1], st_b, True, True)

    def post(y, u_in):
        y = y.reshape(u_in.shape) + d_skip.astype(f32) * u_in.astype(f32)
        y = jax.nn.gelu(y).astype(u_in.dtype)
        y_a, y_b = jnp.split(y @ w_glu, 2, axis=-1)
        return y_a * jax.nn.sigmoid(y_b)

    out_lat = post(yl_f + yl_b, h_lat)
    out_ctx = post(yc_f + yc_b, h_ctx) if need_ctx_out else None
    return out_ctx, out_lat


def _qk_heads(t):
    b, l, _ = t.shape
    t = _l2norm(t.reshape(b, l, GDN_KEY_HEADS, GDN_HEAD_DIM))
    return jnp.repeat(t, GDN_VALUE_HEADS // GDN_KEY_HEADS, axis=2)


def _gated_delta_chunked(q, k, v, beta, g, state0, with_output):
    out_dtype = v.dtype
    bsz, seq, nh, _ = k.shape
    nc = seq // GDN_CHUNK

    def blocks(t):
        t = t.astype(jnp.float32).reshape((bsz, nc, GDN_CHUNK, nh) + t.shape[3:])
        return jnp.moveaxis(jnp.moveaxis(t, 3, 2), 1, 0)

    k_b, v_b, beta_b = blocks(k), blocks(v), blocks(beta)
    g_b = jnp.cumsum(blocks(g), axis=-1)
    incl = jnp.tril(jnp.ones((GDN_CHUNK, GDN_CHUNK), dtype=bool))
    strict = jnp.tril(jnp.ones((GDN_CHUNK, GDN_CHUNK), dtype=bool), k=-1)
    decay = jnp.exp(jnp.where(incl, g_b[..., :, None] - g_b[..., None, :], -jnp.inf))
    k_beta = k_b * beta_b[..., None]
    lower = jnp.where(strict, jnp.einsum('nbhcd,nbhsd->nbhcs', k_beta, k_b) * decay, 0.0)
    eye = jnp.eye(GDN_CHUNK, dtype=jnp.float32)
    t_mat = lax.linalg.triangular_solve(lower + eye, jnp.broadcast_to(eye, lower.shape),
                                        left_side=True, lower=True, unit_diagonal=True)
    u = jnp.einsum('nbhcs,nbhse->nbhce', t_mat, v_b * beta_b[..., None])
    w = jnp.einsum('nbhcs,nbhsd->nbhcd', t_mat, k_beta * jnp.exp(g_b)[..., None])
    g_last = g_b[..., -1]
    k_end = k_b * jnp.exp(g_last[..., None] - g_b)[..., None]
    state0 = state0.astype(jnp.float32)

    def advance(state, w_i, u_i, kend_i, glast_i):
        v_new = u_i - jnp.einsum('bhcd,bhde->bhce', w_i, state)
        new_state = state * jnp.exp(glast_i)[..., None, None] + jnp.einsum('bhcd,bhce->bhde', kend_i, v_new)
        return v_new, new_state

    if not with_output:
        def step_state(state, xs):
            _, new_state = advance(state, *xs)
            return new_state, None
        state, _ = lax.scan(step_state, state0, (w, u, k_end, g_last))
        return None, state

    q_b = blocks(q)
    q_g = q_b * jnp.exp(g_b)[..., None]
    intra = jnp.where(incl, jnp.einsum('nbhcd,nbhsd->nbhcs', q_b, k_b) * decay, 0.0)

    def step(state, xs):
        w_i, u_i, kend_i, glast_i, qg_i, intra_i = xs
        v_new, new_state = advance(state, w_i, u_i, kend_i, glast_i)
        o_i = jnp.einsum('bhcd,bhde->bhce', qg_i, state) + jnp.einsum('bhcs,bhse->bhce', intra_i, v_new)
        return new_state, o_i

    state, o = lax.scan(step, state0, (w, u, k_end, g_last, q_g, intra))
    o = jnp.moveaxis(jnp.moveaxis(o, 0, 1), 2, 3).reshape(bsz, seq, nh, -1)
    return o.astype(out_dtype), state


def _flip_seq(t):
    return None if t is None else jnp.flip(t, axis=1)


def _gdn_bidirectional(q, k, v, beta, g, s_f, s_b, with_out):
    o_f, s_f = _gated_delta_chunked(q, k, v, beta[:, :, 0], g[:, :, 0], s_f, with_out)
    o_b, s_b = _gated_delta_chunked(_flip_seq(q), _flip_seq(k), _flip_seq(v),
                                    _flip_seq(beta[:, :, 1]), _flip_seq(g[:, :, 1]), s_b, with_out)
    o = o_f + _flip_seq(o_b) if with_out else None
    return o, s_f, s_b


def _gdn_mixer(h_ctx, h_lat, w_in, conv_w, a_log, dt_bias, onorm_w, w_out, need_ctx_out):
    f32 = jnp.float32
    bsz, seq, _ = h_lat.shape
    rows = seq // GRID_W
    qk, vd = GDN_QK_DIM, GDN_V_DIM

    def conv_lat(t, w):
        y = _depthwise_conv(t.reshape(bsz * rows, GRID_W, t.shape[-1]), w)
        return jax.nn.silu(y.reshape(t.shape))

    def conv_ctx(t, w):
        return jax.nn.silu(_depthwise_conv(t, w))

    def queries(q_raw, conv):
        return _qk_heads(conv(q_raw, conv_w[:, :qk])) * GDN_HEAD_DIM ** -0.5

    def keys_values_gates(rest, conv):
        b, l, _ = rest.shape
        kv = conv(rest[..., :qk + vd], conv_w[:, qk:])
        k = _qk_heads(kv[..., :qk])
        v = kv[..., qk:].reshape(b, l, GDN_VALUE_HEADS, GDN_HEAD_DIM)
        gl = rest[..., qk + vd:].astype(f32).reshape(b, l, 2, 2, GDN_VALUE_HEADS)
        beta = jax.nn.sigmoid(gl[:, :, 0])
        g = -jnp.exp(a_log.astype(f32)) * jax.nn.softplus(gl[:, :, 1] + dt_bias.astype(f32))
        return k, v, beta, g

    def readout(o, z):
        b, l = z.shape[:2]
        o32 = o.astype(f32)
        o32 = o32 * lax.rsqrt(jnp.mean(jnp.square(o32), axis=-1, keepdims=True) + NORM_EPS) * onorm_w.astype(f32)
        gated = o32 * jax.nn.silu(z.astype(f32)).reshape(o.shape)
        return gated.reshape(b, l, vd).astype(z.dtype) @ w_out

    zero_state = jnp.zeros((bsz, GDN_VALUE_HEADS, GDN_HEAD_DIM, GDN_HEAD_DIM), f32)
    if need_ctx_out:
        proj_c = h_ctx @ w_in
        q_c = queries(proj_c[..., :qk], conv_ctx)
        z_c = proj_c[..., qk:qk + vd]
        rest_c = proj_c[..., qk + vd:]
    else:
        q_c, z_c = None, None
        rest_c = h_ctx @ w_in[:, qk + vd:]
    k_c, v_c, beta_c, g_c = keys_values_gates(rest_c, conv_ctx)
    o_c, s_f, s_b = _gdn_bidirectional(q_c, k_c, v_c, beta_c, g_c, zero_state, zero_state, need_ctx_out)
    proj_l = h_lat @ w_in
    q_l = queries(proj_l[..., :qk], conv_lat)
    z_l = proj_l[..., qk:qk + vd]
    k_l, v_l, beta_l, g_l = keys_values_gates(proj_l[..., qk + vd:], conv_lat)
    o_l, _, _ = _gdn_bidirectional(q_l, k_l, v_l, beta_l, g_l, s_f, s_b, True)
    out_lat = readout(o_l, z_l)
    out_ctx = readout(o_c, z_c) if need_ctx_out else None
    return out_ctx, out_lat


def setup_inputs(seed: int = 0) -> dict:
    key = jax.random.key(seed)
    ks = jax.random.split(key, 32)
    f32 = jnp.float32

    def nrm(k, shape, scale):
        return jax.random.normal(k, shape, f32) * scale

    ns, ng = N_S5_LAYERS, N_GDN_LAYERS
    x = nrm(ks[0], (BATCH, SEQ, D_MODEL), 1.0)
    c = nrm(ks[1], (BATCH, D_MODEL), 1.0)
    ctx = nrm(ks[2], (BATCH, CTX_LEN, D_MODEL), 1.0)
    c_ctx = nrm(ks[3], (D_MODEL,), 1.0)
    ada_w = nrm(ks[4], (DEPTH, D_MODEL, 6 * D_MODEL), 0.5 * D_MODEL ** -0.5)
    ada_b = nrm(ks[5], (DEPTH, 6 * D_MODEL), 0.02)
    norm1_w = 1.0 + nrm(ks[6], (DEPTH, D_MODEL), 0.05)
    norm2_w = 1.0 + nrm(ks[7], (DEPTH, D_MODEL), 0.05)
    mlp_w1 = nrm(ks[8], (DEPTH, D_MODEL, D_FF), D_MODEL ** -0.5)
    mlp_w2 = nrm(ks[9], (DEPTH, D_FF, D_MODEL), D_FF ** -0.5)
    s5_lam_re = -0.5 * jnp.exp(nrm(ks[10], (ns, 2, S5_GROUPS, S5_STATE), 0.05))
    s5_lam_im = jnp.broadcast_to(math.pi * jnp.arange(S5_STATE, dtype=f32), (ns, 2, S5_GROUPS, S5_STATE))
    s5_log_dt = jax.random.uniform(ks[11], (ns, 2, S5_GROUPS), f32, math.log(S5_DT_MIN), math.log(S5_DT_MAX))
    s5_b_re = nrm(ks[12], (ns, 2, S5_GROUPS, S5_STATE, S5_GROUP_CH), 0.5 ** 0.5)
    s5_b_im = nrm(ks[13], (ns, 2, S5_GROUPS, S5_STATE, S5_GROUP_CH), 0.5 ** 0.5)
    s5_c_re = nrm(ks[14], (ns, 2, S5_GROUPS, S5_GROUP_CH, S5_STATE), (2 * S5_STATE) ** -0.5)
    s5_c_im = nrm(ks[15], (ns, 2, S5_GROUPS, S5_GROUP_CH, S5_STATE), (2 * S5_STATE) ** -0.5)
    s5_d = nrm(ks[16], (ns, D_MODEL), 1.0)
    s5_w_glu = nrm(ks[17], (ns, D_MODEL, 2 * D_MODEL), D_MODEL ** -0.5)
    gdn_w_in = nrm(ks[18], (ng, D_MODEL, GDN_IN_DIM), D_MODEL ** -0.5)
    gdn_conv_w = nrm(ks[19], (ng, GDN_CONV, 2 * GDN_QK_DIM + GDN_V_DIM), GDN_CONV ** -0.5)
    gdn_a_log = jnp.log(jax.random.uniform(ks[20], (ng, 2, GDN_VALUE_HEADS), f32, 1.0, 16.0))
    dt0 = jnp.exp(jax.random.uniform(ks[21], (ng, 2, GDN_VALUE_HEADS), f32, math.log(1e-3), math.log(1e-1)))
    gdn_dt_bias = dt0 + jnp.log(-jnp.expm1(-dt0))
    gdn_onorm_w = 1.0 + nrm(ks[22], (ng, GDN_HEAD_DIM), 0.05)
    gdn_w_out = nrm(ks[23], (ng, GDN_V_DIM, D_MODEL), GDN_V_DIM ** -0.5)
    final_norm_w = 1.0 + nrm(ks[24], (D_MODEL,), 0.05)
    return {
        'x': x, 'c': c, 'ctx': ctx, 'c_ctx': c_ctx,
        'ada_w': ada_w, 'ada_b': ada_b, 'norm1_w': norm1_w, 'norm2_w': norm2_w,
        'mlp_w1': mlp_w1, 'mlp_w2': mlp_w2,
        's5_lam_re': s5_lam_re, 's5_lam_im': s5_lam_im, 's5_log_dt': s5_log_dt,
        's5_b_re': s5_b_re, 's5_b_im': s5_b_im, 's5_c_re': s5_c_re, 's5_c_im': s5_c_im,
        's5_d': s5_d, 's5_w_glu': s5_w_glu,
        'gdn_w_in': gdn_w_in, 'gdn_conv_w': gdn_conv_w, 'gdn_a_log': gdn_a_log,
        'gdn_dt_bias': gdn_dt_bias, 'gdn_onorm_w': gdn_onorm_w, 'gdn_w_out': gdn_w_out,
        'final_norm_w': final_norm_w,
    }


def reference(x, c, ctx, c_ctx, ada_w, ada_b, norm1_w, norm2_w, mlp_w1, mlp_w2,
              s5_lam_re, s5_lam_im, s5_log_dt, s5_b_re, s5_b_im, s5_c_re, s5_c_im, s5_d, s5_w_glu,
              gdn_w_in, gdn_conv_w, gdn_a_log, gdn_dt_bias, gdn_onorm_w, gdn_w_out, final_norm_w):
    silu_c = jax.nn.silu(c)
    silu_cc = jax.nn.silu(c_ctx)
    for i in range(DEPTH):
        last = i == DEPTH - 1
        j = i // N_MIXERS
        mod_l = jnp.split((silu_c @ ada_w[i] + ada_b[i])[:, None, :], 6, axis=-1)
        mod_c = jnp.split(silu_cc @ ada_w[i] + ada_b[i], 6, axis=-1)
        h_l = _modulate(_rmsnorm(x, norm1_w[i]), mod_l[0], mod_l[1])
        h_c = _modulate(_rmsnorm(ctx, norm1_w[i]), mod_c[0], mod_c[1])
        if i % N_MIXERS == 0:
            m_c, m_l = _s5_mixer(h_c, h_l, s5_lam_re[j], s5_lam_im[j], s5_log_dt[j], s5_b_re[j], s5_b_im[j],
                                 s5_c_re[j], s5_c_im[j], s5_d[j], s5_w_glu[j], not last)
        else:
            m_c, m_l = _gdn_mixer(h_c, h_l, gdn_w_in[j], gdn_conv_w[j], gdn_a_log[j], gdn_dt_bias[j],
                                  gdn_onorm_w[j], gdn_w_out[j], not last)
        x = x + mod_l[2] * m_l
        x = x + mod_l[5] * _sqrelu_mlp(_modulate(_rmsnorm(x, norm2_w[i]), mod_l[3], mod_l[4]), mlp_w1[i], mlp_w2[i])
        if not last:
            ctx = ctx + mod_c[2] * m_c
            ctx = ctx + mod_c[5] * _sqrelu_mlp(_modulate(_rmsnorm(ctx, norm2_w[i]), mod_c[3], mod_c[4]), mlp_w1[i], mlp_w2[i])
    return _rmsnorm(x, final_norm_w)
```

```python
import math
from concourse.bass_utils import run_bass_kernel_spmd
import contextlib
import numpy as np
import concourse.bass as bass
import concourse.mybir as mybir

F32 = mybir.dt.float32
BF16 = mybir.dt.bfloat16
ALU = mybir.AluOpType
AF = mybir.ActivationFunctionType
AX = mybir.AxisListType

NDSEM = 8


def _ov(a, b):
    return a[0] < b[1] and b[0] < a[1] and a[2] < b[3] and b[2] < a[3]


def _cov(a, b):
    return a[0] <= b[0] and a[1] >= b[1] and a[2] <= b[2] and a[3] >= b[3]


class V:
    __slots__ = ("tile", "ap", "box")

    def __init__(self, tile, ap, box):
        self.tile, self.ap, self.box = tile, ap, box

    def map(self, f):
        return V(self.tile, f(self.ap), self.box)


class Tile:
    def __init__(self, h, shape, name, base=None, foff=0):
        self.h = h
        self.shape = tuple(shape)
        self.name = name
        self.base = base
        self.foff = foff
        st = []
        acc = 1
        for n in reversed(self.shape[1:]):
            st.append(acc)
            acc *= n
        self.fstr = list(reversed(st))
        if base is None:
            self.writes = []
            self.reads = []

    def sub(self, foff, shape, name="sub"):
        n = 1
        for s in shape[1:]:
            n *= s
        assert len(self.shape) == 2 and foff + n <= self.shape[1], (name, foff, n, self.shape)
        ap = self.h[0:shape[0], foff:foff + n]
        if len(shape) == 3:
            ap = ap.rearrange("p (a b) -> p a b", a=shape[1], b=shape[2])
        elif len(shape) == 4:
            ap = ap.rearrange("p (a b c) -> p a b c", a=shape[1], b=shape[2], c=shape[3])
        return Tile(_APH(ap), shape, name, base=self.base or self, foff=self.foff + foff)

    def __getitem__(self, idx):
        if not isinstance(idx, tuple):
            idx = (idx,)
        idx = idx + (slice(None),) * (len(self.shape) - len(idx))
        lo, hi = [], []
        for s, n in zip(idx, self.shape):
            if isinstance(s, int):
                lo.append(s)
                hi.append(s + 1)
            else:
                a = 0 if s.start is None else s.start
                b = n if s.stop is None else s.stop
                stp = 1 if s.step is None else s.step
                assert 0 <= a < b <= n, (self.name, idx, self.shape)
                lo.append(a)
                hi.append(a + ((b - a - 1) // stp) * stp + 1)
        f0 = sum(l * s for l, s in zip(lo[1:], self.fstr))
        f1 = sum((h - 1) * s for h, s in zip(hi[1:], self.fstr)) + 1
        bt = self.base or self
        es = getattr(self, "escale", 1)
        if es != 1:
            f0 = f0 // es
            f1 = -(-f1 // es)
        f0 += self.foff
        f1 += self.foff
        if getattr(bt, "bankgran", 0):
            g = bt.bankgran
            return V(bt, self.h[idx], (0, 128, (f0 // g) * g, -(-f1 // g) * g))
        return V(bt, self.h[idx], (lo[0], hi[0], f0, f1))


class _APH:
    def __init__(self, ap):
        self.ap = ap

    def __getitem__(self, idx):
        return self.ap[idx]


class Eng:
    def __init__(self, name, e, sem, dsems):
        self.name, self.e, self.sem, self.dsems = name, e, sem, dsems
        self.count = 0
        self.ndma = 0
        self.seen = {}
        self.prog = []


class Fw:
    def __init__(self, nc):
        self.nc = nc
        self.stack = contextlib.ExitStack()
        self.engs = {}
        self.semobj = {}
        for name, e, nd in (("pe", nc.tensor, 0), ("dve", nc.vector, 0), ("act", nc.scalar, NDSEM),
                            ("pool", nc.gpsimd, NDSEM), ("sync", nc.sync, NDSEM)):
            sem = self.stack.enter_context(nc.semaphore("s_" + name))
            ds = [self.stack.enter_context(nc.semaphore("d_%s%d" % (name, i))) for i in range(nd)]
            self.engs[name] = Eng(name, e, sem, ds)
        self.ntile = 0

    def sbuf(self, shape, dtype=F32, name=None):
        self.ntile += 1
        name = name or "t%d" % self.ntile
        h = self.stack.enter_context(self.nc.sbuf_tensor(name, list(shape), dtype))
        return Tile(h, shape, name)

    def psum(self, shape, dtype=F32, name=None):
        self.ntile += 1
        name = name or "p%d" % self.ntile
        h = self.stack.enter_context(self.nc.psum_tensor(name, list(shape), dtype))
        t = Tile(h, shape, name)
        t.bankgran = 512
        return t

    def dram(self, name, shape, dtype=F32, kind="Internal"):
        h = self.nc.dram_tensor(name, list(shape), dtype, kind=kind).ap()
        return Tile(h, shape, name)

    def _emit(self, ename, fn, w, r, dma=False):
        E = self.engs[ename]
        need = {}
        w = list(w) + [v for v in r if getattr(v.tile, "bankgran", 0)]

        def add(tok, kind):
            s, v = tok
            if s is E.sem:
                if ename == "pe":
                    return
                if kind != "raw":
                    return
            if need.get(id(s), (None, 0))[1] < v:
                need[id(s)] = (s, v)

        for v in r:
            for box, tok in v.tile.writes:
                if _ov(box, v.box):
                    add(tok, "raw")
        for v in w:
            for box, tok in v.tile.writes:
                if _ov(box, v.box):
                    add(tok, "waw")
            for box, tok in v.tile.reads:
                if _ov(box, v.box):
                    add(tok, "war")
        if dma:
            slot = E.ndma % NDSEM
            rnd = E.ndma // NDSEM
            E.ndma += 1
            dsem = E.dsems[slot]
            if rnd > 0:
                if need.get(id(dsem), (None, 0))[1] < 16 * rnd:
                    need[id(dsem)] = (dsem, 16 * rnd)
            tok = (dsem, 16 * (rnd + 1))
        else:
            E.count += 1
            tok = (E.sem, E.count)
        for s, v in need.values():
            if E.seen.get(id(s), 0) < v:
                E.seen[id(s)] = v
                E.prog.append(("w", s, v))
        E.prog.append(("i", fn, tok[0], 16 if dma else 1))
        for v in r:
            t = v.tile
            t.reads = [(b, k) for (b, k) in t.reads if not (k[0] is tok[0] and _cov(v.box, b))]
            t.reads.append((v.box, tok))
            if len(t.reads) > 48:
                self._compact(t)
        for v in w:
            t = v.tile
            t.writes = [(b, k) for (b, k) in t.writes if not _cov(v.box, b)]
            t.reads = [(b, k) for (b, k) in t.reads if not _cov(v.box, b)]
            t.writes.append((v.box, tok))
            if len(t.writes) > 48:
                self._compact(t)
        return tok

    def _compact(self, t):
        for attr in ("reads", "writes"):
            m = {}
            for b, k in getattr(t, attr):
                key = id(k[0])
                if key in m:
                    ob, ok = m[key]
                    m[key] = ((min(ob[0], b[0]), max(ob[1], b[1]), min(ob[2], b[2]), max(ob[3], b[3])),
                              (k[0], max(ok[1], k[1])))
                else:
                    m[key] = (b, k)
            setattr(t, attr, list(m.values()))

    def X(self, eng, meth, out, *args, **kw):
        ws, rs, a2, k2 = [out], [], [], {}
        for a in args:
            if isinstance(a, V):
                rs.append(a)
                a2.append(a.ap)
            else:
                a2.append(a)
        for k, a in kw.items():
            if isinstance(a, V):
                (ws if k == "accum_out" else rs).append(a)
                k2[k] = a.ap
            else:
                k2[k] = a
        oap = out.ap
        return self._emit(eng, lambda e: getattr(e, meth)(oap, *a2, **k2), ws, rs)

    def pe(self, fn, w, r):
        return self._emit("pe", fn, w, r)

    def dve(self, fn, w, r):
        return self._emit("dve", fn, w, r)

    def act(self, fn, w, r):
        return self._emit("act", fn, w, r)

    def pool(self, fn, w, r):
        return self._emit("pool", fn, w, r)

    def dma(self, q, out, in_, **kw):
        return self._emit(q, lambda e: e.dma_start(out=out.ap, in_=in_.ap, **kw), [out], [in_], dma=True)

    def finish(self):
        S = self.engs["sync"]
        for E in self.engs.values():
            if E.count > 0 and E is not S:
                S.prog.append(("w", E.sem, E.count))
            for i, ds in enumerate(E.dsems):
                n = (E.ndma - i + NDSEM - 1) // NDSEM if E.ndma > i else 0
                if n > 0:
                    S.prog.append(("w", ds, 16 * n))
        nc = self.nc
        with nc.Block() as block:
            def run(E):
                def body(e):
                    for it in E.prog:
                        if it[0] == "w":
                            e.wait_ge(it[1], it[2])
                        else:
                            it[1](e).then_inc(it[2], it[3])
                return body
            block.tensor(run(self.engs["pe"]))
            block.vector(run(self.engs["dve"]))
            block.scalar(run(self.engs["act"]))
            block.gpsimd(run(self.engs["pool"]))
            block.sync(run(self.engs["sync"]))
        self.stack.close()
        return {k: (E.count, E.ndma) for k, E in self.engs.items()}


I32 = mybir.dt.int32
EPS = 1e-6
SEQ = 2304
NT = 4608
TWO_PI = 2.0 * math.pi


def bc(ap):
    return ap.partition_broadcast(128)


class Ctx:
    pass


def setup(fw, dbg=False, ext_in=()):
    K = Ctx()
    K.fw = fw
    ext = lambda n, s: fw.dram(n, s, kind="ExternalInput")
    K.xin = ext("xin", [NT, 1024])
    K.cvec = ext("cvec", [128, 1024])
    K.ada_w = [ext("ada_w%d" % l, [1024, 6144]) for l in range(2)]
    K.ada_b = [ext("ada_b%d" % l, [1, 6144]) for l in range(2)]
    K.n1w = [ext("n1w%d" % l, [1, 1024]) for l in range(2)]
    K.n2w = [ext("n2w%d" % l, [1, 1024]) for l in range(2)]
    K.fnw = ext("fnw", [1, 1024])
    K.w1 = [ext("w1_%d" % l, [1024, 4096]) for l in range(2)]
    K.w2 = [ext("w2_%d" % l, [4096, 1024]) for l in range(2)]
    K.wglu = ext("wglu", [1024, 2048])
    K.win = ext("win", [1024, 6208])
    K.wout = ext("wout", [2048, 1024])
    K.lre = ext("lre", [128, 128])
    K.lim = ext("lim", [128, 128])
    K.ldt = ext("ldt", [128, 128])
    K.bz = ext("bz", [128, 2048])
    K.biz = ext("biz", [128, 2048])
    K.cz = ext("cz", [128, 2048])
    K.ciz = ext("ciz", [128, 2048])
    K.s5d = ext("s5d", [128, 8])
    K.convw = ext("convw", [128, 160])
    K.alog = ext("alog", [1, 32])
    K.dtb = ext("dtb", [1, 32])
    K.onw = ext("onw", [1, 128])
    K.cst = ext("cst", [128, 2048])
    K.ttab = ext("ttab", [4, SEQ])
    K.out = fw.dram("out", [4096, 1024], kind="ExternalOutput")
    sk = "ExternalOutput" if dbg else "Internal"
    scr = lambda n, s: fw.dram(n, s, kind=("ExternalInput" if n in ext_in else sk))
    K.silu_c = scr("silu_c", [128, 1024])
    K.mods = [scr("mods%d" % l, [128, 6144]) for l in range(2)]
    K.H = scr("H", [NT, 1024])
    K.Y = scr("Y", [NT, 1024])
    K.X1 = scr("X1", [NT, 1024])
    K.X2 = scr("X2", [NT, 1024])
    K.HID = scr("HID", [NT, 4096])
    K.P = scr("P", [NT, 6208])
    K.O = scr("O", [NT, 2048])
    K.X3 = scr("X3", [NT, 1024])
    K.X4 = scr("X4", [NT, 1024])
    K.C = fw.sbuf([128, 2048], name="consts")
    fw.dma("sync", K.C[:], K.cst[:])
    K.ident = K.C.sub(0, [128, 128])
    K.ones = K.C.sub(128, [128, 128])
    K.sgnA = K.C.sub(256, [128, 1])
    K.sgnB = K.C.sub(257, [128, 1])
    K.gm = K.C.sub(264, [128, 8])
    K.arena = fw.sbuf([128, 41 * 1024], name="arena")
    K.ps = fw.psum([128, 4096], name="ps")
    K.segs_all = []
    K.segs_lat = []
    for s in range(2):
        K.segs_all += [(s * SEQ, 2, 2), (s * SEQ + 256, 16, s)]
        K.segs_lat += [(s * SEQ + 256, 16, s)]
    return K


def host_consts():
    c = np.zeros((128, 2048), np.float32)
    c[:, 0:128] = np.eye(128)
    c[:, 128:256] = 1.0
    c[:64, 256] = -1.0
    c[64:, 256] = 1.0
    c[:64, 257] = 1.0
    c[64:, 257] = -1.0
    for g in range(8):
        c[16 * g:16 * g + 16, 264 + g] = 1.0
    return gdn_consts(c)


class Arena:
    def __init__(self, K):
        self.K = K
        self.off = 0

    def get(self, shape, name="a"):
        n = 1
        for s in shape[1:]:
            n *= s
        t = self.K.arena.sub(self.off, shape, name)
        self.off += n
        return t


def mods_phase(K):
    fw = K.fw
    A = Arena(K)
    t = A.get([128, 1024])
    s = A.get([128, 1024])
    fw.dma("sync", t[:], K.cvec[:])
    fw.X("act", "activation", s[:], t[:], AF.Sigmoid)
    fw.X("dve", "tensor_tensor", s[:], s[:], t[:], ALU.mult)
    fw.dma("pool", K.silu_c[:], s[:])
    for l in range(2):
        gemm(K, K.silu_c, 1024, K.ada_w[l], 6144, K.mods[l], [(0, 1, 0)], "bias", bias=K.ada_b[l])


def normmod(K, X, Y, segs, wrow, mods, ksh, ksc, yoff=0, omap=False):
    fw = K.fw
    A = Arena(K)
    Ar, Br, Tr = A.get([128, 1024]), A.get([128, 1024]), A.get([128, 1024])
    xt = [A.get([128, 1024]) for _ in range(2)]
    yt = [A.get([128, 1024]) for _ in range(2)]
    junk = A.get([128, 1024])
    ss = [A.get([128, 1]) for _ in range(2)]
    n = 0
    for (r0, nt, mr) in segs:
        fw.dma("sync", Ar[:], wrow[:].map(bc))
        if ksc is not None:
            fw.dma("sync", Tr[:], mods[mr:mr + 1, ksc * 1024:(ksc + 1) * 1024].map(bc))
            fw.X("dve", "scalar_tensor_tensor", Ar[:], Tr[:], 1.0, Ar[:], ALU.add, ALU.mult)
            fw.dma("sync", Br[:], mods[mr:mr + 1, ksh * 1024:(ksh + 1) * 1024].map(bc))
        for t in range(nt):
            x, y, s = xt[n % 2], yt[n % 2], ss[n % 2]
            n += 1
            rows = slice(r0 + 128 * t, r0 + 128 * t + 128)
            fw.dma("sync", x[:], X[rows, :])
            fw.X("dve", "memset", s[:], 0.0)
            fw.X("act", "activation", junk[:], x[:], AF.Square, accum_out=s[:])
            fw.X("act", "activation", s[:], s[:], AF.Sqrt, bias=EPS, scale=1.0 / 1024)
            fw.X("dve", "reciprocal", s[:], s[:])
            fw.X("dve", "scalar_tensor_tensor", y[:], x[:], s[:], Ar[:], ALU.mult, ALU.mult)
            if ksc is not None:
                fw.X("pool", "tensor_tensor", y[:], y[:], Br[:], ALU.add)
            orows = slice(rows.start - yoff, rows.stop - yoff)
            if omap:
                sq = rows.start // SEQ
                orows = slice(rows.start - 256 * (sq + 1), rows.stop - 256 * (sq + 1))
            fw.dma("pool", Y[orows, :], y[:])


def gemm(K, X, Kd, W, N, Y, segs, epi, bias=None, R=None, mods=None, kgate=None, woff=0, xoff=0, pre=None):
    fw = K.fw
    KC = Kd // 128
    NP = 512 if KC <= 16 else 256
    G = 4 if KC <= 16 else 2
    A = Arena(K)
    gx = [A.get([128, Kd]) for _ in range(2)]
    XT = A.get([128, KC, G * 128])
    nwb = 2 if epi != "glu" else 4
    Wp = [A.get([128, KC, NP]) for _ in range(nwb)]
    ot = [A.get([128, NP]) for _ in range(2)]
    rt = [A.get([128, NP]) for _ in range(2)]
    sg = [A.get([128, NP]) for _ in range(2)]
    grow = A.get([128, 1024])
    if pre is not None:
        assert Kd == 1024
        pAr, pBr, pTr, pjunk = [A.get([128, 1024]) for _ in range(4)]
        prss = [A.get([128, 1]) for _ in range(2)]
        npre = 0
    Wv = W.h.rearrange("(kc p) n -> p kc n", p=128)
    Wt = Tile(_APHk(Wv), [128, KC, W.shape[1]], W.name + "_v")
    nps = 0
    npw = 0
    nev = 0
    for (r0, nt, mr) in segs:
        if epi in ("res", "glu"):
            fw.dma("sync", grow[:], mods[mr:mr + 1, kgate * 1024:(kgate + 1) * 1024].map(bc))
        if pre is not None:
            pw, pm, pksh, pksc = pre
            fw.dma("sync", pAr[:], pw[:].map(bc))
            fw.dma("sync", pTr[:], pm[mr:mr + 1, pksc * 1024:(pksc + 1) * 1024].map(bc))
            fw.X("dve", "scalar_tensor_tensor", pAr[:], pTr[:], 1.0, pAr[:], ALU.add, ALU.mult)
            fw.dma("sync", pBr[:], pm[mr:mr + 1, pksh * 1024:(pksh + 1) * 1024].map(bc))
        for g0 in range(0, nt, G):
            gn = min(G, nt - g0)
            for t in range(gn):
                x = gx[t % 2]
                rows = slice(r0 + 128 * (g0 + t) - xoff, r0 + 128 * (g0 + t) + 128 - xoff)
                fw.dma("sync", x[:], X[rows, :])
                if pre is not None:
                    sq = prss[npre % 2]
                    npre += 1
                    fw.X("dve", "memset", sq[:], 0.0)
                    fw.X("act", "activation", pjunk[:], x[:], AF.Square, accum_out=sq[:])
                    fw.X("act", "activation", sq[:], sq[:], AF.Sqrt, bias=EPS, scale=1.0 / 1024)
                    fw.X("dve", "reciprocal", sq[:], sq[:])
                    fw.X("dve", "scalar_tensor_tensor", x[:], x[:], sq[:], pAr[:], ALU.mult, ALU.mult)
                    fw.X("dve", "tensor_tensor", x[:], x[:], pBr[:], ALU.add)
                for kc in range(KC):
                    pt = K.ps.sub(512 * (nps % 2), [128, 128])
                    nps += 1
                    fw.X("pe", "transpose", pt[:], x[:, 128 * kc:128 * kc + 128], K.ident[:])
                    fw.X("act" if kc % 2 else "dve", "copy" if kc % 2 else "tensor_copy",
                         XT[:, kc, 128 * t:128 * t + 128], pt[:])
            for n0 in range(0, N, NP):
                w = min(NP, N - n0)
                if epi == "glu":
                    cols = [n0, 1024 + n0]
                else:
                    cols = [woff + n0]
                wps = []
                for c0 in cols:
                    wp = Wp[npw % nwb]
                    npw += 1
                    wps.append(wp)
                    for k0 in range(0, KC, 8):
                        fw.dma("sync", wp[:, k0:k0 + 8, 0:w], Wt[:, k0:k0 + 8, c0:c0 + w])
                for t in range(gn):
                    rows = slice(r0 + 128 * (g0 + t), r0 + 128 * (g0 + t) + 128)
                    pss = []
                    for wi, wp in enumerate(wps):
                        py = K.ps.sub(1024 + 512 * (nev % 4), [128, w])
                        nev += 1
                        pss.append(py)
                        for kc in range(KC):
                            fw.X("pe", "matmul", py[:], XT[:, kc, 128 * t:128 * t + 128], wp[:, kc, 0:w],
                                 start=(kc == 0), stop=(kc == KC - 1))
                    o = ot[nev % 2]
                    py = pss[0]
                    if epi == "none":
                        fw.X("act", "copy", o[:, 0:w], py[:])
                    elif epi == "relu2":
                        fw.X("act", "activation", o[:, 0:w], py[:], AF.Relu)
                        fw.X("pool", "tensor_tensor", o[:, 0:w], o[:, 0:w], o[:, 0:w], ALU.mult)
                    elif epi == "bias":
                        r = rt[nev % 2]
                        fw.dma("sync", r[:, 0:w], bias[0:1, n0:n0 + w].map(bc))
                        fw.X("dve", "tensor_tensor", o[:, 0:w], py[:], r[:, 0:w], ALU.add)
                    elif epi in ("res", "glu"):
                        r = rt[nev % 2]
                        fw.dma("sync", r[:, 0:w], R[rows, n0:n0 + w])
                        sgt = sg[nev % 2]
                        if epi == "glu":
                            fw.X("act", "activation", sgt[:, 0:w], pss[1][:], AF.Sigmoid)
                            fw.X("dve", "tensor_tensor", sgt[:, 0:w], py[:], sgt[:, 0:w], ALU.mult)
                            fw.X("dve", "tensor_tensor", sgt[:, 0:w], sgt[:, 0:w], grow[:, n0:n0 + w], ALU.mult)
                        else:
                            fw.X("dve", "tensor_tensor", sgt[:, 0:w], py[:], grow[:, n0:n0 + w], ALU.mult)
                        fw.X("pool", "tensor_tensor", o[:, 0:w], sgt[:, 0:w], r[:, 0:w], ALU.add)
                    fw.dma("pool", Y[rows, n0:n0 + w], o[:, 0:w])


def bf_tile(K, A, shape):
    n = 1
    for d_ in shape[1:]:
        n *= d_
    nf = (n + 1) // 2
    off = A.off
    A.off += nf
    assert A.off <= K.arena.shape[1], ("arena overflow", A.off)
    ap = K.arena.h[0:shape[0], off:off + nf].bitcast(BF16)
    if len(shape) == 3:
        ap = ap.rearrange("p (a b) -> p a b", a=shape[1], b=shape[2])
    t = Tile(_APHk(ap), shape, "bf", base=K.arena, foff=off)
    t.escale = 2
    return t


def gemm3(K, X, Kd, W, N, Y, segs, epi, bias=None, R=None, mods=None, kgate=None, woff=0, xoff=0, pre=None):
    fw = K.fw
    KC = Kd // 128
    NP = {8: 512, 16: 256, 32: 128}[KC]
    if epi == "glu":
        NP = 256
    G = 4 if KC <= 16 else 2
    A = Arena(K)
    gx = [A.get([128, Kd]) for _ in range(2)]
    XTh = bf_tile(K, A, [128, KC, G * 128])
    XTl = bf_tile(K, A, [128, KC, G * 128])
    nwb = 2 if epi != "glu" else 4
    Wp = [A.get([128, KC, NP]) for _ in range(nwb)]
    Wh = [bf_tile(K, A, [128, KC, NP]) for _ in range(nwb)]
    Wl = [bf_tile(K, A, [128, KC, NP]) for _ in range(nwb)]
    ot = [A.get([128, NP]) for _ in range(2)]
    rt = [A.get([128, NP]) for _ in range(2)]
    sg = [A.get([128, NP]) for _ in range(2)]
    grow = A.get([128, 1024])
    if pre is not None:
        assert Kd == 1024
        pAr, pBr, pTr, pjunk = [A.get([128, 1024]) for _ in range(4)]
        prss = [A.get([128, 1]) for _ in range(2)]
        npre = 0
    Wv = W.h.rearrange("(kc p) n -> p kc n", p=128)
    Wt = Tile(_APHk(Wv), [128, KC, W.shape[1]], W.name + "_v")
    nps = 0
    npw = 0
    nev = 0
    for (r0, nt, mr) in segs:
        if epi in ("res", "glu"):
            fw.dma("sync", grow[:], mods[mr:mr + 1, kgate * 1024:(kgate + 1) * 1024].map(bc))
        if pre is not None:
            pw, pm, pksh, pksc = pre
            fw.dma("sync", pAr[:], pw[:].map(bc))
            fw.dma("sync", pTr[:], pm[mr:mr + 1, pksc * 1024:(pksc + 1) * 1024].map(bc))
            fw.X("dve", "scalar_tensor_tensor", pAr[:], pTr[:], 1.0, pAr[:], ALU.add, ALU.mult)
            fw.dma("sync", pBr[:], pm[mr:mr + 1, pksh * 1024:(pksh + 1) * 1024].map(bc))
        for g0 in range(0, nt, G):
            gn = min(G, nt - g0)
            for t in range(gn):
                x = gx[t % 2]
                rows = slice(r0 + 128 * (g0 + t) - xoff, r0 + 128 * (g0 + t) + 128 - xoff)
                fw.dma("sync", x[:], X[rows, :])
                if pre is not None:
                    sq = prss[npre % 2]
                    npre += 1
                    fw.X("dve", "memset", sq[:], 0.0)
                    fw.X("act", "activation", pjunk[:], x[:], AF.Square, accum_out=sq[:])
                    fw.X("act", "activation", sq[:], sq[:], AF.Sqrt, bias=EPS, scale=1.0 / 1024)
                    fw.X("dve", "reciprocal", sq[:], sq[:])
                    fw.X("dve", "scalar_tensor_tensor", x[:], x[:], sq[:], pAr[:], ALU.mult, ALU.mult)
                    fw.X("dve", "tensor_tensor", x[:], x[:], pBr[:], ALU.add)
                for kc0 in range(0, KC, 4):
                    pt = K.ps.sub(512 * (nps % 2), [128, 4, 128])
                    nps += 1
                    for kk in range(4):
                        fw.X("pe", "transpose", pt[:, kk, :], x[:, 128 * (kc0 + kk):128 * (kc0 + kk) + 128], K.ident[:])
                    fw.X("act", "copy", XTh[:, kc0:kc0 + 4, 128 * t:128 * t + 128], pt[:])
                    fw.X("dve", "tensor_tensor", XTl[:, kc0:kc0 + 4, 128 * t:128 * t + 128], pt[:],
                         XTh[:, kc0:kc0 + 4, 128 * t:128 * t + 128], ALU.subtract)
            for n0 in range(0, N, NP):
                w = min(NP, N - n0)
                if epi == "glu":
                    cols = [n0, 1024 + n0]
                else:
                    cols = [woff + n0]
                wps = []
                for c0 in cols:
                    wp, wh, wl = Wp[npw % nwb], Wh[npw % nwb], Wl[npw % nwb]
                    npw += 1
                    wps.append((wh, wl))
                    for k0 in range(0, KC, 8):
                        fw.dma("sync", wp[:, k0:k0 + 8, 0:w], Wt[:, k0:k0 + 8, c0:c0 + w])
                    fw.X("act", "copy", wh[:, :, 0:w], wp[:, :, 0:w])
                    fw.X("dve", "tensor_tensor", wl[:, :, 0:w], wp[:, :, 0:w], wh[:, :, 0:w], ALU.subtract)
                for t in range(gn):
                    rows = slice(r0 + 128 * (g0 + t), r0 + 128 * (g0 + t) + 128)
                    pss = []
                    for wi, (wh, wl) in enumerate(wps):
                        py = K.ps.sub(1024 + 512 * (nev % 4), [128, w])
                        nev += 1
                        pss.append(py)
                        for kc in range(KC):
                            xh = XTh[:, kc, 128 * t:128 * t + 128]
                            xl = XTl[:, kc, 128 * t:128 * t + 128]
                            fw.X("pe", "matmul", py[:], xh, wh[:, kc, 0:w], start=(kc == 0), stop=False)
                            fw.X("pe", "matmul", py[:], xh, wl[:, kc, 0:w], start=False, stop=False)
                            fw.X("pe", "matmul", py[:], xl, wh[:, kc, 0:w], start=False, stop=(kc == KC - 1))
                    o = ot[nev % 2]
                    py = pss[0]
                    if epi == "none":
                        fw.X("act", "copy", o[:, 0:w], py[:])
                    elif epi == "relu2":
                        fw.X("act", "activation", o[:, 0:w], py[:], AF.Relu)
                        fw.X("pool", "tensor_tensor", o[:, 0:w], o[:, 0:w], o[:, 0:w], ALU.mult)
                    elif epi == "bias":
                        r = rt[nev % 2]
                        fw.dma("sync", r[:, 0:w], bias[0:1, n0:n0 + w].map(bc))
                        fw.X("dve", "tensor_tensor", o[:, 0:w], py[:], r[:, 0:w], ALU.add)
                    elif epi in ("res", "glu"):
                        r = rt[nev % 2]
                        fw.dma("sync", r[:, 0:w], R[rows, n0:n0 + w])
                        sgt = sg[nev % 2]
                        if epi == "glu":
                            fw.X("act", "activation", sgt[:, 0:w], pss[1][:], AF.Sigmoid)
                            fw.X("dve", "tensor_tensor", sgt[:, 0:w], py[:], sgt[:, 0:w], ALU.mult)
                            fw.X("dve", "tensor_tensor", sgt[:, 0:w], sgt[:, 0:w], grow[:, n0:n0 + w], ALU.mult)
                        else:
                            fw.X("dve", "tensor_tensor", sgt[:, 0:w], py[:], grow[:, n0:n0 + w], ALU.mult)
                        fw.X("pool", "tensor_tensor", o[:, 0:w], sgt[:, 0:w], r[:, 0:w], ALU.add)
                    fw.dma("pool", Y[rows, n0:n0 + w], o[:, 0:w])


class _APHk:
    def __init__(self, ap):
        self.ap = ap

    def __getitem__(self, idx):
        return self.ap[idx]


def s5_setup(K):
    fw = K.fw
    Pm = fw.sbuf([128, 3 * 128 + 4 * 2048], name="s5p")
    K.s5R = Pm.sub(0, [128, 128])
    K.s5thn = Pm.sub(128, [128, 128])
    K.s5thn64 = Pm.sub(256, [128, 128])
    K.bbar = Pm.sub(384, [128, 2048])
    K.ibbar = Pm.sub(384 + 2048, [128, 2048])
    K.cstm = Pm.sub(384 + 4096, [128, 2048])
    K.cstim = Pm.sub(384 + 6144, [128, 2048])
    K.s5dt = fw.sbuf([128, 8], name="s5dsb")
    fw.dma("sync", K.s5dt[:], K.s5d[:])
    A = Arena(K)
    g = lambda: A.get([128, 128])
    lre, lim, ldt, th, rho, un, fr, cs, sn, are, aim, xm, t1, t2, cre, cim = [g() for _ in range(16)]
    ki = K.arena.sub(A.off, [128, 128])
    A.off += 128
    kint = Tile(_APHk(ki.h.ap.bitcast(I32)), [128, 128], "kint", base=ki.base, foff=ki.foff)
    fw.dma("sync", lre[:], K.lre[:])
    fw.dma("sync", lim[:], K.lim[:])
    fw.dma("sync", ldt[:], K.ldt[:])
    X = fw.X
    X("act", "activation", ldt[:], ldt[:], AF.Exp)
    X("dve", "tensor_tensor", th[:], lim[:], ldt[:], ALU.mult)
    X("dve", "tensor_tensor", rho[:], lre[:], ldt[:], ALU.mult)
    X("act", "activation", K.s5R[:], rho[:], AF.Exp)

    def frac(dst, src):
        X("dve", "tensor_copy", kint[:], src[:])
        X("dve", "tensor_tensor", dst[:], src[:], kint[:], ALU.subtract)

    X("dve", "tensor_scalar", un[:], th[:], 1.0 / TWO_PI, None, ALU.mult)
    frac(K.s5thn, un)
    X("dve", "tensor_scalar", un[:], K.s5thn[:], 64.0, None, ALU.mult)
    frac(K.s5thn64, un)
    X("dve", "tensor_scalar", un[:], K.s5thn[:], 0.25, None, ALU.add)
    frac(fr, un)
    X("act", "activation", cs[:], fr[:], AF.Sin, scale=TWO_PI)
    X("act", "activation", sn[:], K.s5thn[:], AF.Sin, scale=TWO_PI)
    X("dve", "tensor_tensor", are[:], K.s5R[:], cs[:], ALU.mult)
    X("dve", "tensor_tensor", aim[:], K.s5R[:], sn[:], ALU.mult)
    X("dve", "tensor_scalar", xm[:], are[:], -1.0, None, ALU.add)
    X("dve", "tensor_tensor", t1[:], xm[:], lre[:], ALU.mult)
    X("dve", "tensor_tensor", t2[:], aim[:], lim[:], ALU.mult)
    X("dve", "tensor_tensor", cre[:], t1[:], t2[:], ALU.add)
    X("dve", "tensor_tensor", t1[:], aim[:], lre[:], ALU.mult)
    X("dve", "tensor_tensor", t2[:], xm[:], lim[:], ALU.mult)
    X("dve", "tensor_tensor", cim[:], t1[:], t2[:], ALU.subtract)
    X("dve", "tensor_tensor", t1[:], lre[:], lre[:], ALU.mult)
    X("dve", "tensor_tensor", t2[:], lim[:], lim[:], ALU.mult)
    X("dve", "tensor_tensor", t1[:], t1[:], t2[:], ALU.add)
    X("dve", "reciprocal", t1[:], t1[:])
    X("dve", "tensor_tensor", cre[:], cre[:], t1[:], ALU.mult)
    X("dve", "tensor_tensor", cim[:], cim[:], t1[:], ALU.mult)
    bz, ib, tmp = A.get([128, 128, 16]), A.get([128, 128, 16]), A.get([128, 128, 16])
    fw.dma("sync", bz[:], K.bz[:].map(lambda a: a.rearrange("p (a b) -> p a b", b=16)))
    fw.dma("sync", ib[:], K.biz[:].map(lambda a: a.rearrange("p (a b) -> p a b", b=16)))
    X("dve", "tensor_scalar", ib[:], ib[:], K.sgnA[:], None, ALU.mult)
    b3 = lambda t: t[:].map(lambda a: a.unsqueeze(2).to_broadcast([128, 128, 16]))
    v3 = lambda t: t[:].map(lambda a: a.rearrange("p (a b) -> p a b", b=16))
    X("dve", "tensor_tensor", v3(K.bbar), bz[:], b3(cre), ALU.mult)
    X("dve", "tensor_tensor", tmp[:], ib[:], b3(cim), ALU.mult)
    X("dve", "tensor_tensor", v3(K.bbar), v3(K.bbar), tmp[:], ALU.add)
    X("dve", "tensor_tensor", v3(K.ibbar), ib[:], b3(cre), ALU.mult)
    X("dve", "tensor_tensor", tmp[:], bz[:], b3(cim), ALU.mult)
    X("dve", "tensor_tensor", v3(K.ibbar), v3(K.ibbar), tmp[:], ALU.subtract)
    cz = A.get([128, 2048])
    fw.dma("sync", cz[:], K.cz[:])
    X("dve", "tensor_scalar", K.cstm[:], cz[:], K.sgnB[:], None, ALU.mult)
    cz2 = A.get([128, 2048])
    fw.dma("sync", cz2[:], K.ciz[:])
    X("dve", "tensor_scalar", K.cstim[:], cz2[:], -1.0, None, ALU.mult)


FLAGS = set()
PIECES = [(0, 512), (512, 1024), (1024, 1536), (1536, 2048), (2048, 2304)]


def s5_core(K, H, Y, chunks=range(8), dbg=None):
    fw = K.fw
    X = fw.X
    A = Arena(K)
    big = lambda: A.get([128, SEQ])
    hT = [big(), big()]
    yacc = [big(), big()]
    G0, gs, GS, U, SIN, COS, T1, T0 = [big() for _ in range(8)]
    G0s = [G0, big()]
    GSs = [GS, big()]
    npo = 0
    kraw = big()
    KI = Tile(_APHk(kraw.h.ap.bitcast(I32)), [128, SEQ], "KI", base=kraw.base, foff=kraw.foff)
    LB, LIB, LC, LCI = [A.get([128, 8, 128]) for _ in range(4)]
    tmp = [A.get([128, 512]) for _ in range(2)]
    stg = [A.get([128, 128]) for _ in range(4)]
    X("dve", "memset", LC[:], 0.0)
    X("dve", "memset", LCI[:], 0.0)
    rev = lambda v: v.map(lambda a: a[:, ::-1])
    nst = 0
    npp = 0
    for c in chunks:
        for s in range(2):
            for t in range(18):
                st = stg[nst % 4]
                nst += 1
                fw.dma("sync", st[:], H[s * SEQ + 128 * t: s * SEQ + 128 * t + 128, 128 * c:128 * c + 128])
                pt = K.ps.sub(512 * (nst % 2), [128, 128])
                X("pe", "transpose", pt[:], st[:], K.ident[:])
                X("act", "copy", hT[s][:, 128 * t:128 * t + 128], pt[:])
        first = True
        for d in range(2):
            fw.dma("sync", T1[:], K.ttab[2 * d:2 * d + 1, :].map(bc))
            fw.dma("sync", T0[:], K.ttab[2 * d + 1:2 * d + 2, :].map(bc))
            col0 = (d * 64 + 8 * c) * 16
            for (src, dst) in ((K.bbar, LB), (K.ibbar, LIB)):
                pt = K.ps.sub(1024, [128, 128])
                X("pe", "transpose", pt[:], src[:, col0:col0 + 128], K.ident[:])
                for gl in range(8):
                    X("dve", "tensor_scalar", dst[:, gl, :], pt[:], K.gm[:, gl:gl + 1], None, ALU.mult)
            for gl in range(8):
                X("pool", "tensor_copy", LC[:, gl, 16 * gl:16 * gl + 16], K.cstm[:, col0 + 16 * gl:col0 + 16 * gl + 16])
                X("pool", "tensor_copy", LCI[:, gl, 16 * gl:16 * gl + 16], K.cstim[:, col0 + 16 * gl:col0 + 16 * gl + 16])
            if "stop1" in FLAGS:
                continue
            for gl in range(8):
                dg = d * 64 + 8 * c + gl
                thn = K.s5thn[:, dg:dg + 1]
                thn64 = K.s5thn64[:, dg:dg + 1]
                rbc = lambda n: K.s5R[:, dg:dg + 1].map(lambda a: a.to_broadcast([128, n]))
                if "notab" in FLAGS:
                    X("dve", "memset", SIN[:], 0.0)
                    X("dve", "memset", COS[:], 1.0)
                for _ in ([] if "notab" in FLAGS else [0]):
                  X("dve", "tensor_scalar", U[:], T1[:], thn64, None, ALU.mult)
                  X("dve", "scalar_tensor_tensor", U[:], T0[:], thn, U[:], ALU.mult, ALU.add)
                  X("dve", "tensor_copy", KI[:], U[:])
                  X("dve", "tensor_tensor", SIN[:], U[:], KI[:], ALU.subtract)
                  X("act", "activation", SIN[:], SIN[:], AF.Sin, scale=TWO_PI)
                  X("dve", "tensor_scalar", U[:], U[:], 0.25, None, ALU.add)
                  X("dve", "tensor_copy", KI[:], U[:])
                  X("dve", "tensor_tensor", COS[:], U[:], KI[:], ALU.subtract)
                  X("act", "activation", COS[:], COS[:], AF.Sin, scale=TWO_PI)
                for s in range(2):
                    G0 = G0s[s]
                    for (p0, p1) in PIECES:
                        n = p1 - p0
                        ps1 = K.ps.sub(1536 + 1024 * (npp % 2), [128, n])
                        ps2 = K.ps.sub(2048 + 1024 * (npp % 2), [128, n])
                        tm = tmp[npp % 2]
                        npp += 1
                        X("pe", "matmul", ps1[:], LB[:, gl, :], hT[s][:, p0:p1], start=True, stop=True)
                        X("pe", "matmul", ps2[:], LIB[:, gl, :], hT[s][:, p0:p1], start=True, stop=True)
                        X("dve", "tensor_tensor", tm[:, 0:n], ps2[:], SIN[:, p0:p1], ALU.mult)
                        X("dve", "tensor_tensor", G0[:, p0:p1], ps1[:], COS[:, p0:p1], ALU.mult)
                        X("dve", "tensor_tensor", G0[:, p0:p1], G0[:, p0:p1], tm[:, 0:n], ALU.subtract)
                for s in range(2):
                    G0 = G0s[s]
                    GS = GSs[s]
                    if "noscan" in FLAGS:
                        X("dve", "tensor_copy", gs[:], G0[:])
                    elif d == 0:
                        X("dve", "tensor_tensor_scan", gs[:], rbc(SEQ), G0[:], 0.0, ALU.mult, ALU.add)
                    else:
                        X("dve", "tensor_tensor_scan", rev(gs[:, 0:256]), rbc(256), rev(G0[:, 0:256]), 0.0,
                          ALU.mult, ALU.add)
                        X("dve", "tensor_tensor_scan", rev(gs[:, 256:SEQ]), rbc(SEQ - 256), rev(G0[:, 256:SEQ]),
                          gs[:, 0:1], ALU.mult, ALU.add)
                    X("dve", "tensor_tensor", G0[:], gs[:], COS[:], ALU.mult)
                    X("dve", "tensor_tensor", GS[:], gs[:], SIN[:], ALU.mult)
                    for (p0, p1) in PIECES:
                        n = p1 - p0
                        po = K.ps.sub((3584, 0, 512)[npo % 3], [128, n])
                        npo += 1
                        X("pe", "matmul", po[:], LC[:, gl, :], G0[:, p0:p1], start=True, stop=False)
                        X("pe", "matmul", po[:], LCI[:, gl, :], GS[:, p0:p1], start=False, stop=True)
                        if first:
                            X("act", "copy", yacc[s][:, p0:p1], po[:])
                        else:
                            X("dve", "tensor_tensor", yacc[s][:, p0:p1], yacc[s][:, p0:p1], po[:], ALU.add)
                first = False
        G0 = G0s[0]
        for s in ([] if ("stop1" in FLAGS or "stop2" in FLAGS) else range(2)):
            y = yacc[s]
            X("dve", "scalar_tensor_tensor", y[:], hT[s][:], K.s5dt[:, c:c + 1], y[:], ALU.mult, ALU.add)
            X("dve", "tensor_tensor", G0[:], y[:], y[:], ALU.mult)
            X("dve", "tensor_scalar", G0[:], G0[:], 0.044715, 1.0, ALU.mult, ALU.add)
            X("dve", "tensor_tensor", G0[:], G0[:], y[:], ALU.mult)
            X("act", "activation", G0[:], G0[:], AF.Tanh, scale=math.sqrt(2.0 / math.pi))
            X("dve", "tensor_scalar", G0[:], G0[:], 1.0, 0.5, ALU.add, ALU.mult)
            X("dve", "tensor_tensor", y[:], G0[:], y[:], ALU.mult)
            for t in range(18):
                st = stg[nst % 4]
                nst += 1
                pt = K.ps.sub(512 * (nst % 2), [128, 128])
                X("pe", "transpose", pt[:], y[:, 128 * t:128 * t + 128], K.ident[:])
                X("act", "copy", st[:], pt[:])
                fw.dma("pool", Y[s * SEQ + 128 * t: s * SEQ + 128 * t + 128, 128 * c:128 * c + 128], st[:])


def layer0(K):
    m = K.mods[0]
    normmod(K, K.xin, K.H, K.segs_all, K.n1w[0], m, 0, 1)
    s5_core(K, K.H, K.Y)
    gemm3(K, K.Y, 1024, K.wglu, 1024, K.X1, K.segs_all, "glu", R=K.xin, mods=m, kgate=2)
    gemm3(K, K.X1, 1024, K.w1[0], 4096, K.HID, K.segs_all, "relu2", pre=(K.n2w[0], m, 3, 4))
    gemm3(K, K.HID, 4096, K.w2[0], 1024, K.X2, K.segs_all, "res", R=K.X1, mods=m, kgate=5)


def host_shared(inp):
    f = lambda a: np.ascontiguousarray(np.asarray(a, dtype=np.float32))
    d = {}
    for l in range(2):
        d["ada_w%d" % l] = f(inp["ada_w"][l])
        d["ada_b%d" % l] = f(inp["ada_b"][l][None, :])
        d["n1w%d" % l] = f(inp["norm1_w"][l][None, :])
        d["n2w%d" % l] = f(inp["norm2_w"][l][None, :])
        d["w1_%d" % l] = f(inp["mlp_w1"][l])
        d["w2_%d" % l] = f(inp["mlp_w2"][l])
    d["fnw"] = f(inp["final_norm_w"][None, :])
    d["wglu"] = f(inp["s5_w_glu"][0])
    d["win"] = f(inp["gdn_w_in"][0])
    d["wout"] = f(inp["gdn_w_out"][0])
    lre = np.transpose(inp["s5_lam_re"][0], (2, 0, 1)).reshape(64, 128)
    lim = np.transpose(inp["s5_lam_im"][0], (2, 0, 1)).reshape(64, 128)
    d["lre"] = f(np.concatenate([lre, lre], 0))
    d["lim"] = f(np.concatenate([lim, lim], 0))
    d["ldt"] = f(np.broadcast_to(inp["s5_log_dt"][0].reshape(1, 128), (128, 128)))
    bre = np.transpose(inp["s5_b_re"][0], (2, 0, 1, 3)).reshape(64, 2048)
    bim = np.transpose(inp["s5_b_im"][0], (2, 0, 1, 3)).reshape(64, 2048)
    d["bz"] = f(np.concatenate([bre, bim], 0))
    d["biz"] = f(np.concatenate([bim, bre], 0))
    cre = np.transpose(inp["s5_c_re"][0], (3, 0, 1, 2)).reshape(64, 2048)
    cim = np.transpose(inp["s5_c_im"][0], (3, 0, 1, 2)).reshape(64, 2048)
    d["cz"] = f(np.concatenate([cre, cim], 0))
    d["ciz"] = f(np.concatenate([cim, cre], 0))
    d["s5d"] = f(inp["s5_d"][0].reshape(8, 128).T)
    d["convw"] = f(np.transpose(inp["gdn_conv_w"][0].reshape(5, 32, 128), (2, 1, 0)).reshape(128, 160))
    d["alog"] = f(inp["gdn_a_log"][0].reshape(1, 32))
    d["dtb"] = f(inp["gdn_dt_bias"][0].reshape(1, 32))
    d["onw"] = f(inp["gdn_onorm_w"][0][None, :])
    d["cst"] = host_consts()
    t = np.arange(SEQ)
    trev = np.where(t < 256, 255 - t, 2559 - t)
    d["ttab"] = f(np.stack([t // 64, t % 64, trev // 64, trev % 64]))
    return d


def host_core(inp, core):
    d = {}
    b0 = 2 * core
    x = np.asarray(inp["x"]); ctx = np.asarray(inp["ctx"])
    d["xin"] = np.ascontiguousarray(np.concatenate([ctx[b0], x[b0], ctx[b0 + 1], x[b0 + 1]], 0).astype(np.float32))
    cv = np.zeros((128, 1024), np.float32)
    cv[0] = inp["c"][b0]; cv[1] = inp["c"][b0 + 1]; cv[2] = inp["c_ctx"]
    d["cvec"] = cv
    return d


def gdn_consts(c):
    t = np.arange(64)
    c[:64, 512:576] = (t[:, None] <= t[None, :])
    c[:64, 576:640] = (t[:, None] >= t[None, :])
    c[63, 640:768] = 1.0
    c[0, 768:896] = 1.0
    ji = lambda f: np.concatenate([f(t[:, None], t[None, :])] * 2, 1).astype(np.float32)
    c[:64, 896:1024] = ji(lambda j, i: i >= j)
    c[:64, 1024:1152] = -ji(lambda j, i: i > j)
    c[:64, 1152:1280] = ji(lambda j, i: i <= j)
    c[:64, 1280:1408] = -ji(lambda j, i: i < j)
    c[:64, 1408:1536] = ji(lambda j, i: i == j)
    return c


def interleave(gens):
    gens = list(gens)
    lim = [int(f[2:]) for f in FLAGS if f.startswith("ut")]
    nround = 0
    while gens:
        if lim and nround >= lim[0]:
            return
        nround += 1
        nxt = []
        for g in gens:
            try:
                next(g)
                nxt.append(g)
            except StopIteration:
                pass
        gens = nxt


def gdn_core(K, P, O, seqs=(0, 1), khs=range(8)):
    fw = K.fw
    X = fw.X
    C = K.C
    id64 = C.sub(0, [64, 64])
    ones64 = C.sub(128, [64, 64])
    TRI = [C.sub(512, [64, 64]), C.sub(576, [64, 64])]
    SEL = [C.sub(640, [64, 128]), C.sub(768, [64, 128])]
    MASKI = [C.sub(896, [64, 2, 64]), C.sub(1152, [64, 2, 64])]
    NMASKS = [C.sub(1024, [64, 2, 64]), C.sub(1280, [64, 2, 64])]
    ID2 = C.sub(1408, [64, 2, 64])
    A = Arena(K)
    NG = 36 * 32
    BETA, Gg, GC, EGC, DK, NEGC = [A.get([64, 36, 32]) for _ in range(6)]
    EGL = A.get([128, 36, 32])
    stg, raw, tmp = A.get([128, SEQ]), A.get([128, SEQ]), A.get([128, SEQ])
    Graw = K.arena.sub(stg.foff, [64, 36, 64])
    Zt = K.arena.sub(stg.foff, [64, 32, 128])
    qT, kT = A.get([128, SEQ]), A.get([128, SEQ])
    vT = [A.get([128, SEQ]) for _ in range(2)]
    OACC = [A.get([64, 36, 128]) for _ in range(2)]
    t2 = lambda: A.get([64, 2, 64])
    UT = []
    for u in range(2):
        d = Ctx()
        d.DG, d.E, d.Ei, d.NEs = t2(), t2(), t2(), t2()
        d.Xb = [t2(), t2()]
        d.Yb = [t2(), t2()]
        d.Wb = [t2(), t2()]
        d.TT = [t2(), t2()]
        d.PT = [t2(), t2()]
        d.KTOK = [A.get([64, 128]) for _ in range(2)]
        d.VTOK = [A.get([64, 2, 128]) for _ in range(2)]
        d.bA = 512 * (2 * u)
        d.bB = 512 * (2 * u + 1)
        UT.append(d)
    RHS2 = [A.get([64, 128]) for _ in range(4)]
    VNEW = [A.get([64, 128]) for _ in range(4)]
    VN2 = [A.get([64, 128]) for _ in range(4)]
    S = [A.get([128, 128]) for _ in range(4)]
    rsp = [A.get([128, 512]) for _ in range(2)]
    onw = A.get([64, 128])
    rows32 = A.get([64, 64])
    ssr = A.get([64, 32])
    cw = A.get([128, 160])
    fw.dma("sync", cw[:], K.convw[:])
    fw.dma("sync", onw[:], K.onw[:].map(lambda a: a.partition_broadcast(64)))
    fw.dma("sync", rows32[:, 0:32], K.alog[:].map(lambda a: a.partition_broadcast(64)))
    fw.dma("sync", rows32[:, 32:64], K.dtb[:].map(lambda a: a.partition_broadcast(64)))
    X("act", "activation", rows32[:, 0:32], rows32[:, 0:32], AF.Exp)
    X("dve", "tensor_scalar", rows32[:, 0:32], rows32[:, 0:32], -1.0, None, ALU.mult)
    b36 = lambda v: v.map(lambda a: a.unsqueeze(1).to_broadcast([64, 36, 32]))
    ps = K.ps
    for s in seqs:
        base = s * SEQ
        fw.dma("sync", Graw[:], P[base:base + SEQ, 6144:6208].map(lambda a: a.rearrange("(c p) g -> p c g", p=64)))
        X("act", "activation", BETA[:], Graw[:, :, 0:32], AF.Sigmoid)
        X("dve", "tensor_tensor", Gg[:], Graw[:, :, 32:64], b36(rows32[:, 32:64]), ALU.add)
        X("act", "activation", Gg[:], Gg[:], AF.Exp)
        X("act", "activation", Gg[:], Gg[:], AF.Ln, bias=1.0)
        X("dve", "tensor_tensor", Gg[:], Gg[:], b36(rows32[:, 0:32]), ALU.mult)
        g2 = Gg[:].map(lambda a: a.rearrange("p c g -> p (c g)"))
        if "nogates" in FLAGS:
            X("dve", "memset", GC[:], -0.1)
            X("dve", "memset", EGL[:], -0.1)
        for d in ([] if "nogates" in FLAGS else range(2)):
            for j in range(3):
                pc = ps.sub(512 * j, [64, 384])
                X("pe", "matmul", pc[:], TRI[d][:], g2.map(lambda a: a[:, 384 * j:384 * j + 384]), start=True, stop=True)
                pc3 = pc[:].map(lambda a: a.rearrange("p (c g) -> p c g", g=32)[:, :, 16 * d:16 * d + 16])
                X("act", "copy", GC[:, 12 * j:12 * j + 12, 16 * d:16 * d + 16], pc3)
        gc2 = GC[:].map(lambda a: a.rearrange("p c g -> p (c g)"))
        for j in ([] if "nogates" in FLAGS else range(3)):
            for d in range(2):
                pl = ps.sub(512 * (3 + j), [128, 384])
                X("pe", "matmul", pl[:], SEL[d][:], gc2.map(lambda a: a[:, 384 * j:384 * j + 384]), start=True, stop=True)
                pl3 = pl[:].map(lambda a: a.rearrange("p (c g) -> p c g", g=32)[:, :, 16 * d:16 * d + 16])
                X("dve", "tensor_copy", EGL[:, 12 * j:12 * j + 12, 16 * d:16 * d + 16], pl3)
        X("dve", "tensor_scalar", Gg[:], GC[:], -1.0, None, ALU.mult)
        X("dve", "tensor_tensor", DK[:], EGL[0:64], GC[:], ALU.subtract)
        X("act", "activation", DK[:], DK[:], AF.Exp)
        X("act", "activation", EGC[:], GC[:], AF.Exp)
        X("dve", "tensor_scalar", NEGC[:], EGC[:], -1.0, None, ALU.mult)
        X("act", "activation", EGL[:], EGL[:], AF.Exp)
        if "g_stop" in FLAGS:
            continue
        for kh in khs:
            specs = [(kh * 128, kh, qT, 128 ** -0.5), (3072 + kh * 128, 8 + kh, kT, 1.0),
                     (4096 + (2 * kh) * 128, 16 + 2 * kh, vT[0], None),
                     (4096 + (2 * kh + 1) * 128, 17 + 2 * kh, vT[1], None)]
            for (col0, cc, dst, nrm) in specs:
                fw.dma("sync", stg[:].map(lambda a: a.rearrange("p (t n) -> p t n", n=128)),
                       P[base:base + SEQ, col0:col0 + 128].map(lambda a: a.rearrange("(t p) n -> p t n", p=128)))
                for t0 in range(0, 18, 4):
                    nt_ = min(4, 18 - t0)
                    pt = ps.sub(512 * (4 + (t0 // 4) % 2), [128, 128 * nt_])
                    for t in range(nt_):
                        X("pe", "transpose", pt[:, 128 * t:128 * t + 128], stg[:, 128 * (t0 + t):128 * (t0 + t) + 128],
                          K.ident[:])
                    X("act", "copy", raw[:, 128 * t0:128 * (t0 + nt_)], pt[:])
                wc = lambda k: cw[:, cc * 5 + k:cc * 5 + k + 1]
                X("dve", "tensor_scalar", dst[:], raw[:], wc(2), None, ALU.mult)
                for k in (0, 1, 3, 4):
                    sh = k - 2
                    lo_o, hi_o = (0, 256 - sh) if sh > 0 else (-sh, 256)
                    X("dve", "scalar_tensor_tensor", dst[:, lo_o:hi_o], raw[:, lo_o + sh:hi_o + sh], wc(k),
                      dst[:, lo_o:hi_o], ALU.mult, ALU.add)
                    lo, hi = (0, 64 - sh) if sh > 0 else (-sh, 64)
                    v3 = lambda tl, a0, a1: tl[:, 256:SEQ].map(
                        lambda a: a.rearrange("p (r w) -> p r w", w=64)[:, :, a0:a1])
                    X("dve", "scalar_tensor_tensor", v3(dst, lo, hi), v3(raw, lo + sh, hi + sh), wc(k),
                      v3(dst, lo, hi), ALU.mult, ALU.add)
                X("act", "activation", tmp[:], dst[:], AF.Sigmoid)
                X("dve", "tensor_tensor", dst[:], dst[:], tmp[:], ALU.mult)
                if nrm is not None:
                    X("dve", "tensor_tensor", tmp[:], dst[:], dst[:], ALU.mult)
                    for pi, (p0, p1) in enumerate(PIECES):
                        n = p1 - p0
                        pp = ps.sub(512 * (6 + pi % 2), [128, n])
                        rs = rsp[pi % 2]
                        X("pe", "matmul", pp[:], K.ones[:], tmp[:, p0:p1], start=True, stop=True)
                        X("act", "activation", rs[:, 0:n], pp[:], AF.Sqrt, bias=EPS, scale=1.0)
                        X("dve", "reciprocal", rs[:, 0:n], rs[:, 0:n])
                        X("dve", "scalar_tensor_tensor", dst[:, p0:p1], dst[:, p0:p1], float(nrm), rs[:, 0:n],
                          ALU.mult, ALU.mult)
            if "p_stop" in FLAGS:
                continue
            for x in range(4):
                X("pool", "memset", S[x][:], 0.0)
            for h in range(2):
                X("pool", "memset", OACC[h][:], 0.0)

            def ut_unit(d, c, b):
                U = UT[d]
                cs = slice(64 * c, 64 * c + 64)
                gcol = [d * 16 + 2 * kh + h for h in range(2)]
                pA = lambda o, shp: ps.sub(U.bA + o, shp)
                pB = lambda o, shp: ps.sub(U.bB + o, shp)
                KK, QK, R = pA(0, [64, 64]), pA(64, [64, 64]), pA(128, [64, 2, 64])
                KTp, V0p, V1p = pB(0, [64, 128]), pB(128, [64, 128]), pB(256, [64, 128])
                YTp, Xn, Yn, Wn = pB(0, [64, 2, 64]), pA(0, [64, 2, 64]), pB(128, [64, 2, 64]), pB(256, [64, 2, 64])
                outp = c >= 4
                for h in range(2):
                    X("dve", "tensor_scalar", U.DG[:, h, :], id64[:], GC[:, c, gcol[h]:gcol[h] + 1], None, ALU.mult)
                if "nomm" not in FLAGS:
                    X("pe", "matmul", KK[:], kT[:, cs], kT[:, cs], start=True, stop=True)
                    if outp:
                        X("pe", "matmul", QK[:], kT[:, cs], qT[:, cs], start=True, stop=True)
                if "notr" not in FLAGS:
                    X("pe", "matmul", KTp[:], kT[:, cs], K.ident[:], start=True, stop=True)
                    X("pe", "matmul", V0p[:], vT[0][:, cs], K.ident[:], start=True, stop=True)
                    X("pe", "matmul", V1p[:], vT[1][:, cs], K.ident[:], start=True, stop=True)
                if "nor" not in FLAGS:
                    X("pe", "matmul", R[:].map(lambda a: a.rearrange("p a b -> p (a b)")), ones64[:],
                      U.DG[:].map(lambda a: a.rearrange("p a b -> p (a b)")), start=True, stop=True)
                if "notr" not in FLAGS and "nocp" not in FLAGS:
                    ce = ("dve", "tensor_copy") if "dvecp" in FLAGS else ("act", "copy")
                    if "rdkk" in FLAGS:
                        X(ce[0], ce[1], RHS2[d][:, 0:64], KK[:])
                    elif "dst2" in FLAGS:
                        X(ce[0], ce[1], RHS2[d][:], KTp[:])
                    elif "src2" in FLAGS:
                        X(ce[0], ce[1], U.KTOK[b][:], RHS2[d][:])
                    else:
                        X(ce[0], ce[1], U.KTOK[b][:], KTp[:])
                    if "novt" not in FLAGS:
                        X(ce[0], ce[1], U.VTOK[b][:, 0, :], V0p[:])
                        X(ce[0], ce[1], U.VTOK[b][:, 1, :], V1p[:])
                yield
                for h in range(2):
                    X("dve", "tensor_tensor", U.E[:, h, :], R[:, h, :], MASKI[d][:, h, :], ALU.mult)
                for h in range(2):
                    X("dve", "scalar_tensor_tensor", U.E[:, h, :], MASKI[d][:, h, :], Gg[:, c, gcol[h]:gcol[h] + 1],
                      U.E[:, h, :], ALU.mult, ALU.add)
                X("act", "activation", U.E[:], U.E[:], AF.Exp)
                yield
                X("dve", "tensor_tensor", U.Ei[:], U.E[:], MASKI[d][:], ALU.mult)
                X("dve", "tensor_tensor", U.NEs[:], U.E[:], NMASKS[d][:], ALU.mult)
                yield
                Xc, Yc, Wc = U.Xb[0], U.Yb[0], U.Wb[0]
                for h in range(2):
                    X("dve", "scalar_tensor_tensor", Xc[:, h, :], KK[:], BETA[:, c, gcol[h]:gcol[h] + 1],
                      U.NEs[:, h, :], ALU.mult, ALU.mult)
                    if outp:
                        X("dve", "tensor_tensor", U.PT[b][:, h, :], QK[:], U.Ei[:, h, :], ALU.mult)
                X("dve", "tensor_tensor", Wc[:], Xc[:], ID2[:], ALU.add)
                yield
                for h in range(2):
                    X("pe", "matmul", YTp[:, h, :], Xc[:, h, :], id64[:], start=True, stop=True)
                X("act", "copy", Yc[:], YTp[:])
                yield
                for k in range(1, 7):
                    Xn_, Yn_ = U.Xb[k % 2], U.Yb[k % 2]
                    do_sq = k <= 5
                    need_x = k <= 4
                    if do_sq:
                        if need_x:
                            for h in range(2):
                                X("pe", "matmul", Xn[:, h, :], Yc[:, h, :], Xc[:, h, :], start=True, stop=True)
                        for h in range(2):
                            X("pe", "matmul", Yn[:, h, :], Xc[:, h, :], Yc[:, h, :], start=True, stop=True)
                    if k >= 2:
                        for h in range(2):
                            X("pe", "matmul", Wn[:, h, :], Yc[:, h, :], Wc[:, h, :], start=True, stop=True)
                    if do_sq:
                        X("act", "copy", Yn_[:], Yn[:])
                        if need_x:
                            X("dve", "tensor_copy", Xn_[:], Xn[:])
                    if k >= 2:
                        Wn_ = U.TT[b] if k == 6 else U.Wb[k % 2]
                        X("dve", "tensor_tensor", Wn_[:], Wn[:], Wc[:], ALU.add)
                        Wc = Wn_
                    if do_sq:
                        Xc, Yc = Xn_, Yn_
                    yield

            def rec(h, d, c, b):
                U = UT[d]
                x = 2 * d + h
                gc = d * 16 + 2 * kh + h
                col = lambda T_: T_[:, c, gc:gc + 1]
                cs = slice(64 * c, 64 * c + 64)
                bk = 2048 + 512 * x
                r0, r1, r2, rS = (ps.sub(bk, [64, 128]), ps.sub(bk + 128, [64, 128]), ps.sub(bk + 256, [64, 128]),
                                  ps.sub(bk + 384, [128, 128]))
                outp = c >= 4
                X("pe", "matmul", r0[:], kT[:, cs], S[x][:], start=True, stop=True)
                yield
                X("dve", "scalar_tensor_tensor", RHS2[x][:], r0[:], col(NEGC), U.VTOK[b][:, h, :], ALU.mult, ALU.add)
                yield
                X("pe", "matmul", r1[:], U.TT[b][:, h, :], RHS2[x][:], start=True, stop=True)
                if outp:
                    X("pe", "matmul", r2[:], qT[:, cs], S[x][:], start=True, stop=True)
                yield
                X("dve", "tensor_scalar", VNEW[x][:], r1[:], col(BETA), None, ALU.mult)
                X("dve", "tensor_scalar", VN2[x][:], VNEW[x][:], col(DK), None, ALU.mult)
                if outp:
                    X("dve", "scalar_tensor_tensor", OACC[h][:, c, :], r2[:], col(EGC), OACC[h][:, c, :],
                      ALU.mult, ALU.add)
                yield
                X("pe", "matmul", rS[:], U.KTOK[b][:], VN2[x][:], start=True, stop=True)
                if outp:
                    X("pe", "matmul", r0[:], U.PT[b][:, h, :], VNEW[x][:], start=True, stop=True)
                yield
                X("dve", "scalar_tensor_tensor", S[x][:], S[x][:], EGL[:, c, gc:gc + 1], rS[:], ALU.mult, ALU.add)
                if outp:
                    X("dve", "tensor_tensor", OACC[h][:, c, :], OACC[h][:, c, :], r0[:], ALU.add)
                yield

            order = [[0, 1, 2, 3] + list(range(4, 36)), [3, 2, 1, 0] + list(range(35, 3, -1))]
            interleave([ut_unit(0, order[0][0], 0), ut_unit(1, order[1][0], 0)])
            if "u_stop" in FLAGS:
                continue
            for n in range(1 if "r_stop" in FLAGS else 36):
                gens = []
                if n + 1 < 36:
                    gens += [ut_unit(0, order[0][n + 1], (n + 1) % 2), ut_unit(1, order[1][n + 1], (n + 1) % 2)]
                gens += [rec(h, d, order[d][n], n % 2) for d in range(2) for h in range(2)]
                interleave(gens)
            for h in range(2):
                hv = 2 * kh + h
                o = OACC[h][:, 4:36, :]
                fw.dma("sync", Zt[:], P[base + 256:base + SEQ, 1024 + hv * 128:1024 + hv * 128 + 128].map(
                    lambda a: a.rearrange("(c p) n -> p c n", p=64)))
                tq = K.arena.sub(tmp.foff, [64, 32, 128])
                tq2 = K.arena.sub(tmp.foff + 4096, [64, 32, 128]) if False else None
                X("dve", "tensor_tensor", tq[:], o, o, ALU.mult)
                X("dve", "reduce_sum", ssr[:], tq[:], AX.X)
                X("act", "activation", ssr[:], ssr[:], AF.Sqrt, bias=EPS, scale=1.0 / 128)
                X("dve", "reciprocal", ssr[:], ssr[:])
                X("dve", "tensor_tensor", o, o, ssr[:].map(lambda a: a.unsqueeze(2).to_broadcast([64, 32, 128])), ALU.mult)
                X("dve", "tensor_tensor", o, o, onw[:].map(lambda a: a.unsqueeze(1).to_broadcast([64, 32, 128])), ALU.mult)
                X("act", "activation", tq[:], Zt[:], AF.Sigmoid)
                X("dve", "tensor_tensor", tq[:], tq[:], Zt[:], ALU.mult)
                X("dve", "tensor_tensor", o, o, tq[:], ALU.mult)
                fw.dma("pool", O[base + 256:base + SEQ, hv * 128:hv * 128 + 128].map(
                    lambda a: a.rearrange("(c p) n -> p c n", p=64)), o)


def layer1(K):
    m = K.mods[1]
    gemm3(K, K.X2, 1024, K.win, 6208, K.P, K.segs_all, "none", pre=(K.n1w[1], m, 0, 1))
    gdn_core(K, K.P, K.O)
    gemm3(K, K.O, 2048, K.wout, 1024, K.X3, K.segs_lat, "res", R=K.X2, mods=m, kgate=2)
    gemm3(K, K.X3, 1024, K.w1[1], 4096, K.HID, K.segs_lat, "relu2", pre=(K.n2w[1], m, 3, 4))
    gemm3(K, K.HID, 4096, K.w2[1], 1024, K.X4, K.segs_lat, "res", R=K.X3, mods=m, kgate=5)


def build_program():
    nc = bass.Bass("TRN2", target_bir_lowering=False)
    fw = Fw(nc)
    K = setup(fw, dbg=False)
    mods_phase(K)
    s5_setup(K)
    layer0(K)
    layer1(K)
    normmod(K, K.X4, K.out, K.segs_lat, K.fnw, None, None, None, omap=True)
    with nc.allow_low_precision("bf16 hi/lo 3-pass split emulating fp32 matmuls"):
        fw.finish()
    return nc


def kernel(**inputs):
    inp = {k: np.asarray(v) for k, v in inputs.items()}
    nc = build_program()
    shared = host_shared(inp)
    in_maps = []
    for core in range(8):
        d = dict(shared)
        d.update(host_core(inp, core))
        in_maps.append(d)
    res = run_bass_kernel_spmd(nc, in_maps, core_ids=list(range(8)))
    out = np.zeros((16, 2048, 1024), np.float32)
    for core in range(8):
        o = res.results[core]["out"]
        out[2 * core] = o[:2048]
        out[2 * core + 1] = o[2048:]
    return out
```

```python
import math
from concourse.bass_utils import run_bass_kernel_spmd
import contextlib
import numpy as np
import concourse.bass as bass
import concourse.mybir as mybir

F32 = mybir.dt.float32
BF16 = mybir.dt.bfloat16
ALU = mybir.AluOpType
AF = mybir.ActivationFunctionType
AX = mybir.AxisListType

NDSEM = 8


def _ov(a, b):
    return a[0] < b[1] and b[0] < a[1] and a[2] < b[3] and b[2] < a[3]


def _cov(a, b):
    return a[0] <= b[0] and a[1] >= b[1] and a[2] <= b[2] and a[3] >= b[3]


class V:
    __slots__ = ("tile", "ap", "box")

    def __init__(self, tile, ap, box):
        self.tile, self.ap, self.box = tile, ap, box

    def map(self, f):
        return V(self.tile, f(self.ap), self.box)


class Tile:
    def __init__(self, h, shape, name, base=None, foff=0):
        self.h = h
        self.shape = tuple(shape)
        self.name = name
        self.base = base
        self.foff = foff
        st = []
        acc = 1
        for n in reversed(self.shape[1:]):
            st.append(acc)
            acc *= n
        self.fstr = list(reversed(st))
        if base is None:
            self.writes = []
            self.reads = []

    def sub(self, foff, shape, name="sub"):
        n = 1
        for s in shape[1:]:
            n *= s
        assert len(self.shape) == 2 and foff + n <= self.shape[1], (name, foff, n, self.shape)
        ap = self.h[0:shape[0], foff:foff + n]
        if len(shape) == 3:
            ap = ap.rearrange("p (a b) -> p a b", a=shape[1], b=shape[2])
        elif len(shape) == 4:
            ap = ap.rearrange("p (a b c) -> p a b c", a=shape[1], b=shape[2], c=shape[3])
        return Tile(_APH(ap), shape, name, base=self.base or self, foff=self.foff + foff)

    def __getitem__(self, idx):
        if not isinstance(idx, tuple):
            idx = (idx,)
        idx = idx + (slice(None),) * (len(self.shape) - len(idx))
        lo, hi = [], []
        for s, n in zip(idx, self.shape):
            if isinstance(s, int):
                lo.append(s)
                hi.append(s + 1)
            else:
                a = 0 if s.start is None else s.start
                b = n if s.stop is None else s.stop
                stp = 1 if s.step is None else s.step
                assert 0 <= a < b <= n, (self.name, idx, self.shape)
                lo.append(a)
                hi.append(a + ((b - a - 1) // stp) * stp + 1)
        f0 = sum(l * s for l, s in zip(lo[1:], self.fstr))
        f1 = sum((h - 1) * s for h, s in zip(hi[1:], self.fstr)) + 1
        bt = self.base or self
        f0 += self.foff
        f1 += self.foff
        if getattr(bt, "bankgran", 0):
            g = bt.bankgran
            return V(bt, self.h[idx], (0, 128, (f0 // g) * g, -(-f1 // g) * g))
        return V(bt, self.h[idx], (lo[0], hi[0], f0, f1))


class _APH:
    def __init__(self, ap):
        self.ap = ap

    def __getitem__(self, idx):
        return self.ap[idx]


class Eng:
    def __init__(self, name, e, sem, dsems):
        self.name, self.e, self.sem, self.dsems = name, e, sem, dsems
        self.count = 0
        self.ndma = 0
        self.seen = {}
        self.prog = []


class Fw:
    def __init__(self, nc):
        self.nc = nc
        self.stack = contextlib.ExitStack()
        self.engs = {}
        self.semobj = {}
        for name, e, nd in (("pe", nc.tensor, 0), ("dve", nc.vector, 0), ("act", nc.scalar, NDSEM),
                            ("pool", nc.gpsimd, NDSEM), ("sync", nc.sync, NDSEM)):
            sem = self.stack.enter_context(nc.semaphore("s_" + name))
            ds = [self.stack.enter_context(nc.semaphore("d_%s%d" % (name, i))) for i in range(nd)]
            self.engs[name] = Eng(name, e, sem, ds)
        self.ntile = 0

    def sbuf(self, shape, dtype=F32, name=None):
        self.ntile += 1
        name = name or "t%d" % self.ntile
        h = self.stack.enter_context(self.nc.sbuf_tensor(name, list(shape), dtype))
        return Tile(h, shape, name)

    def psum(self, shape, dtype=F32, name=None):
        self.ntile += 1
        name = name or "p%d" % self.ntile
        h = self.stack.enter_context(self.nc.psum_tensor(name, list(shape), dtype))
        t = Tile(h, shape, name)
        t.bankgran = 512
        return t

    def dram(self, name, shape, dtype=F32, kind="Internal"):
        h = self.nc.dram_tensor(name, list(shape), dtype, kind=kind).ap()
        return Tile(h, shape, name)

    def _emit(self, ename, fn, w, r, dma=False):
        E = self.engs[ename]
        need = {}
        w = list(w) + [v for v in r if getattr(v.tile, "bankgran", 0)]

        def add(tok, kind):
            s, v = tok
            if s is E.sem:
                if ename == "pe":
                    return
            if need.get(id(s), (None, 0))[1] < v:
                need[id(s)] = (s, v)

        for v in r:
            for box, tok in v.tile.writes:
                if _ov(box, v.box):
                    add(tok, "raw")
        for v in w:
            for box, tok in v.tile.writes:
                if _ov(box, v.box):
                    add(tok, "waw")
            for box, tok in v.tile.reads:
                if _ov(box, v.box):
                    add(tok, "war")
        if dma:
            slot = E.ndma % NDSEM
            rnd = E.ndma // NDSEM
            E.ndma += 1
            dsem = E.dsems[slot]
            if rnd > 0:
                if need.get(id(dsem), (None, 0))[1] < 16 * rnd:
                    need[id(dsem)] = (dsem, 16 * rnd)
            tok = (dsem, 16 * (rnd + 1))
        else:
            E.count += 1
            tok = (E.sem, E.count)
        for s, v in need.values():
            if E.seen.get(id(s), 0) < v:
                E.seen[id(s)] = v
                E.prog.append(("w", s, v))
        E.prog.append(("i", fn, tok[0], 16 if dma else 1))
        for v in r:
            t = v.tile
            t.reads = [(b, k) for (b, k) in t.reads if not (k[0] is tok[0] and _cov(v.box, b))]
            t.reads.append((v.box, tok))
            if len(t.reads) > 48:
                self._compact(t)
        for v in w:
            t = v.tile
            t.writes = [(b, k) for (b, k) in t.writes if not _cov(v.box, b)]
            t.reads = [(b, k) for (b, k) in t.reads if not _cov(v.box, b)]
            t.writes.append((v.box, tok))
            if len(t.writes) > 48:
                self._compact(t)
        return tok

    def _compact(self, t):
        for attr in ("reads", "writes"):
            m = {}
            for b, k in getattr(t, attr):
                key = id(k[0])
                if key in m:
                    ob, ok = m[key]
                    m[key] = ((min(ob[0], b[0]), max(ob[1], b[1]), min(ob[2], b[2]), max(ob[3], b[3])),
                              (k[0], max(ok[1], k[1])))
                else:
                    m[key] = (b, k)
            setattr(t, attr, list(m.values()))

    def X(self, eng, meth, out, *args, **kw):
        ws, rs, a2, k2 = [out], [], [], {}
        for a in args:
            if isinstance(a, V):
                rs.append(a)
                a2.append(a.ap)
            else:
                a2.append(a)
        for k, a in kw.items():
            if isinstance(a, V):
                (ws if k == "accum_out" else rs).append(a)
                k2[k] = a.ap
            else:
                k2[k] = a
        oap = out.ap
        return self._emit(eng, lambda e: getattr(e, meth)(oap, *a2, **k2), ws, rs)

    def pe(self, fn, w, r):
        return self._emit("pe", fn, w, r)

    def dve(self, fn, w, r):
        return self._emit("dve", fn, w, r)

    def act(self, fn, w, r):
        return self._emit("act", fn, w, r)

    def pool(self, fn, w, r):
        return self._emit("pool", fn, w, r)

    def dma(self, q, out, in_, **kw):
        return self._emit(q, lambda e: e.dma_start(out=out.ap, in_=in_.ap, **kw), [out], [in_], dma=True)

    def finish(self):
        S = self.engs["sync"]
        for E in self.engs.values():
            if E.count > 0 and E is not S:
                S.prog.append(("w", E.sem, E.count))
            for i, ds in enumerate(E.dsems):
                n = (E.ndma - i + NDSEM - 1) // NDSEM if E.ndma > i else 0
                if n > 0:
                    S.prog.append(("w", ds, 16 * n))
        nc = self.nc
        with nc.Block() as block:
            def run(E):
                def body(e):
                    for it in E.prog:
                        if it[0] == "w":
                            e.wait_ge(it[1], it[2])
                        else:
                            it[1](e).then_inc(it[2], it[3])
                return body
            block.tensor(run(self.engs["pe"]))
            block.vector(run(self.engs["dve"]))
            block.scalar(run(self.engs["act"]))
            block.gpsimd(run(self.engs["pool"]))
            block.sync(run(self.engs["sync"]))
        self.stack.close()
        return {k: (E.count, E.ndma) for k, E in self.engs.items()}


I32 = mybir.dt.int32
EPS = 1e-6
SEQ = 2304
NT = 4608
TWO_PI = 2.0 * math.pi


def bc(ap):
    return ap.partition_broadcast(128)


class Ctx:
    pass


def setup(fw, dbg=False, ext_in=()):
    K = Ctx()
    K.fw = fw
    ext = lambda n, s: fw.dram(n, s, kind="ExternalInput")
    K.xin = ext("xin", [NT, 1024])
    K.cvec = ext("cvec", [128, 1024])
    K.ada_w = [ext("ada_w%d" % l, [1024, 6144]) for l in range(2)]
    K.ada_b = [ext("ada_b%d" % l, [1, 6144]) for l in range(2)]
    K.n1w = [ext("n1w%d" % l, [1, 1024]) for l in range(2)]
    K.n2w = [ext("n2w%d" % l, [1, 1024]) for l in range(2)]
    K.fnw = ext("fnw", [1, 1024])
    K.w1 = [ext("w1_%d" % l, [1024, 4096]) for l in range(2)]
    K.w2 = [ext("w2_%d" % l, [4096, 1024]) for l in range(2)]
    K.wglu = ext("wglu", [1024, 2048])
    K.win = ext("win", [1024, 6208])
    K.wout = ext("wout", [2048, 1024])
    K.lre = ext("lre", [128, 128])
    K.lim = ext("lim", [128, 128])
    K.ldt = ext("ldt", [128, 128])
    K.bz = ext("bz", [128, 2048])
    K.biz = ext("biz", [128, 2048])
    K.cz = ext("cz", [128, 2048])
    K.ciz = ext("ciz", [128, 2048])
    K.s5d = ext("s5d", [128, 8])
    K.convw = ext("convw", [128, 160])
    K.alog = ext("alog", [1, 32])
    K.dtb = ext("dtb", [1, 32])
    K.onw = ext("onw", [1, 128])
    K.cst = ext("cst", [128, 2048])
    K.ttab = ext("ttab", [4, SEQ])
    K.out = fw.dram("out", [4096, 1024], kind="ExternalOutput")
    sk = "ExternalOutput" if dbg else "Internal"
    scr = lambda n, s: fw.dram(n, s, kind=("ExternalInput" if n in ext_in else sk))
    K.silu_c = scr("silu_c", [128, 1024])
    K.mods = [scr("mods%d" % l, [128, 6144]) for l in range(2)]
    K.H = scr("H", [NT, 1024])
    K.Y = scr("Y", [NT, 1024])
    K.X1 = scr("X1", [NT, 1024])
    K.X2 = scr("X2", [NT, 1024])
    K.HID = scr("HID", [NT, 4096])
    K.P = scr("P", [NT, 6208])
    K.O = scr("O", [NT, 2048])
    K.X3 = scr("X3", [NT, 1024])
    K.X4 = scr("X4", [NT, 1024])
    K.C = fw.sbuf([128, 2048], name="consts")
    fw.dma("sync", K.C[:], K.cst[:])
    K.ident = K.C.sub(0, [128, 128])
    K.ones = K.C.sub(128, [128, 128])
    K.sgnA = K.C.sub(256, [128, 1])
    K.sgnB = K.C.sub(257, [128, 1])
    K.gm = K.C.sub(264, [128, 8])
    K.arena = fw.sbuf([128, 41 * 1024], name="arena")
    K.ps = fw.psum([128, 4096], name="ps")
    K.segs_all = []
    K.segs_lat = []
    for s in range(2):
        K.segs_all += [(s * SEQ, 2, 2), (s * SEQ + 256, 16, s)]
        K.segs_lat += [(s * SEQ + 256, 16, s)]
    return K


def host_consts():
    c = np.zeros((128, 2048), np.float32)
    c[:, 0:128] = np.eye(128)
    c[:, 128:256] = 1.0
    c[:64, 256] = -1.0
    c[64:, 256] = 1.0
    c[:64, 257] = 1.0
    c[64:, 257] = -1.0
    for g in range(8):
        c[16 * g:16 * g + 16, 264 + g] = 1.0
    return gdn_consts(c)


class Arena:
    def __init__(self, K):
        self.K = K
        self.off = 0

    def get(self, shape, name="a"):
        n = 1
        for s in shape[1:]:
            n *= s
        t = self.K.arena.sub(self.off, shape, name)
        self.off += n
        return t


def mods_phase(K):
    fw = K.fw
    A = Arena(K)
    t = A.get([128, 1024])
    s = A.get([128, 1024])
    fw.dma("sync", t[:], K.cvec[:])
    fw.X("act", "activation", s[:], t[:], AF.Sigmoid)
    fw.X("dve", "tensor_tensor", s[:], s[:], t[:], ALU.mult)
    fw.dma("pool", K.silu_c[:], s[:])
    for l in range(2):
        gemm(K, K.silu_c, 1024, K.ada_w[l], 6144, K.mods[l], [(0, 1, 0)], "bias", bias=K.ada_b[l])


def normmod(K, X, Y, segs, wrow, mods, ksh, ksc, yoff=0, omap=False):
    fw = K.fw
    A = Arena(K)
    Ar, Br, Tr = A.get([128, 1024]), A.get([128, 1024]), A.get([128, 1024])
    xt = [A.get([128, 1024]) for _ in range(2)]
    yt = [A.get([128, 1024]) for _ in range(2)]
    junk = A.get([128, 1024])
    ss = [A.get([128, 1]) for _ in range(2)]
    n = 0
    for (r0, nt, mr) in segs:
        fw.dma("sync", Ar[:], wrow[:].map(bc))
        if ksc is not None:
            fw.dma("sync", Tr[:], mods[mr:mr + 1, ksc * 1024:(ksc + 1) * 1024].map(bc))
            fw.X("dve", "scalar_tensor_tensor", Ar[:], Tr[:], 1.0, Ar[:], ALU.add, ALU.mult)
            fw.dma("sync", Br[:], mods[mr:mr + 1, ksh * 1024:(ksh + 1) * 1024].map(bc))
        for t in range(nt):
            x, y, s = xt[n % 2], yt[n % 2], ss[n % 2]
            n += 1
            rows = slice(r0 + 128 * t, r0 + 128 * t + 128)
            fw.dma("sync", x[:], X[rows, :])
            fw.X("dve", "memset", s[:], 0.0)
            fw.X("act", "activation", junk[:], x[:], AF.Square, accum_out=s[:])
            fw.X("act", "activation", s[:], s[:], AF.Sqrt, bias=EPS, scale=1.0 / 1024)
            fw.X("dve", "reciprocal", s[:], s[:])
            fw.X("dve", "scalar_tensor_tensor", y[:], x[:], s[:], Ar[:], ALU.mult, ALU.mult)
            if ksc is not None:
                fw.X("pool", "tensor_tensor", y[:], y[:], Br[:], ALU.add)
            orows = slice(rows.start - yoff, rows.stop - yoff)
            if omap:
                sq = rows.start // SEQ
                orows = slice(rows.start - 256 * (sq + 1), rows.stop - 256 * (sq + 1))
            fw.dma("pool", Y[orows, :], y[:])


def gemm(K, X, Kd, W, N, Y, segs, epi, bias=None, R=None, mods=None, kgate=None, woff=0, xoff=0, pre=None):
    fw = K.fw
    KC = Kd // 128
    NP = 512 if KC <= 16 else 256
    G = 4 if KC <= 16 else 2
    A = Arena(K)
    gx = [A.get([128, Kd]) for _ in range(2)]
    XT = A.get([128, KC, G * 128])
    nwb = 2 if epi != "glu" else 4
    Wp = [A.get([128, KC, NP]) for _ in range(nwb)]
    ot = [A.get([128, NP]) for _ in range(2)]
    rt = [A.get([128, NP]) for _ in range(2)]
    sg = [A.get([128, NP]) for _ in range(2)]
    grow = A.get([128, 1024])
    if pre is not None:
        assert Kd == 1024
        pAr, pBr, pTr, pjunk = [A.get([128, 1024]) for _ in range(4)]
        prss = [A.get([128, 1]) for _ in range(2)]
        npre = 0
    Wv = W.h.rearrange("(kc p) n -> p kc n", p=128)
    Wt = Tile(_APHk(Wv), [128, KC, W.shape[1]], W.name + "_v")
    nps = 0
    npw = 0
    nev = 0
    for (r0, nt, mr) in segs:
        if epi in ("res", "glu"):
            fw.dma("sync", grow[:], mods[mr:mr + 1, kgate * 1024:(kgate + 1) * 1024].map(bc))
        if pre is not None:
            pw, pm, pksh, pksc = pre
            fw.dma("sync", pAr[:], pw[:].map(bc))
            fw.dma("sync", pTr[:], pm[mr:mr + 1, pksc * 1024:(pksc + 1) * 1024].map(bc))
            fw.X("dve", "scalar_tensor_tensor", pAr[:], pTr[:], 1.0, pAr[:], ALU.add, ALU.mult)
            fw.dma("sync", pBr[:], pm[mr:mr + 1, pksh * 1024:(pksh + 1) * 1024].map(bc))
        for g0 in range(0, nt, G):
            gn = min(G, nt - g0)
            for t in range(gn):
                x = gx[t % 2]
                rows = slice(r0 + 128 * (g0 + t) - xoff, r0 + 128 * (g0 + t) + 128 - xoff)
                fw.dma("sync", x[:], X[rows, :])
                if pre is not None:
                    sq = prss[npre % 2]
                    npre += 1
                    fw.X("dve", "memset", sq[:], 0.0)
                    fw.X("act", "activation", pjunk[:], x[:], AF.Square, accum_out=sq[:])
                    fw.X("act", "activation", sq[:], sq[:], AF.Sqrt, bias=EPS, scale=1.0 / 1024)
                    fw.X("dve", "reciprocal", sq[:], sq[:])
                    fw.X("dve", "scalar_tensor_tensor", x[:], x[:], sq[:], pAr[:], ALU.mult, ALU.mult)
                    fw.X("dve", "tensor_tensor", x[:], x[:], pBr[:], ALU.add)
                for kc in range(KC):
                    pt = K.ps.sub(512 * (nps % 2), [128, 128])
                    nps += 1
                    fw.X("pe", "transpose", pt[:], x[:, 128 * kc:128 * kc + 128], K.ident[:])
                    fw.X("act" if kc % 2 else "dve", "copy" if kc % 2 else "tensor_copy",
                         XT[:, kc, 128 * t:128 * t + 128], pt[:])
            for n0 in range(0, N, NP):
                w = min(NP, N - n0)
                if epi == "glu":
                    cols = [n0, 1024 + n0]
                else:
                    cols = [woff + n0]
                wps = []
                for c0 in cols:
                    wp = Wp[npw % nwb]
                    npw += 1
                    wps.append(wp)
                    for k0 in range(0, KC, 8):
                        fw.dma("sync", wp[:, k0:k0 + 8, 0:w], Wt[:, k0:k0 + 8, c0:c0 + w])
                for t in range(gn):
                    rows = slice(r0 + 128 * (g0 + t), r0 + 128 * (g0 + t) + 128)
                    pss = []
                    for wi, wp in enumerate(wps):
                        py = K.ps.sub(1024 + 512 * (nev % 4), [128, w])
                        nev += 1
                        pss.append(py)
                        for kc in range(KC):
                            fw.X("pe", "matmul", py[:], XT[:, kc, 128 * t:128 * t + 128], wp[:, kc, 0:w],
                                 start=(kc == 0), stop=(kc == KC - 1))
                    o = ot[nev % 2]
                    py = pss[0]
                    if epi == "none":
                        fw.X("act", "copy", o[:, 0:w], py[:])
                    elif epi == "relu2":
                        fw.X("act", "activation", o[:, 0:w], py[:], AF.Relu)
                        fw.X("pool", "tensor_tensor", o[:, 0:w], o[:, 0:w], o[:, 0:w], ALU.mult)
                    elif epi == "bias":
                        r = rt[nev % 2]
                        fw.dma("sync", r[:, 0:w], bias[0:1, n0:n0 + w].map(bc))
                        fw.X("dve", "tensor_tensor", o[:, 0:w], py[:], r[:, 0:w], ALU.add)
                    elif epi in ("res", "glu"):
                        r = rt[nev % 2]
                        fw.dma("sync", r[:, 0:w], R[rows, n0:n0 + w])
                        sgt = sg[nev % 2]
                        if epi == "glu":
                            fw.X("act", "activation", sgt[:, 0:w], pss[1][:], AF.Sigmoid)
                            fw.X("dve", "tensor_tensor", sgt[:, 0:w], py[:], sgt[:, 0:w], ALU.mult)
                            fw.X("dve", "tensor_tensor", sgt[:, 0:w], sgt[:, 0:w], grow[:, n0:n0 + w], ALU.mult)
                        else:
                            fw.X("dve", "tensor_tensor", sgt[:, 0:w], py[:], grow[:, n0:n0 + w], ALU.mult)
                        fw.X("pool", "tensor_tensor", o[:, 0:w], sgt[:, 0:w], r[:, 0:w], ALU.add)
                    fw.dma("pool", Y[rows, n0:n0 + w], o[:, 0:w])


class _APHk:
    def __init__(self, ap):
        self.ap = ap

    def __getitem__(self, idx):
        return self.ap[idx]


def s5_setup(K):
    fw = K.fw
    Pm = fw.sbuf([128, 3 * 128 + 4 * 2048], name="s5p")
    K.s5R = Pm.sub(0, [128, 128])
    K.s5thn = Pm.sub(128, [128, 128])
    K.s5thn64 = Pm.sub(256, [128, 128])
    K.bbar = Pm.sub(384, [128, 2048])
    K.ibbar = Pm.sub(384 + 2048, [128, 2048])
    K.cstm = Pm.sub(384 + 4096, [128, 2048])
    K.cstim = Pm.sub(384 + 6144, [128, 2048])
    K.s5dt = fw.sbuf([128, 8], name="s5dsb")
    fw.dma("sync", K.s5dt[:], K.s5d[:])
    A = Arena(K)
    g = lambda: A.get([128, 128])
    lre, lim, ldt, th, rho, un, fr, cs, sn, are, aim, xm, t1, t2, cre, cim = [g() for _ in range(16)]
    ki = K.arena.sub(A.off, [128, 128])
    A.off += 128
    kint = Tile(_APHk(ki.h.ap.bitcast(I32)), [128, 128], "kint", base=ki.base, foff=ki.foff)
    fw.dma("sync", lre[:], K.lre[:])
    fw.dma("sync", lim[:], K.lim[:])
    fw.dma("sync", ldt[:], K.ldt[:])
    X = fw.X
    X("act", "activation", ldt[:], ldt[:], AF.Exp)
    X("dve", "tensor_tensor", th[:], lim[:], ldt[:], ALU.mult)
    X("dve", "tensor_tensor", rho[:], lre[:], ldt[:], ALU.mult)
    X("act", "activation", K.s5R[:], rho[:], AF.Exp)

    def frac(dst, src):
        X("dve", "tensor_copy", kint[:], src[:])
        X("dve", "tensor_tensor", dst[:], src[:], kint[:], ALU.subtract)

    X("dve", "tensor_scalar", un[:], th[:], 1.0 / TWO_PI, None, ALU.mult)
    frac(K.s5thn, un)
    X("dve", "tensor_scalar", un[:], K.s5thn[:], 64.0, None, ALU.mult)
    frac(K.s5thn64, un)
    X("dve", "tensor_scalar", un[:], K.s5thn[:], 0.25, None, ALU.add)
    frac(fr, un)
    X("act", "activation", cs[:], fr[:], AF.Sin, scale=TWO_PI)
    X("act", "activation", sn[:], K.s5thn[:], AF.Sin, scale=TWO_PI)
    X("dve", "tensor_tensor", are[:], K.s5R[:], cs[:], ALU.mult)
    X("dve", "tensor_tensor", aim[:], K.s5R[:], sn[:], ALU.mult)
    X("dve", "tensor_scalar", xm[:], are[:], -1.0, None, ALU.add)
    X("dve", "tensor_tensor", t1[:], xm[:], lre[:], ALU.mult)
    X("dve", "tensor_tensor", t2[:], aim[:], lim[:], ALU.mult)
    X("dve", "tensor_tensor", cre[:], t1[:], t2[:], ALU.add)
    X("dve", "tensor_tensor", t1[:], aim[:], lre[:], ALU.mult)
    X("dve", "tensor_tensor", t2[:], xm[:], lim[:], ALU.mult)
    X("dve", "tensor_tensor", cim[:], t1[:], t2[:], ALU.subtract)
    X("dve", "tensor_tensor", t1[:], lre[:], lre[:], ALU.mult)
    X("dve", "tensor_tensor", t2[:], lim[:], lim[:], ALU.mult)
    X("dve", "tensor_tensor", t1[:], t1[:], t2[:], ALU.add)
    X("dve", "reciprocal", t1[:], t1[:])
    X("dve", "tensor_tensor", cre[:], cre[:], t1[:], ALU.mult)
    X("dve", "tensor_tensor", cim[:], cim[:], t1[:], ALU.mult)
    bz, ib, tmp = A.get([128, 128, 16]), A.get([128, 128, 16]), A.get([128, 128, 16])
    fw.dma("sync", bz[:], K.bz[:].map(lambda a: a.rearrange("p (a b) -> p a b", b=16)))
    fw.dma("sync", ib[:], K.biz[:].map(lambda a: a.rearrange("p (a b) -> p a b", b=16)))
    X("dve", "tensor_scalar", ib[:], ib[:], K.sgnA[:], None, ALU.mult)
    b3 = lambda t: t[:].map(lambda a: a.unsqueeze(2).to_broadcast([128, 128, 16]))
    v3 = lambda t: t[:].map(lambda a: a.rearrange("p (a b) -> p a b", b=16))
    X("dve", "tensor_tensor", v3(K.bbar), bz[:], b3(cre), ALU.mult)
    X("dve", "tensor_tensor", tmp[:], ib[:], b3(cim), ALU.mult)
    X("dve", "tensor_tensor", v3(K.bbar), v3(K.bbar), tmp[:], ALU.add)
    X("dve", "tensor_tensor", v3(K.ibbar), ib[:], b3(cre), ALU.mult)
    X("dve", "tensor_tensor", tmp[:], bz[:], b3(cim), ALU.mult)
    X("dve", "tensor_tensor", v3(K.ibbar), v3(K.ibbar), tmp[:], ALU.subtract)
    cz = A.get([128, 2048])
    fw.dma("sync", cz[:], K.cz[:])
    X("dve", "tensor_scalar", K.cstm[:], cz[:], K.sgnB[:], None, ALU.mult)
    cz2 = A.get([128, 2048])
    fw.dma("sync", cz2[:], K.ciz[:])
    X("dve", "tensor_scalar", K.cstim[:], cz2[:], -1.0, None, ALU.mult)


FLAGS = set()
PIECES = [(0, 512), (512, 1024), (1024, 1536), (1536, 2048), (2048, 2304)]


def s5_core(K, H, Y, chunks=range(8), dbg=None):
    fw = K.fw
    X = fw.X
    A = Arena(K)
    big = lambda: A.get([128, SEQ])
    hT = [big(), big()]
    yacc = [big(), big()]
    G0, gs, GS, U, SIN, COS, T1, T0 = [big() for _ in range(8)]
    G0s = [G0, big()]
    GSs = [GS, big()]
    npo = 0
    kraw = big()
    KI = Tile(_APHk(kraw.h.ap.bitcast(I32)), [128, SEQ], "KI", base=kraw.base, foff=kraw.foff)
    LB, LIB, LC, LCI = [A.get([128, 8, 128]) for _ in range(4)]
    tmp = [A.get([128, 512]) for _ in range(2)]
    stg = [A.get([128, 128]) for _ in range(4)]
    X("dve", "memset", LC[:], 0.0)
    X("dve", "memset", LCI[:], 0.0)
    rev = lambda v: v.map(lambda a: a[:, ::-1])
    nst = 0
    npp = 0
    for c in chunks:
        for s in range(2):
            for t in range(18):
                st = stg[nst % 4]
                nst += 1
                fw.dma("sync", st[:], H[s * SEQ + 128 * t: s * SEQ + 128 * t + 128, 128 * c:128 * c + 128])
                pt = K.ps.sub(512 * (nst % 2), [128, 128])
                X("pe", "transpose", pt[:], st[:], K.ident[:])
                X("act", "copy", hT[s][:, 128 * t:128 * t + 128], pt[:])
        first = True
        for d in range(2):
            fw.dma("sync", T1[:], K.ttab[2 * d:2 * d + 1, :].map(bc))
            fw.dma("sync", T0[:], K.ttab[2 * d + 1:2 * d + 2, :].map(bc))
            col0 = (d * 64 + 8 * c) * 16
            for (src, dst) in ((K.bbar, LB), (K.ibbar, LIB)):
                pt = K.ps.sub(1024, [128, 128])
                X("pe", "transpose", pt[:], src[:, col0:col0 + 128], K.ident[:])
                for gl in range(8):
                    X("dve", "tensor_scalar", dst[:, gl, :], pt[:], K.gm[:, gl:gl + 1], None, ALU.mult)
            for gl in range(8):
                X("pool", "tensor_copy", LC[:, gl, 16 * gl:16 * gl + 16], K.cstm[:, col0 + 16 * gl:col0 + 16 * gl + 16])
                X("pool", "tensor_copy", LCI[:, gl, 16 * gl:16 * gl + 16], K.cstim[:, col0 + 16 * gl:col0 + 16 * gl + 16])
            if "stop1" in FLAGS:
                continue
            for gl in range(8):
                dg = d * 64 + 8 * c + gl
                thn = K.s5thn[:, dg:dg + 1]
                thn64 = K.s5thn64[:, dg:dg + 1]
                rbc = lambda n: K.s5R[:, dg:dg + 1].map(lambda a: a.to_broadcast([128, n]))
                if "notab" in FLAGS:
                    X("dve", "memset", SIN[:], 0.0)
                    X("dve", "memset", COS[:], 1.0)
                for _ in ([] if "notab" in FLAGS else [0]):
                  X("dve", "tensor_scalar", U[:], T1[:], thn64, None, ALU.mult)
                  X("dve", "scalar_tensor_tensor", U[:], T0[:], thn, U[:], ALU.mult, ALU.add)
                  X("dve", "tensor_copy", KI[:], U[:])
                  X("dve", "tensor_tensor", SIN[:], U[:], KI[:], ALU.subtract)
                  X("act", "activation", SIN[:], SIN[:], AF.Sin, scale=TWO_PI)
                  X("dve", "tensor_scalar", U[:], U[:], 0.25, None, ALU.add)
                  X("dve", "tensor_copy", KI[:], U[:])
                  X("dve", "tensor_tensor", COS[:], U[:], KI[:], ALU.subtract)
                  X("act", "activation", COS[:], COS[:], AF.Sin, scale=TWO_PI)
                for s in range(2):
                    G0 = G0s[s]
                    for (p0, p1) in PIECES:
                        n = p1 - p0
                        ps1 = K.ps.sub(1536 + 1024 * (npp % 2), [128, n])
                        ps2 = K.ps.sub(2048 + 1024 * (npp % 2), [128, n])
                        tm = tmp[npp % 2]
                        npp += 1
                        X("pe", "matmul", ps1[:], LB[:, gl, :], hT[s][:, p0:p1], start=True, stop=True)
                        X("pe", "matmul", ps2[:], LIB[:, gl, :], hT[s][:, p0:p1], start=True, stop=True)
                        X("dve", "tensor_tensor", tm[:, 0:n], ps2[:], SIN[:, p0:p1], ALU.mult)
                        X("dve", "tensor_tensor", G0[:, p0:p1], ps1[:], COS[:, p0:p1], ALU.mult)
                        X("dve", "tensor_tensor", G0[:, p0:p1], G0[:, p0:p1], tm[:, 0:n], ALU.subtract)
                for s in range(2):
                    G0 = G0s[s]
                    GS = GSs[s]
                    if "noscan" in FLAGS:
                        X("dve", "tensor_copy", gs[:], G0[:])
                    elif d == 0:
                        X("dve", "tensor_tensor_scan", gs[:], rbc(SEQ), G0[:], 0.0, ALU.mult, ALU.add)
                    else:
                        X("dve", "tensor_tensor_scan", rev(gs[:, 0:256]), rbc(256), rev(G0[:, 0:256]), 0.0,
                          ALU.mult, ALU.add)
                        X("dve", "tensor_tensor_scan", rev(gs[:, 256:SEQ]), rbc(SEQ - 256), rev(G0[:, 256:SEQ]),
                          gs[:, 0:1], ALU.mult, ALU.add)
                    X("dve", "tensor_tensor", G0[:], gs[:], COS[:], ALU.mult)
                    X("dve", "tensor_tensor", GS[:], gs[:], SIN[:], ALU.mult)
                    for (p0, p1) in PIECES:
                        n = p1 - p0
                        po = K.ps.sub((3584, 0, 512)[npo % 3], [128, n])
                        npo += 1
                        X("pe", "matmul", po[:], LC[:, gl, :], G0[:, p0:p1], start=True, stop=False)
                        X("pe", "matmul", po[:], LCI[:, gl, :], GS[:, p0:p1], start=False, stop=True)
                        if first:
                            X("act", "copy", yacc[s][:, p0:p1], po[:])
                        else:
                            X("dve", "tensor_tensor", yacc[s][:, p0:p1], yacc[s][:, p0:p1], po[:], ALU.add)
                first = False
        G0 = G0s[0]
        for s in ([] if ("stop1" in FLAGS or "stop2" in FLAGS) else range(2)):
            y = yacc[s]
            X("dve", "scalar_tensor_tensor", y[:], hT[s][:], K.s5dt[:, c:c + 1], y[:], ALU.mult, ALU.add)
            X("dve", "tensor_tensor", G0[:], y[:], y[:], ALU.mult)
            X("dve", "tensor_scalar", G0[:], G0[:], 0.044715, 1.0, ALU.mult, ALU.add)
            X("dve", "tensor_tensor", G0[:], G0[:], y[:], ALU.mult)
            X("act", "activation", G0[:], G0[:], AF.Tanh, scale=math.sqrt(2.0 / math.pi))
            X("dve", "tensor_scalar", G0[:], G0[:], 1.0, 0.5, ALU.add, ALU.mult)
            X("dve", "tensor_tensor", y[:], G0[:], y[:], ALU.mult)
            for t in range(18):
                st = stg[nst % 4]
                nst += 1
                pt = K.ps.sub(512 * (nst % 2), [128, 128])
                X("pe", "transpose", pt[:], y[:, 128 * t:128 * t + 128], K.ident[:])
                X("act", "copy", st[:], pt[:])
                fw.dma("pool", Y[s * SEQ + 128 * t: s * SEQ + 128 * t + 128, 128 * c:128 * c + 128], st[:])


def layer0(K):
    m = K.mods[0]
    normmod(K, K.xin, K.H, K.segs_all, K.n1w[0], m, 0, 1)
    s5_core(K, K.H, K.Y)
    gemm(K, K.Y, 1024, K.wglu, 1024, K.X1, K.segs_all, "glu", R=K.xin, mods=m, kgate=2)
    gemm(K, K.X1, 1024, K.w1[0], 4096, K.HID, K.segs_all, "relu2", pre=(K.n2w[0], m, 3, 4))
    gemm(K, K.HID, 4096, K.w2[0], 1024, K.X2, K.segs_all, "res", R=K.X1, mods=m, kgate=5)


def host_shared(inp):
    f = lambda a: np.ascontiguousarray(np.asarray(a, dtype=np.float32))
    d = {}
    for l in range(2):
        d["ada_w%d" % l] = f(inp["ada_w"][l])
        d["ada_b%d" % l] = f(inp["ada_b"][l][None, :])
        d["n1w%d" % l] = f(inp["norm1_w"][l][None, :])
        d["n2w%d" % l] = f(inp["norm2_w"][l][None, :])
        d["w1_%d" % l] = f(inp["mlp_w1"][l])
        d["w2_%d" % l] = f(inp["mlp_w2"][l])
    d["fnw"] = f(inp["final_norm_w"][None, :])
    d["wglu"] = f(inp["s5_w_glu"][0])
    d["win"] = f(inp["gdn_w_in"][0])
    d["wout"] = f(inp["gdn_w_out"][0])
    lre = np.transpose(inp["s5_lam_re"][0], (2, 0, 1)).reshape(64, 128)
    lim = np.transpose(inp["s5_lam_im"][0], (2, 0, 1)).reshape(64, 128)
    d["lre"] = f(np.concatenate([lre, lre], 0))
    d["lim"] = f(np.concatenate([lim, lim], 0))
    d["ldt"] = f(np.broadcast_to(inp["s5_log_dt"][0].reshape(1, 128), (128, 128)))
    bre = np.transpose(inp["s5_b_re"][0], (2, 0, 1, 3)).reshape(64, 2048)
    bim = np.transpose(inp["s5_b_im"][0], (2, 0, 1, 3)).reshape(64, 2048)
    d["bz"] = f(np.concatenate([bre, bim], 0))
    d["biz"] = f(np.concatenate([bim, bre], 0))
    cre = np.transpose(inp["s5_c_re"][0], (3, 0, 1, 2)).reshape(64, 2048)
    cim = np.transpose(inp["s5_c_im"][0], (3, 0, 1, 2)).reshape(64, 2048)
    d["cz"] = f(np.concatenate([cre, cim], 0))
    d["ciz"] = f(np.concatenate([cim, cre], 0))
    d["s5d"] = f(inp["s5_d"][0].reshape(8, 128).T)
    d["convw"] = f(np.transpose(inp["gdn_conv_w"][0].reshape(5, 32, 128), (2, 1, 0)).reshape(128, 160))
    d["alog"] = f(inp["gdn_a_log"][0].reshape(1, 32))
    d["dtb"] = f(inp["gdn_dt_bias"][0].reshape(1, 32))
    d["onw"] = f(inp["gdn_onorm_w"][0][None, :])
    d["cst"] = host_consts()
    t = np.arange(SEQ)
    trev = np.where(t < 256, 255 - t, 2559 - t)
    d["ttab"] = f(np.stack([t // 64, t % 64, trev // 64, trev % 64]))
    return d


def host_core(inp, core):
    d = {}
    b0 = 2 * core
    x = np.asarray(inp["x"]); ctx = np.asarray(inp["ctx"])
    d["xin"] = np.ascontiguousarray(np.concatenate([ctx[b0], x[b0], ctx[b0 + 1], x[b0 + 1]], 0).astype(np.float32))
    cv = np.zeros((128, 1024), np.float32)
    cv[0] = inp["c"][b0]; cv[1] = inp["c"][b0 + 1]; cv[2] = inp["c_ctx"]
    d["cvec"] = cv
    return d


def gdn_consts(c):
    t = np.arange(64)
    c[:64, 512:576] = (t[:, None] <= t[None, :])
    c[:64, 576:640] = (t[:, None] >= t[None, :])
    c[63, 640:768] = 1.0
    c[0, 768:896] = 1.0
    ji = lambda f: np.concatenate([f(t[:, None], t[None, :])] * 2, 1).astype(np.float32)
    c[:64, 896:1024] = ji(lambda j, i: i >= j)
    c[:64, 1024:1152] = -ji(lambda j, i: i > j)
    c[:64, 1152:1280] = ji(lambda j, i: i <= j)
    c[:64, 1280:1408] = -ji(lambda j, i: i < j)
    c[:64, 1408:1536] = ji(lambda j, i: i == j)
    return c


def interleave(gens):
    gens = list(gens)
    lim = [int(f[2:]) for f in FLAGS if f.startswith("ut")]
    nround = 0
    while gens:
        if lim and nround >= lim[0]:
            return
        nround += 1
        nxt = []
        for g in gens:
            try:
                next(g)
                nxt.append(g)
            except StopIteration:
                pass
        gens = nxt


def gdn_core(K, P, O, seqs=(0, 1), khs=range(8)):
    fw = K.fw
    X = fw.X
    C = K.C
    id64 = C.sub(0, [64, 64])
    ones64 = C.sub(128, [64, 64])
    TRI = [C.sub(512, [64, 64]), C.sub(576, [64, 64])]
    SEL = [C.sub(640, [64, 128]), C.sub(768, [64, 128])]
    MASKI = [C.sub(896, [64, 2, 64]), C.sub(1152, [64, 2, 64])]
    NMASKS = [C.sub(1024, [64, 2, 64]), C.sub(1280, [64, 2, 64])]
    ID2 = C.sub(1408, [64, 2, 64])
    A = Arena(K)
    NG = 36 * 32
    BETA, Gg, GC, EGC, DK, NEGC = [A.get([64, 36, 32]) for _ in range(6)]
    EGL = A.get([128, 36, 32])
    stg, raw, tmp = A.get([128, SEQ]), A.get([128, SEQ]), A.get([128, SEQ])
    Graw = K.arena.sub(stg.foff, [64, 36, 64])
    Zt = K.arena.sub(stg.foff, [64, 32, 128])
    qT, kT = A.get([128, SEQ]), A.get([128, SEQ])
    vT = [A.get([128, SEQ]) for _ in range(2)]
    OACC = [A.get([64, 36, 128]) for _ in range(2)]
    t2 = lambda: A.get([64, 2, 64])
    UT = []
    for u in range(2):
        d = Ctx()
        d.DG, d.E, d.Ei, d.NEs = t2(), t2(), t2(), t2()
        d.Xb = [t2(), t2()]
        d.Yb = [t2(), t2()]
        d.Wb = [t2(), t2()]
        d.TT = [t2(), t2()]
        d.PT = [t2(), t2()]
        d.KTOK = [A.get([64, 128]) for _ in range(2)]
        d.VTOK = [A.get([64, 2, 128]) for _ in range(2)]
        d.bA = 512 * (2 * u)
        d.bB = 512 * (2 * u + 1)
        UT.append(d)
    RHS2 = [A.get([64, 128]) for _ in range(4)]
    VNEW = [A.get([64, 128]) for _ in range(4)]
    VN2 = [A.get([64, 128]) for _ in range(4)]
    S = [A.get([128, 128]) for _ in range(4)]
    rsp = [A.get([128, 512]) for _ in range(2)]
    onw = A.get([64, 128])
    rows32 = A.get([64, 64])
    ssr = A.get([64, 32])
    cw = A.get([128, 160])
    fw.dma("sync", cw[:], K.convw[:])
    fw.dma("sync", onw[:], K.onw[:].map(lambda a: a.partition_broadcast(64)))
    fw.dma("sync", rows32[:, 0:32], K.alog[:].map(lambda a: a.partition_broadcast(64)))
    fw.dma("sync", rows32[:, 32:64], K.dtb[:].map(lambda a: a.partition_broadcast(64)))
    X("act", "activation", rows32[:, 0:32], rows32[:, 0:32], AF.Exp)
    X("dve", "tensor_scalar", rows32[:, 0:32], rows32[:, 0:32], -1.0, None, ALU.mult)
    b36 = lambda v: v.map(lambda a: a.unsqueeze(1).to_broadcast([64, 36, 32]))
    ps = K.ps
    for s in seqs:
        base = s * SEQ
        fw.dma("sync", Graw[:], P[base:base + SEQ, 6144:6208].map(lambda a: a.rearrange("(c p) g -> p c g", p=64)))
        X("act", "activation", BETA[:], Graw[:, :, 0:32], AF.Sigmoid)
        X("dve", "tensor_tensor", Gg[:], Graw[:, :, 32:64], b36(rows32[:, 32:64]), ALU.add)
        X("act", "activation", Gg[:], Gg[:], AF.Exp)
        X("act", "activation", Gg[:], Gg[:], AF.Ln, bias=1.0)
        X("dve", "tensor_tensor", Gg[:], Gg[:], b36(rows32[:, 0:32]), ALU.mult)
        g2 = Gg[:].map(lambda a: a.rearrange("p c g -> p (c g)"))
        if "nogates" in FLAGS:
            X("dve", "memset", GC[:], -0.1)
            X("dve", "memset", EGL[:], -0.1)
        for d in ([] if "nogates" in FLAGS else range(2)):
            for j in range(3):
                pc = ps.sub(512 * j, [64, 384])
                X("pe", "matmul", pc[:], TRI[d][:], g2.map(lambda a: a[:, 384 * j:384 * j + 384]), start=True, stop=True)
                pc3 = pc[:].map(lambda a: a.rearrange("p (c g) -> p c g", g=32)[:, :, 16 * d:16 * d + 16])
                X("act", "copy", GC[:, 12 * j:12 * j + 12, 16 * d:16 * d + 16], pc3)
        gc2 = GC[:].map(lambda a: a.rearrange("p c g -> p (c g)"))
        for j in ([] if "nogates" in FLAGS else range(3)):
            for d in range(2):
                pl = ps.sub(512 * (3 + j), [128, 384])
                X("pe", "matmul", pl[:], SEL[d][:], gc2.map(lambda a: a[:, 384 * j:384 * j + 384]), start=True, stop=True)
                pl3 = pl[:].map(lambda a: a.rearrange("p (c g) -> p c g", g=32)[:, :, 16 * d:16 * d + 16])
                X("dve", "tensor_copy", EGL[:, 12 * j:12 * j + 12, 16 * d:16 * d + 16], pl3)
        X("dve", "tensor_scalar", Gg[:], GC[:], -1.0, None, ALU.mult)
        X("dve", "tensor_tensor", DK[:], EGL[0:64], GC[:], ALU.subtract)
        X("act", "activation", DK[:], DK[:], AF.Exp)
        X("act", "activation", EGC[:], GC[:], AF.Exp)
        X("dve", "tensor_scalar", NEGC[:], EGC[:], -1.0, None, ALU.mult)
        X("act", "activation", EGL[:], EGL[:], AF.Exp)
        if "g_stop" in FLAGS:
            continue
        for kh in khs:
            specs = [(kh * 128, kh, qT, 128 ** -0.5), (3072 + kh * 128, 8 + kh, kT, 1.0),
                     (4096 + (2 * kh) * 128, 16 + 2 * kh, vT[0], None),
                     (4096 + (2 * kh + 1) * 128, 17 + 2 * kh, vT[1], None)]
            for (col0, cc, dst, nrm) in specs:
                fw.dma("sync", stg[:].map(lambda a: a.rearrange("p (t n) -> p t n", n=128)),
                       P[base:base + SEQ, col0:col0 + 128].map(lambda a: a.rearrange("(t p) n -> p t n", p=128)))
                for t0 in range(0, 18, 4):
                    nt_ = min(4, 18 - t0)
                    pt = ps.sub(512 * (4 + (t0 // 4) % 2), [128, 128 * nt_])
                    for t in range(nt_):
                        X("pe", "transpose", pt[:, 128 * t:128 * t + 128], stg[:, 128 * (t0 + t):128 * (t0 + t) + 128],
                          K.ident[:])
                    X("act", "copy", raw[:, 128 * t0:128 * (t0 + nt_)], pt[:])
                wc = lambda k: cw[:, cc * 5 + k:cc * 5 + k + 1]
                X("dve", "tensor_scalar", dst[:], raw[:], wc(2), None, ALU.mult)
                for k in (0, 1, 3, 4):
                    sh = k - 2
                    lo_o, hi_o = (0, 256 - sh) if sh > 0 else (-sh, 256)
                    X("dve", "scalar_tensor_tensor", dst[:, lo_o:hi_o], raw[:, lo_o + sh:hi_o + sh], wc(k),
                      dst[:, lo_o:hi_o], ALU.mult, ALU.add)
                    lo, hi = (0, 64 - sh) if sh > 0 else (-sh, 64)
                    v3 = lambda tl, a0, a1: tl[:, 256:SEQ].map(
                        lambda a: a.rearrange("p (r w) -> p r w", w=64)[:, :, a0:a1])
                    X("dve", "scalar_tensor_tensor", v3(dst, lo, hi), v3(raw, lo + sh, hi + sh), wc(k),
                      v3(dst, lo, hi), ALU.mult, ALU.add)
                X("act", "activation", tmp[:], dst[:], AF.Sigmoid)
                X("dve", "tensor_tensor", dst[:], dst[:], tmp[:], ALU.mult)
                if nrm is not None:
                    X("dve", "tensor_tensor", tmp[:], dst[:], dst[:], ALU.mult)
                    for pi, (p0, p1) in enumerate(PIECES):
                        n = p1 - p0
                        pp = ps.sub(512 * (6 + pi % 2), [128, n])
                        rs = rsp[pi % 2]
                        X("pe", "matmul", pp[:], K.ones[:], tmp[:, p0:p1], start=True, stop=True)
                        X("act", "activation", rs[:, 0:n], pp[:], AF.Sqrt, bias=EPS, scale=1.0)
                        X("dve", "reciprocal", rs[:, 0:n], rs[:, 0:n])
                        X("dve", "scalar_tensor_tensor", dst[:, p0:p1], dst[:, p0:p1], float(nrm), rs[:, 0:n],
                          ALU.mult, ALU.mult)
            if "p_stop" in FLAGS:
                continue
            for x in range(4):
                X("pool", "memset", S[x][:], 0.0)
            for h in range(2):
                X("pool", "memset", OACC[h][:], 0.0)

            def ut_unit(d, c, b):
                U = UT[d]
                cs = slice(64 * c, 64 * c + 64)
                gcol = [d * 16 + 2 * kh + h for h in range(2)]
                pA = lambda o, shp: ps.sub(U.bA + o, shp)
                pB = lambda o, shp: ps.sub(U.bB + o, shp)
                KK, QK, R = pA(0, [64, 64]), pA(64, [64, 64]), pA(128, [64, 2, 64])
                KTp, V0p, V1p = pB(0, [64, 128]), pB(128, [64, 128]), pB(256, [64, 128])
                YTp, Xn, Yn, Wn = pB(0, [64, 2, 64]), pA(0, [64, 2, 64]), pB(128, [64, 2, 64]), pB(256, [64, 2, 64])
                outp = c >= 4
                for h in range(2):
                    X("dve", "tensor_scalar", U.DG[:, h, :], id64[:], GC[:, c, gcol[h]:gcol[h] + 1], None, ALU.mult)
                if "nomm" not in FLAGS:
                    X("pe", "matmul", KK[:], kT[:, cs], kT[:, cs], start=True, stop=True)
                    if outp:
                        X("pe", "matmul", QK[:], kT[:, cs], qT[:, cs], start=True, stop=True)
                if "notr" not in FLAGS:
                    X("pe", "matmul", KTp[:], kT[:, cs], K.ident[:], start=True, stop=True)
                    X("pe", "matmul", V0p[:], vT[0][:, cs], K.ident[:], start=True, stop=True)
                    X("pe", "matmul", V1p[:], vT[1][:, cs], K.ident[:], start=True, stop=True)
                if "nor" not in FLAGS:
                    X("pe", "matmul", R[:].map(lambda a: a.rearrange("p a b -> p (a b)")), ones64[:],
                      U.DG[:].map(lambda a: a.rearrange("p a b -> p (a b)")), start=True, stop=True)
                if "notr" not in FLAGS and "nocp" not in FLAGS:
                    ce = ("dve", "tensor_copy") if "dvecp" in FLAGS else ("act", "copy")
                    if "rdkk" in FLAGS:
                        X(ce[0], ce[1], RHS2[d][:, 0:64], KK[:])
                    elif "dst2" in FLAGS:
                        X(ce[0], ce[1], RHS2[d][:], KTp[:])
                    elif "src2" in FLAGS:
                        X(ce[0], ce[1], U.KTOK[b][:], RHS2[d][:])
                    else:
                        X(ce[0], ce[1], U.KTOK[b][:], KTp[:])
                    if "novt" not in FLAGS:
                        X(ce[0], ce[1], U.VTOK[b][:, 0, :], V0p[:])
                        X(ce[0], ce[1], U.VTOK[b][:, 1, :], V1p[:])
                yield
                for h in range(2):
                    X("dve", "tensor_tensor", U.E[:, h, :], R[:, h, :], MASKI[d][:, h, :], ALU.mult)
                for h in range(2):
                    X("dve", "scalar_tensor_tensor", U.E[:, h, :], MASKI[d][:, h, :], Gg[:, c, gcol[h]:gcol[h] + 1],
                      U.E[:, h, :], ALU.mult, ALU.add)
                X("act", "activation", U.E[:], U.E[:], AF.Exp)
                yield
                X("dve", "tensor_tensor", U.Ei[:], U.E[:], MASKI[d][:], ALU.mult)
                X("dve", "tensor_tensor", U.NEs[:], U.E[:], NMASKS[d][:], ALU.mult)
                yield
                Xc, Yc, Wc = U.Xb[0], U.Yb[0], U.Wb[0]
                for h in range(2):
                    X("dve", "scalar_tensor_tensor", Xc[:, h, :], KK[:], BETA[:, c, gcol[h]:gcol[h] + 1],
                      U.NEs[:, h, :], ALU.mult, ALU.mult)
                    if outp:
                        X("dve", "tensor_tensor", U.PT[b][:, h, :], QK[:], U.Ei[:, h, :], ALU.mult)
                X("dve", "tensor_tensor", Wc[:], Xc[:], ID2[:], ALU.add)
                yield
                for h in range(2):
                    X("pe", "matmul", YTp[:, h, :], Xc[:, h, :], id64[:], start=True, stop=True)
                X("act", "copy", Yc[:], YTp[:])
                yield
                for k in range(1, 6):
                    last = k == 5
                    Xn_, Yn_ = U.Xb[k % 2], U.Yb[k % 2]
                    if not last:
                        for h in range(2):
                            X("pe", "matmul", Xn[:, h, :], Yc[:, h, :], Xc[:, h, :], start=True, stop=True)
                    for h in range(2):
                        X("pe", "matmul", Yn[:, h, :], Xc[:, h, :], Yc[:, h, :], start=True, stop=True)
                    X("act", "copy", Yn_[:], Yn[:])
                    if not last:
                        X("dve", "tensor_copy", Xn_[:], Xn[:])
                    yield
                    for h in range(2):
                        X("pe", "matmul", Wn[:, h, :], Yn_[:, h, :], Wc[:, h, :], start=True, stop=True)
                    Wn_ = U.TT[b] if last else U.Wb[k % 2]
                    X("dve", "tensor_tensor", Wn_[:], Wn[:], Wc[:], ALU.add)
                    Xc, Yc, Wc = Xn_, Yn_, Wn_
                    yield

            def rec(h, d, c, b):
                U = UT[d]
                x = 2 * d + h
                gc = d * 16 + 2 * kh + h
                col = lambda T_: T_[:, c, gc:gc + 1]
                cs = slice(64 * c, 64 * c + 64)
                bk = 2048 + 512 * x
                r0, r1, r2, rS = (ps.sub(bk, [64, 128]), ps.sub(bk + 128, [64, 128]), ps.sub(bk + 256, [64, 128]),
                                  ps.sub(bk + 384, [128, 128]))
                outp = c >= 4
                X("pe", "matmul", r0[:], kT[:, cs], S[x][:], start=True, stop=True)
                yield
                X("dve", "scalar_tensor_tensor", RHS2[x][:], r0[:], col(NEGC), U.VTOK[b][:, h, :], ALU.mult, ALU.add)
                yield
                X("pe", "matmul", r1[:], U.TT[b][:, h, :], RHS2[x][:], start=True, stop=True)
                if outp:
                    X("pe", "matmul", r2[:], qT[:, cs], S[x][:], start=True, stop=True)
                yield
                X("dve", "tensor_scalar", VNEW[x][:], r1[:], col(BETA), None, ALU.mult)
                X("dve", "tensor_scalar", VN2[x][:], VNEW[x][:], col(DK), None, ALU.mult)
                if outp:
                    X("dve", "scalar_tensor_tensor", OACC[h][:, c, :], r2[:], col(EGC), OACC[h][:, c, :],
                      ALU.mult, ALU.add)
                yield
                X("pe", "matmul", rS[:], U.KTOK[b][:], VN2[x][:], start=True, stop=True)
                if outp:
                    X("pe", "matmul", r0[:], U.PT[b][:, h, :], VNEW[x][:], start=True, stop=True)
                yield
                X("dve", "scalar_tensor_tensor", S[x][:], S[x][:], EGL[:, c, gc:gc + 1], rS[:], ALU.mult, ALU.add)
                if outp:
                    X("dve", "tensor_tensor", OACC[h][:, c, :], OACC[h][:, c, :], r0[:], ALU.add)
                yield

            order = [[0, 1, 2, 3] + list(range(4, 36)), [3, 2, 1, 0] + list(range(35, 3, -1))]
            interleave([ut_unit(0, order[0][0], 0), ut_unit(1, order[1][0], 0)])
            if "u_stop" in FLAGS:
                continue
            for n in range(1 if "r_stop" in FLAGS else 36):
                gens = []
                if n + 1 < 36:
                    gens += [ut_unit(0, order[0][n + 1], (n + 1) % 2), ut_unit(1, order[1][n + 1], (n + 1) % 2)]
                gens += [rec(h, d, order[d][n], n % 2) for d in range(2) for h in range(2)]
                interleave(gens)
            for h in range(2):
                hv = 2 * kh + h
                o = OACC[h][:, 4:36, :]
                fw.dma("sync", Zt[:], P[base + 256:base + SEQ, 1024 + hv * 128:1024 + hv * 128 + 128].map(
                    lambda a: a.rearrange("(c p) n -> p c n", p=64)))
                tq = K.arena.sub(tmp.foff, [64, 32, 128])
                tq2 = K.arena.sub(tmp.foff + 4096, [64, 32, 128]) if False else None
                X("dve", "tensor_tensor", tq[:], o, o, ALU.mult)
                X("dve", "reduce_sum", ssr[:], tq[:], AX.X)
                X("act", "activation", ssr[:], ssr[:], AF.Sqrt, bias=EPS, scale=1.0 / 128)
                X("dve", "reciprocal", ssr[:], ssr[:])
                X("dve", "tensor_tensor", o, o, ssr[:].map(lambda a: a.unsqueeze(2).to_broadcast([64, 32, 128])), ALU.mult)
                X("dve", "tensor_tensor", o, o, onw[:].map(lambda a: a.unsqueeze(1).to_broadcast([64, 32, 128])), ALU.mult)
                X("act", "activation", tq[:], Zt[:], AF.Sigmoid)
                X("dve", "tensor_tensor", tq[:], tq[:], Zt[:], ALU.mult)
                X("dve", "tensor_tensor", o, o, tq[:], ALU.mult)
                fw.dma("pool", O[base + 256:base + SEQ, hv * 128:hv * 128 + 128].map(
                    lambda a: a.rearrange("(c p) n -> p c n", p=64)), o)


def layer1(K):
    m = K.mods[1]
    gemm(K, K.X2, 1024, K.win, 6208, K.P, K.segs_all, "none", pre=(K.n1w[1], m, 0, 1))
    gdn_core(K, K.P, K.O)
    gemm(K, K.O, 2048, K.wout, 1024, K.X3, K.segs_lat, "res", R=K.X2, mods=m, kgate=2)
    gemm(K, K.X3, 1024, K.w1[1], 4096, K.HID, K.segs_lat, "relu2", pre=(K.n2w[1], m, 3, 4))
    gemm(K, K.HID, 4096, K.w2[1], 1024, K.X4, K.segs_lat, "res", R=K.X3, mods=m, kgate=5)


def build_program():
    nc = bass.Bass("TRN2", target_bir_lowering=False)
    fw = Fw(nc)
    K = setup(fw, dbg=False)
    mods_phase(K)
    s5_setup(K)
    layer0(K)
    layer1(K)
    normmod(K, K.X4, K.out, K.segs_lat, K.fnw, None, None, None, omap=True)
    fw.finish()
    return nc


def kernel(**inputs):
    inp = {k: np.asarray(v) for k, v in inputs.items()}
    nc = build_program()
    shared = host_shared(inp)
    in_maps = []
    for core in range(8):
        d = dict(shared)
        d.update(host_core(inp, core))
        in_maps.append(d)
    res = run_bass_kernel_spmd(nc, in_maps, core_ids=list(range(8)))
    out = np.zeros((16, 2048, 1024), np.float32)
    for core in range(8):
        o = res.results[core]["out"]
        out[2 * core] = o[:2048]
        out[2 * core + 1] = o[2048:]
    return out
```

```python
import math
from concourse.bass_utils import run_bass_kernel_spmd
import contextlib
import numpy as np
import concourse.bass as bass
import concourse.mybir as mybir

F32 = mybir.dt.float32
BF16 = mybir.dt.bfloat16
ALU = mybir.AluOpType
AF = mybir.ActivationFunctionType
AX = mybir.AxisListType

NDSEM = 8


def _ov(a, b):
    return a[0] < b[1] and b[0] < a[1] and a[2] < b[3] and b[2] < a[3]


def _cov(a, b):
    return a[0] <= b[0] and a[1] >= b[1] and a[2] <= b[2] and a[3] >= b[3]


class V:
    __slots__ = ("tile", "ap", "box")

    def __init__(self, tile, ap, box):
        self.tile, self.ap, self.box = tile, ap, box

    def map(self, f):
        return V(self.tile, f(self.ap), self.box)


class Tile:
    def __init__(self, h, shape, name, base=None, foff=0):
        self.h = h
        self.shape = tuple(shape)
        self.name = name
        self.base = base
        self.foff = foff
        st = []
        acc = 1
        for n in reversed(self.shape[1:]):
            st.append(acc)
            acc *= n
        self.fstr = list(reversed(st))
        if base is None:
            self.writes = []
            self.reads = []

    def sub(self, foff, shape, name="sub"):
        n = 1
        for s in shape[1:]:
            n *= s
        assert len(self.shape) == 2 and foff + n <= self.shape[1], (name, foff, n, self.shape)
        ap = self.h[0:shape[0], foff:foff + n]
        if len(shape) == 3:
            ap = ap.rearrange("p (a b) -> p a b", a=shape[1], b=shape[2])
        elif len(shape) == 4:
            ap = ap.rearrange("p (a b c) -> p a b c", a=shape[1], b=shape[2], c=shape[3])
        return Tile(_APH(ap), shape, name, base=self.base or self, foff=self.foff + foff)

    def __getitem__(self, idx):
        if not isinstance(idx, tuple):
            idx = (idx,)
        idx = idx + (slice(None),) * (len(self.shape) - len(idx))
        lo, hi = [], []
        for s, n in zip(idx, self.shape):
            if isinstance(s, int):
                lo.append(s)
                hi.append(s + 1)
            else:
                a = 0 if s.start is None else s.start
                b = n if s.stop is None else s.stop
                stp = 1 if s.step is None else s.step
                assert 0 <= a < b <= n, (self.name, idx, self.shape)
                lo.append(a)
                hi.append(a + ((b - a - 1) // stp) * stp + 1)
        f0 = sum(l * s for l, s in zip(lo[1:], self.fstr))
        f1 = sum((h - 1) * s for h, s in zip(hi[1:], self.fstr)) + 1
        bt = self.base or self
        f0 += self.foff
        f1 += self.foff
        if getattr(bt, "bankgran", 0):
            g = bt.bankgran
            return V(bt, self.h[idx], (0, 128, (f0 // g) * g, -(-f1 // g) * g))
        return V(bt, self.h[idx], (lo[0], hi[0], f0, f1))


class _APH:
    def __init__(self, ap):
        self.ap = ap

    def __getitem__(self, idx):
        return self.ap[idx]


class Eng:
    def __init__(self, name, e, sem, dsems):
        self.name, self.e, self.sem, self.dsems = name, e, sem, dsems
        self.count = 0
        self.ndma = 0
        self.seen = {}
        self.prog = []


class Fw:
    def __init__(self, nc):
        self.nc = nc
        self.stack = contextlib.ExitStack()
        self.engs = {}
        self.semobj = {}
        for name, e, nd in (("pe", nc.tensor, 0), ("dve", nc.vector, 0), ("act", nc.scalar, NDSEM),
                            ("pool", nc.gpsimd, NDSEM), ("sync", nc.sync, NDSEM)):
            sem = self.stack.enter_context(nc.semaphore("s_" + name))
            ds = [self.stack.enter_context(nc.semaphore("d_%s%d" % (name, i))) for i in range(nd)]
            self.engs[name] = Eng(name, e, sem, ds)
        self.ntile = 0

    def sbuf(self, shape, dtype=F32, name=None):
        self.ntile += 1
        name = name or "t%d" % self.ntile
        h = self.stack.enter_context(self.nc.sbuf_tensor(name, list(shape), dtype))
        return Tile(h, shape, name)

    def psum(self, shape, dtype=F32, name=None):
        self.ntile += 1
        name = name or "p%d" % self.ntile
        h = self.stack.enter_context(self.nc.psum_tensor(name, list(shape), dtype))
        t = Tile(h, shape, name)
        t.bankgran = 512
        return t

    def dram(self, name, shape, dtype=F32, kind="Internal"):
        h = self.nc.dram_tensor(name, list(shape), dtype, kind=kind).ap()
        return Tile(h, shape, name)

    def _emit(self, ename, fn, w, r, dma=False):
        E = self.engs[ename]
        need = {}
        w = list(w) + [v for v in r if getattr(v.tile, "bankgran", 0)]

        def add(tok, kind):
            s, v = tok
            if s is E.sem:
                if ename == "pe":
                    return
            if need.get(id(s), (None, 0))[1] < v:
                need[id(s)] = (s, v)

        for v in r:
            for box, tok in v.tile.writes:
                if _ov(box, v.box):
                    add(tok, "raw")
        for v in w:
            for box, tok in v.tile.writes:
                if _ov(box, v.box):
                    add(tok, "waw")
            for box, tok in v.tile.reads:
                if _ov(box, v.box):
                    add(tok, "war")
        if dma:
            slot = E.ndma % NDSEM
            rnd = E.ndma // NDSEM
            E.ndma += 1
            dsem = E.dsems[slot]
            if rnd > 0:
                if need.get(id(dsem), (None, 0))[1] < 16 * rnd:
                    need[id(dsem)] = (dsem, 16 * rnd)
            tok = (dsem, 16 * (rnd + 1))
        else:
            E.count += 1
            tok = (E.sem, E.count)
        for s, v in need.values():
            if E.seen.get(id(s), 0) < v:
                E.seen[id(s)] = v
                E.prog.append(("w", s, v))
        E.prog.append(("i", fn, tok[0], 16 if dma else 1))
        for v in r:
            t = v.tile
            t.reads = [(b, k) for (b, k) in t.reads if not (k[0] is tok[0] and _cov(v.box, b))]
            t.reads.append((v.box, tok))
            if len(t.reads) > 48:
                self._compact(t)
        for v in w:
            t = v.tile
            t.writes = [(b, k) for (b, k) in t.writes if not _cov(v.box, b)]
            t.reads = [(b, k) for (b, k) in t.reads if not _cov(v.box, b)]
            t.writes.append((v.box, tok))
            if len(t.writes) > 48:
                self._compact(t)
        return tok

    def _compact(self, t):
        for attr in ("reads", "writes"):
            m = {}
            for b, k in getattr(t, attr):
                key = id(k[0])
                if key in m:
                    ob, ok = m[key]
                    m[key] = ((min(ob[0], b[0]), max(ob[1], b[1]), min(ob[2], b[2]), max(ob[3], b[3])),
                              (k[0], max(ok[1], k[1])))
                else:
                    m[key] = (b, k)
            setattr(t, attr, list(m.values()))

    def X(self, eng, meth, out, *args, **kw):
        ws, rs, a2, k2 = [out], [], [], {}
        for a in args:
            if isinstance(a, V):
                rs.append(a)
                a2.append(a.ap)
            else:
                a2.append(a)
        for k, a in kw.items():
            if isinstance(a, V):
                (ws if k == "accum_out" else rs).append(a)
                k2[k] = a.ap
            else:
                k2[k] = a
        oap = out.ap
        return self._emit(eng, lambda e: getattr(e, meth)(oap, *a2, **k2), ws, rs)

    def pe(self, fn, w, r):
        return self._emit("pe", fn, w, r)

    def dve(self, fn, w, r):
        return self._emit("dve", fn, w, r)

    def act(self, fn, w, r):
        return self._emit("act", fn, w, r)

    def pool(self, fn, w, r):
        return self._emit("pool", fn, w, r)

    def dma(self, q, out, in_, **kw):
        return self._emit(q, lambda e: e.dma_start(out=out.ap, in_=in_.ap, **kw), [out], [in_], dma=True)

    def finish(self):
        S = self.engs["sync"]
        for E in self.engs.values():
            if E.count > 0 and E is not S:
                S.prog.append(("w", E.sem, E.count))
            for i, ds in enumerate(E.dsems):
                n = (E.ndma - i + NDSEM - 1) // NDSEM if E.ndma > i else 0
                if n > 0:
                    S.prog.append(("w", ds, 16 * n))
        nc = self.nc
        with nc.Block() as block:
            def run(E):
                def body(e):
                    for it in E.prog:
                        if it[0] == "w":
                            e.wait_ge(it[1], it[2])
                        else:
                            it[1](e).then_inc(it[2], it[3])
                return body
            block.tensor(run(self.engs["pe"]))
            block.vector(run(self.engs["dve"]))
            block.scalar(run(self.engs["act"]))
            block.gpsimd(run(self.engs["pool"]))
            block.sync(run(self.engs["sync"]))
        self.stack.close()
        return {k: (E.count, E.ndma) for k, E in self.engs.items()}


I32 = mybir.dt.int32
EPS = 1e-6
SEQ = 2304
NT = 4608
TWO_PI = 2.0 * math.pi


def bc(ap):
    return ap.partition_broadcast(128)


class Ctx:
    pass


def setup(fw, dbg=False, ext_in=()):
    K = Ctx()
    K.fw = fw
    ext = lambda n, s: fw.dram(n, s, kind="ExternalInput")
    K.xin = ext("xin", [NT, 1024])
    K.cvec = ext("cvec", [128, 1024])
    K.ada_w = [ext("ada_w%d" % l, [1024, 6144]) for l in range(2)]
    K.ada_b = [ext("ada_b%d" % l, [1, 6144]) for l in range(2)]
    K.n1w = [ext("n1w%d" % l, [1, 1024]) for l in range(2)]
    K.n2w = [ext("n2w%d" % l, [1, 1024]) for l in range(2)]
    K.fnw = ext("fnw", [1, 1024])
    K.w1 = [ext("w1_%d" % l, [1024, 4096]) for l in range(2)]
    K.w2 = [ext("w2_%d" % l, [4096, 1024]) for l in range(2)]
    K.wglu = ext("wglu", [1024, 2048])
    K.win = ext("win", [1024, 6208])
    K.wout = ext("wout", [2048, 1024])
    K.lre = ext("lre", [128, 128])
    K.lim = ext("lim", [128, 128])
    K.ldt = ext("ldt", [128, 128])
    K.bz = ext("bz", [128, 2048])
    K.biz = ext("biz", [128, 2048])
    K.cz = ext("cz", [128, 2048])
    K.ciz = ext("ciz", [128, 2048])
    K.s5d = ext("s5d", [128, 8])
    K.convw = ext("convw", [128, 160])
    K.alog = ext("alog", [1, 32])
    K.dtb = ext("dtb", [1, 32])
    K.onw = ext("onw", [1, 128])
    K.cst = ext("cst", [128, 2048])
    K.ttab = ext("ttab", [4, SEQ])
    K.out = fw.dram("out", [4096, 1024], kind="ExternalOutput")
    sk = "ExternalOutput" if dbg else "Internal"
    scr = lambda n, s: fw.dram(n, s, kind=("ExternalInput" if n in ext_in else sk))
    K.silu_c = scr("silu_c", [128, 1024])
    K.mods = [scr("mods%d" % l, [128, 6144]) for l in range(2)]
    K.H = scr("H", [NT, 1024])
    K.Y = scr("Y", [NT, 1024])
    K.X1 = scr("X1", [NT, 1024])
    K.X2 = scr("X2", [NT, 1024])
    K.HID = scr("HID", [NT, 4096])
    K.P = scr("P", [NT, 6208])
    K.O = scr("O", [NT, 2048])
    K.X3 = scr("X3", [NT, 1024])
    K.X4 = scr("X4", [NT, 1024])
    K.C = fw.sbuf([128, 2048], name="consts")
    fw.dma("sync", K.C[:], K.cst[:])
    K.ident = K.C.sub(0, [128, 128])
    K.ones = K.C.sub(128, [128, 128])
    K.sgnA = K.C.sub(256, [128, 1])
    K.sgnB = K.C.sub(257, [128, 1])
    K.gm = K.C.sub(264, [128, 8])
    K.arena = fw.sbuf([128, 41 * 1024], name="arena")
    K.ps = fw.psum([128, 4096], name="ps")
    K.segs_all = []
    K.segs_lat = []
    for s in range(2):
        K.segs_all += [(s * SEQ, 2, 2), (s * SEQ + 256, 16, s)]
        K.segs_lat += [(s * SEQ + 256, 16, s)]
    return K


def host_consts():
    c = np.zeros((128, 2048), np.float32)
    c[:, 0:128] = np.eye(128)
    c[:, 128:256] = 1.0
    c[:64, 256] = -1.0
    c[64:, 256] = 1.0
    c[:64, 257] = 1.0
    c[64:, 257] = -1.0
    for g in range(8):
        c[16 * g:16 * g + 16, 264 + g] = 1.0
    return gdn_consts(c)


class Arena:
    def __init__(self, K):
        self.K = K
        self.off = 0

    def get(self, shape, name="a"):
        n = 1
        for s in shape[1:]:
            n *= s
        t = self.K.arena.sub(self.off, shape, name)
        self.off += n
        return t


def mods_phase(K):
    fw = K.fw
    A = Arena(K)
    t = A.get([128, 1024])
    s = A.get([128, 1024])
    fw.dma("sync", t[:], K.cvec[:])
    fw.X("act", "activation", s[:], t[:], AF.Sigmoid)
    fw.X("dve", "tensor_tensor", s[:], s[:], t[:], ALU.mult)
    fw.dma("pool", K.silu_c[:], s[:])
    for l in range(2):
        gemm(K, K.silu_c, 1024, K.ada_w[l], 6144, K.mods[l], [(0, 1, 0)], "bias", bias=K.ada_b[l])


def normmod(K, X, Y, segs, wrow, mods, ksh, ksc, yoff=0, omap=False):
    fw = K.fw
    A = Arena(K)
    Ar, Br, Tr = A.get([128, 1024]), A.get([128, 1024]), A.get([128, 1024])
    xt = [A.get([128, 1024]) for _ in range(2)]
    yt = [A.get([128, 1024]) for _ in range(2)]
    junk = A.get([128, 1024])
    ss = [A.get([128, 1]) for _ in range(2)]
    n = 0
    for (r0, nt, mr) in segs:
        fw.dma("sync", Ar[:], wrow[:].map(bc))
        if ksc is not None:
            fw.dma("sync", Tr[:], mods[mr:mr + 1, ksc * 1024:(ksc + 1) * 1024].map(bc))
            fw.X("dve", "scalar_tensor_tensor", Ar[:], Tr[:], 1.0, Ar[:], ALU.add, ALU.mult)
            fw.dma("sync", Br[:], mods[mr:mr + 1, ksh * 1024:(ksh + 1) * 1024].map(bc))
        for t in range(nt):
            x, y, s = xt[n % 2], yt[n % 2], ss[n % 2]
            n += 1
            rows = slice(r0 + 128 * t, r0 + 128 * t + 128)
            fw.dma("sync", x[:], X[rows, :])
            fw.X("dve", "memset", s[:], 0.0)
            fw.X("act", "activation", junk[:], x[:], AF.Square, accum_out=s[:])
            fw.X("act", "activation", s[:], s[:], AF.Sqrt, bias=EPS, scale=1.0 / 1024)
            fw.X("dve", "reciprocal", s[:], s[:])
            fw.X("dve", "scalar_tensor_tensor", y[:], x[:], s[:], Ar[:], ALU.mult, ALU.mult)
            if ksc is not None:
                fw.X("pool", "tensor_tensor", y[:], y[:], Br[:], ALU.add)
            orows = slice(rows.start - yoff, rows.stop - yoff)
            if omap:
                sq = rows.start // SEQ
                orows = slice(rows.start - 256 * (sq + 1), rows.stop - 256 * (sq + 1))
            fw.dma("pool", Y[orows, :], y[:])


def gemm(K, X, Kd, W, N, Y, segs, epi, bias=None, R=None, mods=None, kgate=None, woff=0, xoff=0, pre=None):
    fw = K.fw
    KC = Kd // 128
    NP = 512 if KC <= 16 else 256
    G = 8 if KC <= 8 else (4 if KC <= 16 else 2)
    A = Arena(K)
    gx = [A.get([128, Kd]) for _ in range(2)]
    XT = A.get([128, KC, G * 128])
    nwb = 2 if epi != "glu" else 4
    Wp = [A.get([128, KC, NP]) for _ in range(nwb)]
    ot = [A.get([128, NP]) for _ in range(2)]
    rt = [A.get([128, NP]) for _ in range(2)]
    sg = [A.get([128, NP]) for _ in range(2)]
    grow = A.get([128, 1024])
    if pre is not None:
        assert Kd == 1024
        pAr, pBr, pTr, pjunk = [A.get([128, 1024]) for _ in range(4)]
        prss = [A.get([128, 1]) for _ in range(2)]
        npre = 0
    Wv = W.h.rearrange("(kc p) n -> p kc n", p=128)
    Wt = Tile(_APHk(Wv), [128, KC, W.shape[1]], W.name + "_v")
    nps = 0
    npw = 0
    nev = 0
    for (r0, nt, mr) in segs:
        if epi in ("res", "glu"):
            fw.dma("sync", grow[:], mods[mr:mr + 1, kgate * 1024:(kgate + 1) * 1024].map(bc))
        if pre is not None:
            pw, pm, pksh, pksc = pre
            fw.dma("sync", pAr[:], pw[:].map(bc))
            fw.dma("sync", pTr[:], pm[mr:mr + 1, pksc * 1024:(pksc + 1) * 1024].map(bc))
            fw.X("dve", "scalar_tensor_tensor", pAr[:], pTr[:], 1.0, pAr[:], ALU.add, ALU.mult)
            fw.dma("sync", pBr[:], pm[mr:mr + 1, pksh * 1024:(pksh + 1) * 1024].map(bc))
        for g0 in range(0, nt, G):
            gn = min(G, nt - g0)
            for t in range(gn):
                x = gx[t % 2]
                rows = slice(r0 + 128 * (g0 + t) - xoff, r0 + 128 * (g0 + t) + 128 - xoff)
                fw.dma("sync", x[:], X[rows, :])
                if pre is not None:
                    sq = prss[npre % 2]
                    npre += 1
                    fw.X("dve", "memset", sq[:], 0.0)
                    fw.X("act", "activation", pjunk[:], x[:], AF.Square, accum_out=sq[:])
                    fw.X("act", "activation", sq[:], sq[:], AF.Sqrt, bias=EPS, scale=1.0 / 1024)
                    fw.X("dve", "reciprocal", sq[:], sq[:])
                    fw.X("dve", "scalar_tensor_tensor", x[:], x[:], sq[:], pAr[:], ALU.mult, ALU.mult)
                    fw.X("dve", "tensor_tensor", x[:], x[:], pBr[:], ALU.add)
                for kc in range(KC):
                    pt = K.ps.sub(512 * (nps % 2), [128, 128])
                    nps += 1
                    fw.X("pe", "transpose", pt[:], x[:, 128 * kc:128 * kc + 128], K.ident[:])
                    fw.X("act" if kc % 2 else "dve", "copy" if kc % 2 else "tensor_copy",
                         XT[:, kc, 128 * t:128 * t + 128], pt[:])
            for n0 in range(0, N, NP):
                w = min(NP, N - n0)
                if epi == "glu":
                    cols = [n0, 1024 + n0]
                else:
                    cols = [woff + n0]
                wps = []
                for c0 in cols:
                    wp = Wp[npw % nwb]
                    npw += 1
                    wps.append(wp)
                    for k0 in range(0, KC, 8):
                        fw.dma("sync", wp[:, k0:k0 + 8, 0:w], Wt[:, k0:k0 + 8, c0:c0 + w])
                for t in range(gn):
                    rows = slice(r0 + 128 * (g0 + t), r0 + 128 * (g0 + t) + 128)
                    pss = []
                    for wi, wp in enumerate(wps):
                        py = K.ps.sub(1024 + 512 * (nev % 4), [128, w])
                        nev += 1
                        pss.append(py)
                        for kc in range(KC):
                            fw.X("pe", "matmul", py[:], XT[:, kc, 128 * t:128 * t + 128], wp[:, kc, 0:w],
                                 start=(kc == 0), stop=(kc == KC - 1))
                    o = ot[nev % 2]
                    py = pss[0]
                    if epi == "none":
                        fw.X("act", "copy", o[:, 0:w], py[:])
                    elif epi == "relu2":
                        fw.X("act", "activation", o[:, 0:w], py[:], AF.Relu)
                        fw.X("pool", "tensor_tensor", o[:, 0:w], o[:, 0:w], o[:, 0:w], ALU.mult)
                    elif epi == "bias":
                        r = rt[nev % 2]
                        fw.dma("sync", r[:, 0:w], bias[0:1, n0:n0 + w].map(bc))
                        fw.X("dve", "tensor_tensor", o[:, 0:w], py[:], r[:, 0:w], ALU.add)
                    elif epi in ("res", "glu"):
                        r = rt[nev % 2]
                        fw.dma("sync", r[:, 0:w], R[rows, n0:n0 + w])
                        sgt = sg[nev % 2]
                        if epi == "glu":
                            fw.X("act", "activation", sgt[:, 0:w], pss[1][:], AF.Sigmoid)
                            fw.X("dve", "tensor_tensor", sgt[:, 0:w], py[:], sgt[:, 0:w], ALU.mult)
                            fw.X("dve", "tensor_tensor", sgt[:, 0:w], sgt[:, 0:w], grow[:, n0:n0 + w], ALU.mult)
                        else:
                            fw.X("dve", "tensor_tensor", sgt[:, 0:w], py[:], grow[:, n0:n0 + w], ALU.mult)
                        fw.X("pool", "tensor_tensor", o[:, 0:w], sgt[:, 0:w], r[:, 0:w], ALU.add)
                    fw.dma("pool", Y[rows, n0:n0 + w], o[:, 0:w])


class _APHk:
    def __init__(self, ap):
        self.ap = ap

    def __getitem__(self, idx):
        return self.ap[idx]


def s5_setup(K):
    fw = K.fw
    Pm = fw.sbuf([128, 3 * 128 + 4 * 2048], name="s5p")
    K.s5R = Pm.sub(0, [128, 128])
    K.s5thn = Pm.sub(128, [128, 128])
    K.s5thn64 = Pm.sub(256, [128, 128])
    K.bbar = Pm.sub(384, [128, 2048])
    K.ibbar = Pm.sub(384 + 2048, [128, 2048])
    K.cstm = Pm.sub(384 + 4096, [128, 2048])
    K.cstim = Pm.sub(384 + 6144, [128, 2048])
    K.s5dt = fw.sbuf([128, 8], name="s5dsb")
    fw.dma("sync", K.s5dt[:], K.s5d[:])
    A = Arena(K)
    g = lambda: A.get([128, 128])
    lre, lim, ldt, th, rho, un, fr, cs, sn, are, aim, xm, t1, t2, cre, cim = [g() for _ in range(16)]
    ki = K.arena.sub(A.off, [128, 128])
    A.off += 128
    kint = Tile(_APHk(ki.h.ap.bitcast(I32)), [128, 128], "kint", base=ki.base, foff=ki.foff)
    fw.dma("sync", lre[:], K.lre[:])
    fw.dma("sync", lim[:], K.lim[:])
    fw.dma("sync", ldt[:], K.ldt[:])
    X = fw.X
    X("act", "activation", ldt[:], ldt[:], AF.Exp)
    X("dve", "tensor_tensor", th[:], lim[:], ldt[:], ALU.mult)
    X("dve", "tensor_tensor", rho[:], lre[:], ldt[:], ALU.mult)
    X("act", "activation", K.s5R[:], rho[:], AF.Exp)

    def frac(dst, src):
        X("dve", "tensor_copy", kint[:], src[:])
        X("dve", "tensor_tensor", dst[:], src[:], kint[:], ALU.subtract)

    X("dve", "tensor_scalar", un[:], th[:], 1.0 / TWO_PI, None, ALU.mult)
    frac(K.s5thn, un)
    X("dve", "tensor_scalar", un[:], K.s5thn[:], 64.0, None, ALU.mult)
    frac(K.s5thn64, un)
    X("dve", "tensor_scalar", un[:], K.s5thn[:], 0.25, None, ALU.add)
    frac(fr, un)
    X("act", "activation", cs[:], fr[:], AF.Sin, scale=TWO_PI)
    X("act", "activation", sn[:], K.s5thn[:], AF.Sin, scale=TWO_PI)
    X("dve", "tensor_tensor", are[:], K.s5R[:], cs[:], ALU.mult)
    X("dve", "tensor_tensor", aim[:], K.s5R[:], sn[:], ALU.mult)
    X("dve", "tensor_scalar", xm[:], are[:], -1.0, None, ALU.add)
    X("dve", "tensor_tensor", t1[:], xm[:], lre[:], ALU.mult)
    X("dve", "tensor_tensor", t2[:], aim[:], lim[:], ALU.mult)
    X("dve", "tensor_tensor", cre[:], t1[:], t2[:], ALU.add)
    X("dve", "tensor_tensor", t1[:], aim[:], lre[:], ALU.mult)
    X("dve", "tensor_tensor", t2[:], xm[:], lim[:], ALU.mult)
    X("dve", "tensor_tensor", cim[:], t1[:], t2[:], ALU.subtract)
    X("dve", "tensor_tensor", t1[:], lre[:], lre[:], ALU.mult)
    X("dve", "tensor_tensor", t2[:], lim[:], lim[:], ALU.mult)
    X("dve", "tensor_tensor", t1[:], t1[:], t2[:], ALU.add)
    X("dve", "reciprocal", t1[:], t1[:])
    X("dve", "tensor_tensor", cre[:], cre[:], t1[:], ALU.mult)
    X("dve", "tensor_tensor", cim[:], cim[:], t1[:], ALU.mult)
    bz, ib, tmp = A.get([128, 128, 16]), A.get([128, 128, 16]), A.get([128, 128, 16])
    fw.dma("sync", bz[:], K.bz[:].map(lambda a: a.rearrange("p (a b) -> p a b", b=16)))
    fw.dma("sync", ib[:], K.biz[:].map(lambda a: a.rearrange("p (a b) -> p a b", b=16)))
    X("dve", "tensor_scalar", ib[:], ib[:], K.sgnA[:], None, ALU.mult)
    b3 = lambda t: t[:].map(lambda a: a.unsqueeze(2).to_broadcast([128, 128, 16]))
    v3 = lambda t: t[:].map(lambda a: a.rearrange("p (a b) -> p a b", b=16))
    X("dve", "tensor_tensor", v3(K.bbar), bz[:], b3(cre), ALU.mult)
    X("dve", "tensor_tensor", tmp[:], ib[:], b3(cim), ALU.mult)
    X("dve", "tensor_tensor", v3(K.bbar), v3(K.bbar), tmp[:], ALU.add)
    X("dve", "tensor_tensor", v3(K.ibbar), ib[:], b3(cre), ALU.mult)
    X("dve", "tensor_tensor", tmp[:], bz[:], b3(cim), ALU.mult)
    X("dve", "tensor_tensor", v3(K.ibbar), v3(K.ibbar), tmp[:], ALU.subtract)
    cz = A.get([128, 2048])
    fw.dma("sync", cz[:], K.cz[:])
    X("dve", "tensor_scalar", K.cstm[:], cz[:], K.sgnB[:], None, ALU.mult)
    cz2 = A.get([128, 2048])
    fw.dma("sync", cz2[:], K.ciz[:])
    X("dve", "tensor_scalar", K.cstim[:], cz2[:], -1.0, None, ALU.mult)


FLAGS = set()
PIECES = [(0, 512), (512, 1024), (1024, 1536), (1536, 2048), (2048, 2304)]


def s5_core(K, H, Y, chunks=range(8), dbg=None):
    fw = K.fw
    X = fw.X
    A = Arena(K)
    big = lambda: A.get([128, SEQ])
    hT = [big(), big()]
    yacc = [big(), big()]
    G0, gs, GS, U, SIN, COS, T1, T0 = [big() for _ in range(8)]
    G0s = [G0, big()]
    GSs = [GS, big()]
    npo = 0
    kraw = big()
    KI = Tile(_APHk(kraw.h.ap.bitcast(I32)), [128, SEQ], "KI", base=kraw.base, foff=kraw.foff)
    LB, LIB, LC, LCI = [A.get([128, 8, 128]) for _ in range(4)]
    tmp = [A.get([128, 512]) for _ in range(2)]
    stg = [A.get([128, 128]) for _ in range(4)]
    X("dve", "memset", LC[:], 0.0)
    X("dve", "memset", LCI[:], 0.0)
    rev = lambda v: v.map(lambda a: a[:, ::-1])
    nst = 0
    npp = 0
    for c in chunks:
        for s in range(2):
            for t in range(18):
                st = stg[nst % 4]
                nst += 1
                fw.dma("sync", st[:], H[s * SEQ + 128 * t: s * SEQ + 128 * t + 128, 128 * c:128 * c + 128])
                pt = K.ps.sub(512 * (nst % 2), [128, 128])
                X("pe", "transpose", pt[:], st[:], K.ident[:])
                X("act", "copy", hT[s][:, 128 * t:128 * t + 128], pt[:])
        first = True
        for d in range(2):
            fw.dma("sync", T1[:], K.ttab[2 * d:2 * d + 1, :].map(bc))
            fw.dma("sync", T0[:], K.ttab[2 * d + 1:2 * d + 2, :].map(bc))
            col0 = (d * 64 + 8 * c) * 16
            for (src, dst) in ((K.bbar, LB), (K.ibbar, LIB)):
                pt = K.ps.sub(1024, [128, 128])
                X("pe", "transpose", pt[:], src[:, col0:col0 + 128], K.ident[:])
                for gl in range(8):
                    X("dve", "tensor_scalar", dst[:, gl, :], pt[:], K.gm[:, gl:gl + 1], None, ALU.mult)
            for gl in range(8):
                X("pool", "tensor_copy", LC[:, gl, 16 * gl:16 * gl + 16], K.cstm[:, col0 + 16 * gl:col0 + 16 * gl + 16])
                X("pool", "tensor_copy", LCI[:, gl, 16 * gl:16 * gl + 16], K.cstim[:, col0 + 16 * gl:col0 + 16 * gl + 16])
            if "stop1" in FLAGS:
                continue
            for gl in range(8):
                dg = d * 64 + 8 * c + gl
                thn = K.s5thn[:, dg:dg + 1]
                thn64 = K.s5thn64[:, dg:dg + 1]
                rbc = lambda n: K.s5R[:, dg:dg + 1].map(lambda a: a.to_broadcast([128, n]))
                if "notab" in FLAGS:
                    X("dve", "memset", SIN[:], 0.0)
                    X("dve", "memset", COS[:], 1.0)
                for _ in ([] if "notab" in FLAGS else [0]):
                  X("dve", "tensor_scalar", U[:], T1[:], thn64, None, ALU.mult)
                  X("dve", "scalar_tensor_tensor", U[:], T0[:], thn, U[:], ALU.mult, ALU.add)
                  X("dve", "tensor_copy", KI[:], U[:])
                  X("dve", "tensor_tensor", SIN[:], U[:], KI[:], ALU.subtract)
                  X("act", "activation", SIN[:], SIN[:], AF.Sin, scale=TWO_PI)
                  X("dve", "tensor_scalar", U[:], U[:], 0.25, None, ALU.add)
                  X("dve", "tensor_copy", KI[:], U[:])
                  X("dve", "tensor_tensor", COS[:], U[:], KI[:], ALU.subtract)
                  X("act", "activation", COS[:], COS[:], AF.Sin, scale=TWO_PI)
                for s in range(2):
                    G0 = G0s[s]
                    for (p0, p1) in PIECES:
                        n = p1 - p0
                        ps1 = K.ps.sub(1536 + 1024 * (npp % 2), [128, n])
                        ps2 = K.ps.sub(2048 + 1024 * (npp % 2), [128, n])
                        tm = tmp[npp % 2]
                        npp += 1
                        X("pe", "matmul", ps1[:], LB[:, gl, :], hT[s][:, p0:p1], start=True, stop=True)
                        X("pe", "matmul", ps2[:], LIB[:, gl, :], hT[s][:, p0:p1], start=True, stop=True)
                        X("dve", "tensor_tensor", tm[:, 0:n], ps2[:], SIN[:, p0:p1], ALU.mult)
                        X("dve", "tensor_tensor", G0[:, p0:p1], ps1[:], COS[:, p0:p1], ALU.mult)
                        X("dve", "tensor_tensor", G0[:, p0:p1], G0[:, p0:p1], tm[:, 0:n], ALU.subtract)
                for s in range(2):
                    G0 = G0s[s]
                    GS = GSs[s]
                    if "noscan" in FLAGS:
                        X("dve", "tensor_copy", gs[:], G0[:])
                    elif d == 0:
                        X("dve", "tensor_tensor_scan", gs[:], rbc(SEQ), G0[:], 0.0, ALU.mult, ALU.add)
                    else:
                        X("dve", "tensor_tensor_scan", rev(gs[:, 0:256]), rbc(256), rev(G0[:, 0:256]), 0.0,
                          ALU.mult, ALU.add)
                        X("dve", "tensor_tensor_scan", rev(gs[:, 256:SEQ]), rbc(SEQ - 256), rev(G0[:, 256:SEQ]),
                          gs[:, 0:1], ALU.mult, ALU.add)
                    X("dve", "tensor_tensor", G0[:], gs[:], COS[:], ALU.mult)
                    X("dve", "tensor_tensor", GS[:], gs[:], SIN[:], ALU.mult)
                    for (p0, p1) in PIECES:
                        n = p1 - p0
                        po = K.ps.sub((3584, 0, 512)[npo % 3], [128, n])
                        npo += 1
                        X("pe", "matmul", po[:], LC[:, gl, :], G0[:, p0:p1], start=True, stop=False)
                        X("pe", "matmul", po[:], LCI[:, gl, :], GS[:, p0:p1], start=False, stop=True)
                        if first:
                            X("act", "copy", yacc[s][:, p0:p1], po[:])
                        else:
                            X("dve", "tensor_tensor", yacc[s][:, p0:p1], yacc[s][:, p0:p1], po[:], ALU.add)
                first = False
        G0 = G0s[0]
        for s in ([] if ("stop1" in FLAGS or "stop2" in FLAGS) else range(2)):
            y = yacc[s]
            X("dve", "scalar_tensor_tensor", y[:], hT[s][:], K.s5dt[:, c:c + 1], y[:], ALU.mult, ALU.add)
            X("dve", "tensor_tensor", G0[:], y[:], y[:], ALU.mult)
            X("dve", "tensor_scalar", G0[:], G0[:], 0.044715, 1.0, ALU.mult, ALU.add)
            X("dve", "tensor_tensor", G0[:], G0[:], y[:], ALU.mult)
            X("act", "activation", G0[:], G0[:], AF.Tanh, scale=math.sqrt(2.0 / math.pi))
            X("dve", "tensor_scalar", G0[:], G0[:], 1.0, 0.5, ALU.add, ALU.mult)
            X("dve", "tensor_tensor", y[:], G0[:], y[:], ALU.mult)
            for t in range(18):
                st = stg[nst % 4]
                nst += 1
                pt = K.ps.sub(512 * (nst % 2), [128, 128])
                X("pe", "transpose", pt[:], y[:, 128 * t:128 * t + 128], K.ident[:])
                X("act", "copy", st[:], pt[:])
                fw.dma("pool", Y[s * SEQ + 128 * t: s * SEQ + 128 * t + 128, 128 * c:128 * c + 128], st[:])


def layer0(K):
    m = K.mods[0]
    normmod(K, K.xin, K.H, K.segs_all, K.n1w[0], m, 0, 1)
    s5_core(K, K.H, K.Y)
    gemm(K, K.Y, 1024, K.wglu, 1024, K.X1, K.segs_all, "glu", R=K.xin, mods=m, kgate=2)
    gemm(K, K.X1, 1024, K.w1[0], 4096, K.HID, K.segs_all, "relu2", pre=(K.n2w[0], m, 3, 4))
    gemm(K, K.HID, 4096, K.w2[0], 1024, K.X2, K.segs_all, "res", R=K.X1, mods=m, kgate=5)


def host_shared(inp):
    f = lambda a: np.ascontiguousarray(np.asarray(a, dtype=np.float32))
    d = {}
    for l in range(2):
        d["ada_w%d" % l] = f(inp["ada_w"][l])
        d["ada_b%d" % l] = f(inp["ada_b"][l][None, :])
        d["n1w%d" % l] = f(inp["norm1_w"][l][None, :])
        d["n2w%d" % l] = f(inp["norm2_w"][l][None, :])
        d["w1_%d" % l] = f(inp["mlp_w1"][l])
        d["w2_%d" % l] = f(inp["mlp_w2"][l])
    d["fnw"] = f(inp["final_norm_w"][None, :])
    d["wglu"] = f(inp["s5_w_glu"][0])
    d["win"] = f(inp["gdn_w_in"][0])
    d["wout"] = f(inp["gdn_w_out"][0])
    lre = np.transpose(inp["s5_lam_re"][0], (2, 0, 1)).reshape(64, 128)
    lim = np.transpose(inp["s5_lam_im"][0], (2, 0, 1)).reshape(64, 128)
    d["lre"] = f(np.concatenate([lre, lre], 0))
    d["lim"] = f(np.concatenate([lim, lim], 0))
    d["ldt"] = f(np.broadcast_to(inp["s5_log_dt"][0].reshape(1, 128), (128, 128)))
    bre = np.transpose(inp["s5_b_re"][0], (2, 0, 1, 3)).reshape(64, 2048)
    bim = np.transpose(inp["s5_b_im"][0], (2, 0, 1, 3)).reshape(64, 2048)
    d["bz"] = f(np.concatenate([bre, bim], 0))
    d["biz"] = f(np.concatenate([bim, bre], 0))
    cre = np.transpose(inp["s5_c_re"][0], (3, 0, 1, 2)).reshape(64, 2048)
    cim = np.transpose(inp["s5_c_im"][0], (3, 0, 1, 2)).reshape(64, 2048)
    d["cz"] = f(np.concatenate([cre, cim], 0))
    d["ciz"] = f(np.concatenate([cim, cre], 0))
    d["s5d"] = f(inp["s5_d"][0].reshape(8, 128).T)
    d["convw"] = f(np.transpose(inp["gdn_conv_w"][0].reshape(5, 32, 128), (2, 1, 0)).reshape(128, 160))
    d["alog"] = f(inp["gdn_a_log"][0].reshape(1, 32))
    d["dtb"] = f(inp["gdn_dt_bias"][0].reshape(1, 32))
    d["onw"] = f(inp["gdn_onorm_w"][0][None, :])
    d["cst"] = host_consts()
    t = np.arange(SEQ)
    trev = np.where(t < 256, 255 - t, 2559 - t)
    d["ttab"] = f(np.stack([t // 64, t % 64, trev // 64, trev % 64]))
    return d


def host_core(inp, core):
    d = {}
    b0 = 2 * core
    x = np.asarray(inp["x"]); ctx = np.asarray(inp["ctx"])
    d["xin"] = np.ascontiguousarray(np.concatenate([ctx[b0], x[b0], ctx[b0 + 1], x[b0 + 1]], 0).astype(np.float32))
    cv = np.zeros((128, 1024), np.float32)
    cv[0] = inp["c"][b0]; cv[1] = inp["c"][b0 + 1]; cv[2] = inp["c_ctx"]
    d["cvec"] = cv
    return d


def gdn_consts(c):
    t = np.arange(64)
    c[:64, 512:576] = (t[:, None] <= t[None, :])
    c[:64, 576:640] = (t[:, None] >= t[None, :])
    c[63, 640:768] = 1.0
    c[0, 768:896] = 1.0
    ji = lambda f: np.concatenate([f(t[:, None], t[None, :])] * 2, 1).astype(np.float32)
    c[:64, 896:1024] = ji(lambda j, i: i >= j)
    c[:64, 1024:1152] = -ji(lambda j, i: i > j)
    c[:64, 1152:1280] = ji(lambda j, i: i <= j)
    c[:64, 1280:1408] = -ji(lambda j, i: i < j)
    c[:64, 1408:1536] = ji(lambda j, i: i == j)
    return c


def interleave(gens):
    gens = list(gens)
    lim = [int(f[2:]) for f in FLAGS if f.startswith("ut")]
    nround = 0
    while gens:
        if lim and nround >= lim[0]:
            return
        nround += 1
        nxt = []
        for g in gens:
            try:
                next(g)
                nxt.append(g)
            except StopIteration:
                pass
        gens = nxt


def gdn_core(K, P, O, seqs=(0, 1), khs=range(8)):
    fw = K.fw
    X = fw.X
    C = K.C
    id64 = C.sub(0, [64, 64])
    ones64 = C.sub(128, [64, 64])
    TRI = [C.sub(512, [64, 64]), C.sub(576, [64, 64])]
    SEL = [C.sub(640, [64, 128]), C.sub(768, [64, 128])]
    MASKI = [C.sub(896, [64, 2, 64]), C.sub(1152, [64, 2, 64])]
    NMASKS = [C.sub(1024, [64, 2, 64]), C.sub(1280, [64, 2, 64])]
    ID2 = C.sub(1408, [64, 2, 64])
    A = Arena(K)
    NG = 36 * 32
    BETA, Gg, GC, EGC, DK, NEGC = [A.get([64, 36, 32]) for _ in range(6)]
    EGL = A.get([128, 36, 32])
    stg, raw, tmp = A.get([128, SEQ]), A.get([128, SEQ]), A.get([128, SEQ])
    Graw = K.arena.sub(stg.foff, [64, 36, 64])
    Zt = K.arena.sub(stg.foff, [64, 32, 128])
    qT, kT = A.get([128, SEQ]), A.get([128, SEQ])
    vT = [A.get([128, SEQ]) for _ in range(2)]
    OACC = [A.get([64, 36, 128]) for _ in range(2)]
    t2 = lambda: A.get([64, 2, 64])
    UT = []
    for u in range(2):
        d = Ctx()
        d.DG, d.E, d.Ei, d.NEs = t2(), t2(), t2(), t2()
        d.Xb = [t2(), t2()]
        d.Yb = [t2(), t2()]
        d.Wb = [t2(), t2()]
        d.TT = [t2(), t2()]
        d.PT = [t2(), t2()]
        d.KTOK = [A.get([64, 128]) for _ in range(2)]
        d.VTOK = [A.get([64, 2, 128]) for _ in range(2)]
        d.bA = 512 * (2 * u)
        d.bB = 512 * (2 * u + 1)
        UT.append(d)
    RHS2 = [A.get([64, 128]) for _ in range(4)]
    VNEW = [A.get([64, 128]) for _ in range(4)]
    VN2 = [A.get([64, 128]) for _ in range(4)]
    S = [A.get([128, 128]) for _ in range(4)]
    rsp = [A.get([128, 512]) for _ in range(2)]
    onw = A.get([64, 128])
    rows32 = A.get([64, 64])
    ssr = A.get([64, 32])
    cw = A.get([128, 160])
    fw.dma("sync", cw[:], K.convw[:])
    fw.dma("sync", onw[:], K.onw[:].map(lambda a: a.partition_broadcast(64)))
    fw.dma("sync", rows32[:, 0:32], K.alog[:].map(lambda a: a.partition_broadcast(64)))
    fw.dma("sync", rows32[:, 32:64], K.dtb[:].map(lambda a: a.partition_broadcast(64)))
    X("act", "activation", rows32[:, 0:32], rows32[:, 0:32], AF.Exp)
    X("dve", "tensor_scalar", rows32[:, 0:32], rows32[:, 0:32], -1.0, None, ALU.mult)
    b36 = lambda v: v.map(lambda a: a.unsqueeze(1).to_broadcast([64, 36, 32]))
    ps = K.ps
    for s in seqs:
        base = s * SEQ
        fw.dma("sync", Graw[:], P[base:base + SEQ, 6144:6208].map(lambda a: a.rearrange("(c p) g -> p c g", p=64)))
        X("act", "activation", BETA[:], Graw[:, :, 0:32], AF.Sigmoid)
        X("dve", "tensor_tensor", Gg[:], Graw[:, :, 32:64], b36(rows32[:, 32:64]), ALU.add)
        X("act", "activation", Gg[:], Gg[:], AF.Exp)
        X("act", "activation", Gg[:], Gg[:], AF.Ln, bias=1.0)
        X("dve", "tensor_tensor", Gg[:], Gg[:], b36(rows32[:, 0:32]), ALU.mult)
        g2 = Gg[:].map(lambda a: a.rearrange("p c g -> p (c g)"))
        if "nogates" in FLAGS:
            X("dve", "memset", GC[:], -0.1)
            X("dve", "memset", EGL[:], -0.1)
        for d in ([] if "nogates" in FLAGS else range(2)):
            for j in range(3):
                pc = ps.sub(512 * j, [64, 384])
                X("pe", "matmul", pc[:], TRI[d][:], g2.map(lambda a: a[:, 384 * j:384 * j + 384]), start=True, stop=True)
                pc3 = pc[:].map(lambda a: a.rearrange("p (c g) -> p c g", g=32)[:, :, 16 * d:16 * d + 16])
                X("act", "copy", GC[:, 12 * j:12 * j + 12, 16 * d:16 * d + 16], pc3)
        gc2 = GC[:].map(lambda a: a.rearrange("p c g -> p (c g)"))
        for j in ([] if "nogates" in FLAGS else range(3)):
            for d in range(2):
                pl = ps.sub(512 * (3 + j), [128, 384])
                X("pe", "matmul", pl[:], SEL[d][:], gc2.map(lambda a: a[:, 384 * j:384 * j + 384]), start=True, stop=True)
                pl3 = pl[:].map(lambda a: a.rearrange("p (c g) -> p c g", g=32)[:, :, 16 * d:16 * d + 16])
                X("dve", "tensor_copy", EGL[:, 12 * j:12 * j + 12, 16 * d:16 * d + 16], pl3)
        X("dve", "tensor_scalar", Gg[:], GC[:], -1.0, None, ALU.mult)
        X("dve", "tensor_tensor", DK[:], EGL[0:64], GC[:], ALU.subtract)
        X("act", "activation", DK[:], DK[:], AF.Exp)
        X("act", "activation", EGC[:], GC[:], AF.Exp)
        X("dve", "tensor_scalar", NEGC[:], EGC[:], -1.0, None, ALU.mult)
        X("act", "activation", EGL[:], EGL[:], AF.Exp)
        if "g_stop" in FLAGS:
            continue
        for kh in khs:
            specs = [(kh * 128, kh, qT, 128 ** -0.5), (3072 + kh * 128, 8 + kh, kT, 1.0),
                     (4096 + (2 * kh) * 128, 16 + 2 * kh, vT[0], None),
                     (4096 + (2 * kh + 1) * 128, 17 + 2 * kh, vT[1], None)]
            for (col0, cc, dst, nrm) in specs:
                fw.dma("sync", stg[:].map(lambda a: a.rearrange("p (t n) -> p t n", n=128)),
                       P[base:base + SEQ, col0:col0 + 128].map(lambda a: a.rearrange("(t p) n -> p t n", p=128)))
                for t0 in range(0, 18, 4):
                    nt_ = min(4, 18 - t0)
                    pt = ps.sub(512 * (4 + (t0 // 4) % 2), [128, 128 * nt_])
                    for t in range(nt_):
                        X("pe", "transpose", pt[:, 128 * t:128 * t + 128], stg[:, 128 * (t0 + t):128 * (t0 + t) + 128],
                          K.ident[:])
                    X("act", "copy", raw[:, 128 * t0:128 * (t0 + nt_)], pt[:])
                wc = lambda k: cw[:, cc * 5 + k:cc * 5 + k + 1]
                X("dve", "tensor_scalar", dst[:], raw[:], wc(2), None, ALU.mult)
                for k in (0, 1, 3, 4):
                    sh = k - 2
                    lo_o, hi_o = (0, 256 - sh) if sh > 0 else (-sh, 256)
                    X("dve", "scalar_tensor_tensor", dst[:, lo_o:hi_o], raw[:, lo_o + sh:hi_o + sh], wc(k),
                      dst[:, lo_o:hi_o], ALU.mult, ALU.add)
                    lo, hi = (0, 64 - sh) if sh > 0 else (-sh, 64)
                    v3 = lambda tl, a0, a1: tl[:, 256:SEQ].map(
                        lambda a: a.rearrange("p (r w) -> p r w", w=64)[:, :, a0:a1])
                    X("dve", "scalar_tensor_tensor", v3(dst, lo, hi), v3(raw, lo + sh, hi + sh), wc(k),
                      v3(dst, lo, hi), ALU.mult, ALU.add)
                X("act", "activation", tmp[:], dst[:], AF.Sigmoid)
                X("dve", "tensor_tensor", dst[:], dst[:], tmp[:], ALU.mult)
                if nrm is not None:
                    X("dve", "tensor_tensor", tmp[:], dst[:], dst[:], ALU.mult)
                    for pi, (p0, p1) in enumerate(PIECES):
                        n = p1 - p0
                        pp = ps.sub(512 * (6 + pi % 2), [128, n])
                        rs = rsp[pi % 2]
                        X("pe", "matmul", pp[:], K.ones[:], tmp[:, p0:p1], start=True, stop=True)
                        X("act", "activation", rs[:, 0:n], pp[:], AF.Sqrt, bias=EPS, scale=1.0)
                        X("dve", "reciprocal", rs[:, 0:n], rs[:, 0:n])
                        X("dve", "scalar_tensor_tensor", dst[:, p0:p1], dst[:, p0:p1], float(nrm), rs[:, 0:n],
                          ALU.mult, ALU.mult)
            if "p_stop" in FLAGS:
                continue
            for x in range(4):
                X("pool", "memset", S[x][:], 0.0)
            for h in range(2):
                X("pool", "memset", OACC[h][:], 0.0)

            def ut_unit(d, c, b):
                U = UT[d]
                cs = slice(64 * c, 64 * c + 64)
                gcol = [d * 16 + 2 * kh + h for h in range(2)]
                pA = lambda o, shp: ps.sub(U.bA + o, shp)
                pB = lambda o, shp: ps.sub(U.bB + o, shp)
                KK, QK, R = pA(0, [64, 64]), pA(64, [64, 64]), pA(128, [64, 2, 64])
                KTp, V0p, V1p = pB(0, [64, 128]), pB(128, [64, 128]), pB(256, [64, 128])
                YTp, Xn, Yn, Wn = pB(0, [64, 2, 64]), pA(0, [64, 2, 64]), pB(128, [64, 2, 64]), pB(256, [64, 2, 64])
                outp = c >= 4
                for h in range(2):
                    X("dve", "tensor_scalar", U.DG[:, h, :], id64[:], GC[:, c, gcol[h]:gcol[h] + 1], None, ALU.mult)
                if "nomm" not in FLAGS:
                    X("pe", "matmul", KK[:], kT[:, cs], kT[:, cs], start=True, stop=True)
                    if outp:
                        X("pe", "matmul", QK[:], kT[:, cs], qT[:, cs], start=True, stop=True)
                if "notr" not in FLAGS:
                    X("pe", "matmul", KTp[:], kT[:, cs], K.ident[:], start=True, stop=True)
                    X("pe", "matmul", V0p[:], vT[0][:, cs], K.ident[:], start=True, stop=True)
                    X("pe", "matmul", V1p[:], vT[1][:, cs], K.ident[:], start=True, stop=True)
                if "nor" not in FLAGS:
                    X("pe", "matmul", R[:].map(lambda a: a.rearrange("p a b -> p (a b)")), ones64[:],
                      U.DG[:].map(lambda a: a.rearrange("p a b -> p (a b)")), start=True, stop=True)
                if "notr" not in FLAGS and "nocp" not in FLAGS:
                    ce = ("dve", "tensor_copy") if "dvecp" in FLAGS else ("act", "copy")
                    if "rdkk" in FLAGS:
                        X(ce[0], ce[1], RHS2[d][:, 0:64], KK[:])
                    elif "dst2" in FLAGS:
                        X(ce[0], ce[1], RHS2[d][:], KTp[:])
                    elif "src2" in FLAGS:
                        X(ce[0], ce[1], U.KTOK[b][:], RHS2[d][:])
                    else:
                        X(ce[0], ce[1], U.KTOK[b][:], KTp[:])
                    if "novt" not in FLAGS:
                        X(ce[0], ce[1], U.VTOK[b][:, 0, :], V0p[:])
                        X(ce[0], ce[1], U.VTOK[b][:, 1, :], V1p[:])
                yield
                for h in range(2):
                    X("dve", "tensor_tensor", U.E[:, h, :], R[:, h, :], MASKI[d][:, h, :], ALU.mult)
                for h in range(2):
                    X("dve", "scalar_tensor_tensor", U.E[:, h, :], MASKI[d][:, h, :], Gg[:, c, gcol[h]:gcol[h] + 1],
                      U.E[:, h, :], ALU.mult, ALU.add)
                X("act", "activation", U.E[:], U.E[:], AF.Exp)
                yield
                X("dve", "tensor_tensor", U.Ei[:], U.E[:], MASKI[d][:], ALU.mult)
                X("dve", "tensor_tensor", U.NEs[:], U.E[:], NMASKS[d][:], ALU.mult)
                yield
                Xc, Yc, Wc = U.Xb[0], U.Yb[0], U.Wb[0]
                for h in range(2):
                    X("dve", "scalar_tensor_tensor", Xc[:, h, :], KK[:], BETA[:, c, gcol[h]:gcol[h] + 1],
                      U.NEs[:, h, :], ALU.mult, ALU.mult)
                    if outp:
                        X("dve", "tensor_tensor", U.PT[b][:, h, :], QK[:], U.Ei[:, h, :], ALU.mult)
                X("dve", "tensor_tensor", Wc[:], Xc[:], ID2[:], ALU.add)
                yield
                for h in range(2):
                    X("pe", "matmul", YTp[:, h, :], Xc[:, h, :], id64[:], start=True, stop=True)
                X("act", "copy", Yc[:], YTp[:])
                yield
                for k in range(1, 6):
                    last = k == 5
                    Xn_, Yn_ = U.Xb[k % 2], U.Yb[k % 2]
                    if not last:
                        for h in range(2):
                            X("pe", "matmul", Xn[:, h, :], Yc[:, h, :], Xc[:, h, :], start=True, stop=True)
                    for h in range(2):
                        X("pe", "matmul", Yn[:, h, :], Xc[:, h, :], Yc[:, h, :], start=True, stop=True)
                    X("act", "copy", Yn_[:], Yn[:])
                    if not last:
                        X("dve", "tensor_copy", Xn_[:], Xn[:])
                    yield
                    for h in range(2):
                        X("pe", "matmul", Wn[:, h, :], Yn_[:, h, :], Wc[:, h, :], start=True, stop=True)
                    Wn_ = U.TT[b] if last else U.Wb[k % 2]
                    X("dve", "tensor_tensor", Wn_[:], Wn[:], Wc[:], ALU.add)
                    Xc, Yc, Wc = Xn_, Yn_, Wn_
                    yield

            def rec(h, d, c, b):
                U = UT[d]
                x = 2 * d + h
                gc = d * 16 + 2 * kh + h
                col = lambda T_: T_[:, c, gc:gc + 1]
                cs = slice(64 * c, 64 * c + 64)
                bk = 2048 + 512 * x
                r0, r1, r2, rS = (ps.sub(bk, [64, 128]), ps.sub(bk + 128, [64, 128]), ps.sub(bk + 256, [64, 128]),
                                  ps.sub(bk + 384, [128, 128]))
                outp = c >= 4
                X("pe", "matmul", r0[:], kT[:, cs], S[x][:], start=True, stop=True)
                yield
                X("dve", "scalar_tensor_tensor", RHS2[x][:], r0[:], col(NEGC), U.VTOK[b][:, h, :], ALU.mult, ALU.add)
                yield
                X("pe", "matmul", r1[:], U.TT[b][:, h, :], RHS2[x][:], start=True, stop=True)
                if outp:
                    X("pe", "matmul", r2[:], qT[:, cs], S[x][:], start=True, stop=True)
                yield
                X("dve", "tensor_scalar", VNEW[x][:], r1[:], col(BETA), None, ALU.mult)
                X("dve", "tensor_scalar", VN2[x][:], VNEW[x][:], col(DK), None, ALU.mult)
                if outp:
                    X("dve", "scalar_tensor_tensor", OACC[h][:, c, :], r2[:], col(EGC), OACC[h][:, c, :],
                      ALU.mult, ALU.add)
                yield
                X("pe", "matmul", rS[:], U.KTOK[b][:], VN2[x][:], start=True, stop=True)
                if outp:
                    X("pe", "matmul", r0[:], U.PT[b][:, h, :], VNEW[x][:], start=True, stop=True)
                yield
                X("dve", "scalar_tensor_tensor", S[x][:], S[x][:], EGL[:, c, gc:gc + 1], rS[:], ALU.mult, ALU.add)
                if outp:
                    X("dve", "tensor_tensor", OACC[h][:, c, :], OACC[h][:, c, :], r0[:], ALU.add)
                yield

            order = [[0, 1, 2, 3] + list(range(4, 36)), [3, 2, 1, 0] + list(range(35, 3, -1))]
            interleave([ut_unit(0, order[0][0], 0), ut_unit(1, order[1][0], 0)])
            if "u_stop" in FLAGS:
                continue
            for n in range(1 if "r_stop" in FLAGS else 36):
                gens = []
                if n + 1 < 36:
                    gens += [ut_unit(0, order[0][n + 1], (n + 1) % 2), ut_unit(1, order[1][n + 1], (n + 1) % 2)]
                gens += [rec(h, d, order[d][n], n % 2) for d in range(2) for h in range(2)]
                interleave(gens)
            for h in range(2):
                hv = 2 * kh + h
                o = OACC[h][:, 4:36, :]
                fw.dma("sync", Zt[:], P[base + 256:base + SEQ, 1024 + hv * 128:1024 + hv * 128 + 128].map(
                    lambda a: a.rearrange("(c p) n -> p c n", p=64)))
                tq = K.arena.sub(tmp.foff, [64, 32, 128])
                tq2 = K.arena.sub(tmp.foff + 4096, [64, 32, 128]) if False else None
                X("dve", "tensor_tensor", tq[:], o, o, ALU.mult)
                X("dve", "reduce_sum", ssr[:], tq[:], AX.X)
                X("act", "activation", ssr[:], ssr[:], AF.Sqrt, bias=EPS, scale=1.0 / 128)
                X("dve", "reciprocal", ssr[:], ssr[:])
                X("dve", "tensor_tensor", o, o, ssr[:].map(lambda a: a.unsqueeze(2).to_broadcast([64, 32, 128])), ALU.mult)
                X("dve", "tensor_tensor", o, o, onw[:].map(lambda a: a.unsqueeze(1).to_broadcast([64, 32, 128])), ALU.mult)
                X("act", "activation", tq[:], Zt[:], AF.Sigmoid)
                X("dve", "tensor_tensor", tq[:], tq[:], Zt[:], ALU.mult)
                X("dve", "tensor_tensor", o, o, tq[:], ALU.mult)
                fw.dma("pool", O[base + 256:base + SEQ, hv * 128:hv * 128 + 128].map(
                    lambda a: a.rearrange("(c p) n -> p c n", p=64)), o)


def layer1(K):
    m = K.mods[1]
    gemm(K, K.X2, 1024, K.win, 6208, K.P, K.segs_all, "none", pre=(K.n1w[1], m, 0, 1))
    gdn_core(K, K.P, K.O)
    gemm(K, K.O, 2048, K.wout, 1024, K.X3, K.segs_lat, "res", R=K.X2, mods=m, kgate=2)
    gemm(K, K.X3, 1024, K.w1[1], 4096, K.HID, K.segs_lat, "relu2", pre=(K.n2w[1], m, 3, 4))
    gemm(K, K.HID, 4096, K.w2[1], 1024, K.X4, K.segs_lat, "res", R=K.X3, mods=m, kgate=5)


def build_program():
    nc = bass.Bass("TRN2", target_bir_lowering=False)
    fw = Fw(nc)
    K = setup(fw, dbg=False)
    mods_phase(K)
    s5_setup(K)
    layer0(K)
    layer1(K)
    normmod(K, K.X4, K.out, K.segs_lat, K.fnw, None, None, None, omap=True)
    fw.finish()
    return nc


def kernel(**inputs):
    inp = {k: np.asarray(v) for k, v in inputs.items()}
    nc = build_program()
    shared = host_shared(inp)
    in_maps = []
    for core in range(8):
        d = dict(shared)
        d.update(host_core(inp, core))
        in_maps.append(d)
    res = run_bass_kernel_spmd(nc, in_maps, core_ids=list(range(8)))
    out = np.zeros((16, 2048, 1024), np.float32)
    for core in range(8):
        o = res.results[core]["out"]
        out[2 * core] = o[:2048]
        out[2 * core + 1] = o[2048:]
    return out
```

```python
import math
from concourse.bass_utils import run_bass_kernel_spmd
import contextlib
import numpy as np
import concourse.bass as bass
import concourse.mybir as mybir

F32 = mybir.dt.float32
BF16 = mybir.dt.bfloat16
ALU = mybir.AluOpType
AF = mybir.ActivationFunctionType
AX = mybir.AxisListType

NDSEM = 8


def _ov(a, b):
    return a[0] < b[1] and b[0] < a[1] and a[2] < b[3] and b[2] < a[3]


def _cov(a, b):
    return a[0] <= b[0] and a[1] >= b[1] and a[2] <= b[2] and a[3] >= b[3]


class V:
    __slots__ = ("tile", "ap", "box")

    def __init__(self, tile, ap, box):
        self.tile, self.ap, self.box = tile, ap, box

    def map(self, f):
        return V(self.tile, f(self.ap), self.box)


class Tile:
    def __init__(self, h, shape, name, base=None, foff=0):
        self.h = h
        self.shape = tuple(shape)
        self.name = name
        self.base = base
        self.foff = foff
        st = []
        acc = 1
        for n in reversed(self.shape[1:]):
            st.append(acc)
            acc *= n
        self.fstr = list(reversed(st))
        if base is None:
            self.writes = []
            self.reads = []

    def sub(self, foff, shape, name="sub"):
        n = 1
        for s in shape[1:]:
            n *= s
        assert len(self.shape) == 2 and foff + n <= self.shape[1], (name, foff, n, self.shape)
        ap = self.h[0:shape[0], foff:foff + n]
        if len(shape) == 3:
            ap = ap.rearrange("p (a b) -> p a b", a=shape[1], b=shape[2])
        elif len(shape) == 4:
            ap = ap.rearrange("p (a b c) -> p a b c", a=shape[1], b=shape[2], c=shape[3])
        return Tile(_APH(ap), shape, name, base=self.base or self, foff=self.foff + foff)

    def __getitem__(self, idx):
        if not isinstance(idx, tuple):
            idx = (idx,)
        idx = idx + (slice(None),) * (len(self.shape) - len(idx))
        lo, hi = [], []
        for s, n in zip(idx, self.shape):
            if isinstance(s, int):
                lo.append(s)
                hi.append(s + 1)
            else:
                a = 0 if s.start is None else s.start
                b = n if s.stop is None else s.stop
                stp = 1 if s.step is None else s.step
                assert 0 <= a < b <= n, (self.name, idx, self.shape)
                lo.append(a)
                hi.append(a + ((b - a - 1) // stp) * stp + 1)
        f0 = sum(l * s for l, s in zip(lo[1:], self.fstr))
        f1 = sum((h - 1) * s for h, s in zip(hi[1:], self.fstr)) + 1
        bt = self.base or self
        f0 += self.foff
        f1 += self.foff
        if getattr(bt, "bankgran", 0):
            g = bt.bankgran
            return V(bt, self.h[idx], (0, 128, (f0 // g) * g, -(-f1 // g) * g))
        return V(bt, self.h[idx], (lo[0], hi[0], f0, f1))


class _APH:
    def __init__(self, ap):
        self.ap = ap

    def __getitem__(self, idx):
        return self.ap[idx]


class Eng:
    def __init__(self, name, e, sem, dsems):
        self.name, self.e, self.sem, self.dsems = name, e, sem, dsems
        self.count = 0
        self.ndma = 0
        self.seen = {}
        self.prog = []


class Fw:
    def __init__(self, nc):
        self.nc = nc
        self.stack = contextlib.ExitStack()
        self.engs = {}
        self.semobj = {}
        for name, e, nd in (("pe", nc.tensor, 0), ("dve", nc.vector, 0), ("act", nc.scalar, NDSEM),
                            ("pool", nc.gpsimd, NDSEM), ("sync", nc.sync, NDSEM)):
            sem = self.stack.enter_context(nc.semaphore("s_" + name))
            ds = [self.stack.enter_context(nc.semaphore("d_%s%d" % (name, i))) for i in range(nd)]
            self.engs[name] = Eng(name, e, sem, ds)
        self.ntile = 0

    def sbuf(self, shape, dtype=F32, name=None):
        self.ntile += 1
        name = name or "t%d" % self.ntile
        h = self.stack.enter_context(self.nc.sbuf_tensor(name, list(shape), dtype))
        return Tile(h, shape, name)

    def psum(self, shape, dtype=F32, name=None):
        self.ntile += 1
        name = name or "p%d" % self.ntile
        h = self.stack.enter_context(self.nc.psum_tensor(name, list(shape), dtype))
        t = Tile(h, shape, name)
        t.bankgran = 512
        return t

    def dram(self, name, shape, dtype=F32, kind="Internal"):
        h = self.nc.dram_tensor(name, list(shape), dtype, kind=kind).ap()
        return Tile(h, shape, name)

    def _emit(self, ename, fn, w, r, dma=False):
        E = self.engs[ename]
        need = {}
        w = list(w) + [v for v in r if getattr(v.tile, "bankgran", 0)]

        def add(tok, kind):
            s, v = tok
            if s is E.sem:
                if ename == "pe":
                    return
            if need.get(id(s), (None, 0))[1] < v:
                need[id(s)] = (s, v)

        for v in r:
            for box, tok in v.tile.writes:
                if _ov(box, v.box):
                    add(tok, "raw")
        for v in w:
            for box, tok in v.tile.writes:
                if _ov(box, v.box):
                    add(tok, "waw")
            for box, tok in v.tile.reads:
                if _ov(box, v.box):
                    add(tok, "war")
        if dma:
            slot = E.ndma % NDSEM
            rnd = E.ndma // NDSEM
            E.ndma += 1
            dsem = E.dsems[slot]
            if rnd > 0:
                if need.get(id(dsem), (None, 0))[1] < 16 * rnd:
                    need[id(dsem)] = (dsem, 16 * rnd)
            tok = (dsem, 16 * (rnd + 1))
        else:
            E.count += 1
            tok = (E.sem, E.count)
        for s, v in need.values():
            if E.seen.get(id(s), 0) < v:
                E.seen[id(s)] = v
                E.prog.append(("w", s, v))
        E.prog.append(("i", fn, tok[0], 16 if dma else 1))
        for v in r:
            t = v.tile
            t.reads = [(b, k) for (b, k) in t.reads if not (k[0] is tok[0] and _cov(v.box, b))]
            t.reads.append((v.box, tok))
            if len(t.reads) > 48:
                self._compact(t)
        for v in w:
            t = v.tile
            t.writes = [(b, k) for (b, k) in t.writes if not _cov(v.box, b)]
            t.reads = [(b, k) for (b, k) in t.reads if not _cov(v.box, b)]
            t.writes.append((v.box, tok))
            if len(t.writes) > 48:
                self._compact(t)
        return tok

    def _compact(self, t):
        for attr in ("reads", "writes"):
            m = {}
            for b, k in getattr(t, attr):
                key = id(k[0])
                if key in m:
                    ob, ok = m[key]
                    m[key] = ((min(ob[0], b[0]), max(ob[1], b[1]), min(ob[2], b[2]), max(ob[3], b[3])),
                              (k[0], max(ok[1], k[1])))
                else:
                    m[key] = (b, k)
            setattr(t, attr, list(m.values()))

    def X(self, eng, meth, out, *args, **kw):
        ws, rs, a2, k2 = [out], [], [], {}
        for a in args:
            if isinstance(a, V):
                rs.append(a)
                a2.append(a.ap)
            else:
                a2.append(a)
        for k, a in kw.items():
            if isinstance(a, V):
                (ws if k == "accum_out" else rs).append(a)
                k2[k] = a.ap
            else:
                k2[k] = a
        oap = out.ap
        return self._emit(eng, lambda e: getattr(e, meth)(oap, *a2, **k2), ws, rs)

    def pe(self, fn, w, r):
        return self._emit("pe", fn, w, r)

    def dve(self, fn, w, r):
        return self._emit("dve", fn, w, r)

    def act(self, fn, w, r):
        return self._emit("act", fn, w, r)

    def pool(self, fn, w, r):
        return self._emit("pool", fn, w, r)

    def dma(self, q, out, in_, **kw):
        return self._emit(q, lambda e: e.dma_start(out=out.ap, in_=in_.ap, **kw), [out], [in_], dma=True)

    def finish(self):
        S = self.engs["sync"]
        for E in self.engs.values():
            if E.count > 0 and E is not S:
                S.prog.append(("w", E.sem, E.count))
            for i, ds in enumerate(E.dsems):
                n = (E.ndma - i + NDSEM - 1) // NDSEM if E.ndma > i else 0
                if n > 0:
                    S.prog.append(("w", ds, 16 * n))
        nc = self.nc
        with nc.Block() as block:
            def run(E):
                def body(e):
                    for it in E.prog:
                        if it[0] == "w":
                            e.wait_ge(it[1], it[2])
                        else:
                            it[1](e).then_inc(it[2], it[3])
                return body
            block.tensor(run(self.engs["pe"]))
            block.vector(run(self.engs["dve"]))
            block.scalar(run(self.engs["act"]))
            block.gpsimd(run(self.engs["pool"]))
            block.sync(run(self.engs["sync"]))
        self.stack.close()
        return {k: (E.count, E.ndma) for k, E in self.engs.items()}


I32 = mybir.dt.int32
EPS = 1e-6
SEQ = 2304
NT = 4608
TWO_PI = 2.0 * math.pi


def bc(ap):
    return ap.partition_broadcast(128)


class Ctx:
    pass


def setup(fw, dbg=False, ext_in=()):
    K = Ctx()
    K.fw = fw
    ext = lambda n, s: fw.dram(n, s, kind="ExternalInput")
    K.xin = ext("xin", [NT, 1024])
    K.cvec = ext("cvec", [128, 1024])
    K.ada_w = [ext("ada_w%d" % l, [1024, 6144]) for l in range(2)]
    K.ada_b = [ext("ada_b%d" % l, [1, 6144]) for l in range(2)]
    K.n1w = [ext("n1w%d" % l, [1, 1024]) for l in range(2)]
    K.n2w = [ext("n2w%d" % l, [1, 1024]) for l in range(2)]
    K.fnw = ext("fnw", [1, 1024])
    K.w1 = [ext("w1_%d" % l, [1024, 4096]) for l in range(2)]
    K.w2 = [ext("w2_%d" % l, [4096, 1024]) for l in range(2)]
    K.wglu = ext("wglu", [1024, 2048])
    K.win = ext("win", [1024, 6208])
    K.wout = ext("wout", [2048, 1024])
    K.lre = ext("lre", [128, 128])
    K.lim = ext("lim", [128, 128])
    K.ldt = ext("ldt", [128, 128])
    K.bz = ext("bz", [128, 2048])
    K.biz = ext("biz", [128, 2048])
    K.cz = ext("cz", [128, 2048])
    K.ciz = ext("ciz", [128, 2048])
    K.s5d = ext("s5d", [128, 8])
    K.convw = ext("convw", [128, 160])
    K.alog = ext("alog", [1, 32])
    K.dtb = ext("dtb", [1, 32])
    K.onw = ext("onw", [1, 128])
    K.cst = ext("cst", [128, 2048])
    K.ttab = ext("ttab", [4, SEQ])
    K.out = fw.dram("out", [4096, 1024], kind="ExternalOutput")
    sk = "ExternalOutput" if dbg else "Internal"
    scr = lambda n, s: fw.dram(n, s, kind=("ExternalInput" if n in ext_in else sk))
    K.silu_c = scr("silu_c", [128, 1024])
    K.mods = [scr("mods%d" % l, [128, 6144]) for l in range(2)]
    K.H = scr("H", [NT, 1024])
    K.Y = scr("Y", [NT, 1024])
    K.X1 = scr("X1", [NT, 1024])
    K.X2 = scr("X2", [NT, 1024])
    K.HID = scr("HID", [NT, 4096])
    K.P = scr("P", [NT, 6208])
    K.O = scr("O", [NT, 2048])
    K.X3 = scr("X3", [NT, 1024])
    K.X4 = scr("X4", [NT, 1024])
    K.C = fw.sbuf([128, 2048], name="consts")
    fw.dma("sync", K.C[:], K.cst[:])
    K.ident = K.C.sub(0, [128, 128])
    K.ones = K.C.sub(128, [128, 128])
    K.sgnA = K.C.sub(256, [128, 1])
    K.sgnB = K.C.sub(257, [128, 1])
    K.gm = K.C.sub(264, [128, 8])
    K.arena = fw.sbuf([128, 41 * 1024], name="arena")
    K.ps = fw.psum([128, 4096], name="ps")
    K.segs_all = []
    K.segs_lat = []
    for s in range(2):
        K.segs_all += [(s * SEQ, 2, 2), (s * SEQ + 256, 16, s)]
        K.segs_lat += [(s * SEQ + 256, 16, s)]
    return K


def host_consts():
    c = np.zeros((128, 2048), np.float32)
    c[:, 0:128] = np.eye(128)
    c[:, 128:256] = 1.0
    c[:64, 256] = -1.0
    c[64:, 256] = 1.0
    c[:64, 257] = 1.0
    c[64:, 257] = -1.0
    for g in range(8):
        c[16 * g:16 * g + 16, 264 + g] = 1.0
    return gdn_consts(c)


class Arena:
    def __init__(self, K):
        self.K = K
        self.off = 0

    def get(self, shape, name="a"):
        n = 1
        for s in shape[1:]:
            n *= s
        t = self.K.arena.sub(self.off, shape, name)
        self.off += n
        return t


def mods_phase(K):
    fw = K.fw
    A = Arena(K)
    t = A.get([128, 1024])
    s = A.get([128, 1024])
    fw.dma("sync", t[:], K.cvec[:])
    fw.X("act", "activation", s[:], t[:], AF.Sigmoid)
    fw.X("dve", "tensor_tensor", s[:], s[:], t[:], ALU.mult)
    fw.dma("pool", K.silu_c[:], s[:])
    for l in range(2):
        gemm(K, K.silu_c, 1024, K.ada_w[l], 6144, K.mods[l], [(0, 1, 0)], "bias", bias=K.ada_b[l])


def normmod(K, X, Y, segs, wrow, mods, ksh, ksc, yoff=0, omap=False):
    fw = K.fw
    A = Arena(K)
    Ar, Br, Tr = A.get([128, 1024]), A.get([128, 1024]), A.get([128, 1024])
    xt = [A.get([128, 1024]) for _ in range(2)]
    yt = [A.get([128, 1024]) for _ in range(2)]
    junk = A.get([128, 1024])
    ss = [A.get([128, 1]) for _ in range(2)]
    n = 0
    for (r0, nt, mr) in segs:
        fw.dma("sync", Ar[:], wrow[:].map(bc))
        if ksc is not None:
            fw.dma("sync", Tr[:], mods[mr:mr + 1, ksc * 1024:(ksc + 1) * 1024].map(bc))
            fw.X("dve", "scalar_tensor_tensor", Ar[:], Tr[:], 1.0, Ar[:], ALU.add, ALU.mult)
            fw.dma("sync", Br[:], mods[mr:mr + 1, ksh * 1024:(ksh + 1) * 1024].map(bc))
        for t in range(nt):
            x, y, s = xt[n % 2], yt[n % 2], ss[n % 2]
            n += 1
            rows = slice(r0 + 128 * t, r0 + 128 * t + 128)
            fw.dma("sync", x[:], X[rows, :])
            fw.X("dve", "memset", s[:], 0.0)
            fw.X("act", "activation", junk[:], x[:], AF.Square, accum_out=s[:])
            fw.X("act", "activation", s[:], s[:], AF.Sqrt, bias=EPS, scale=1.0 / 1024)
            fw.X("dve", "reciprocal", s[:], s[:])
            fw.X("dve", "scalar_tensor_tensor", y[:], x[:], s[:], Ar[:], ALU.mult, ALU.mult)
            if ksc is not None:
                fw.X("pool", "tensor_tensor", y[:], y[:], Br[:], ALU.add)
            orows = slice(rows.start - yoff, rows.stop - yoff)
            if omap:
                sq = rows.start // SEQ
                orows = slice(rows.start - 256 * (sq + 1), rows.stop - 256 * (sq + 1))
            fw.dma("pool", Y[orows, :], y[:])


def gemm(K, X, Kd, W, N, Y, segs, epi, bias=None, R=None, mods=None, kgate=None, woff=0, xoff=0, pre=None):
    fw = K.fw
    KC = Kd // 128
    NP = 512 if KC <= 16 else 128
    G = 8 if KC <= 8 else 4
    A = Arena(K)
    gx = [A.get([128, Kd]) for _ in range(2)]
    XT = A.get([128, KC, G * 128])
    nwb = 2 if epi != "glu" else 4
    Wp = [A.get([128, KC, NP]) for _ in range(nwb)]
    ot = [A.get([128, NP]) for _ in range(2)]
    rt = [A.get([128, NP]) for _ in range(2)]
    sg = [A.get([128, NP]) for _ in range(2)]
    grow = A.get([128, 1024])
    if pre is not None:
        assert Kd == 1024
        pAr, pBr, pTr, pjunk = [A.get([128, 1024]) for _ in range(4)]
        prss = [A.get([128, 1]) for _ in range(2)]
        npre = 0
    Wv = W.h.rearrange("(kc p) n -> p kc n", p=128)
    Wt = Tile(_APHk(Wv), [128, KC, W.shape[1]], W.name + "_v")
    nps = 0
    npw = 0
    nev = 0
    for (r0, nt, mr) in segs:
        if epi in ("res", "glu"):
            fw.dma("sync", grow[:], mods[mr:mr + 1, kgate * 1024:(kgate + 1) * 1024].map(bc))
        if pre is not None:
            pw, pm, pksh, pksc = pre
            fw.dma("sync", pAr[:], pw[:].map(bc))
            fw.dma("sync", pTr[:], pm[mr:mr + 1, pksc * 1024:(pksc + 1) * 1024].map(bc))
            fw.X("dve", "scalar_tensor_tensor", pAr[:], pTr[:], 1.0, pAr[:], ALU.add, ALU.mult)
            fw.dma("sync", pBr[:], pm[mr:mr + 1, pksh * 1024:(pksh + 1) * 1024].map(bc))
        for g0 in range(0, nt, G):
            gn = min(G, nt - g0)
            for t in range(gn):
                x = gx[t % 2]
                rows = slice(r0 + 128 * (g0 + t) - xoff, r0 + 128 * (g0 + t) + 128 - xoff)
                fw.dma("sync", x[:], X[rows, :])
                if pre is not None:
                    sq = prss[npre % 2]
                    npre += 1
                    fw.X("dve", "memset", sq[:], 0.0)
                    fw.X("act", "activation", pjunk[:], x[:], AF.Square, accum_out=sq[:])
                    fw.X("act", "activation", sq[:], sq[:], AF.Sqrt, bias=EPS, scale=1.0 / 1024)
                    fw.X("dve", "reciprocal", sq[:], sq[:])
                    fw.X("dve", "scalar_tensor_tensor", x[:], x[:], sq[:], pAr[:], ALU.mult, ALU.mult)
                    fw.X("dve", "tensor_tensor", x[:], x[:], pBr[:], ALU.add)
                for kc in range(KC):
                    pt = K.ps.sub(512 * (nps % 2), [128, 128])
                    nps += 1
                    fw.X("pe", "transpose", pt[:], x[:, 128 * kc:128 * kc + 128], K.ident[:])
                    fw.X("act" if kc % 2 else "dve", "copy" if kc % 2 else "tensor_copy",
                         XT[:, kc, 128 * t:128 * t + 128], pt[:])
            for n0 in range(0, N, NP):
                w = min(NP, N - n0)
                if epi == "glu":
                    cols = [n0, 1024 + n0]
                else:
                    cols = [woff + n0]
                wps = []
                for c0 in cols:
                    wp = Wp[npw % nwb]
                    npw += 1
                    wps.append(wp)
                    for k0 in range(0, KC, 8):
                        fw.dma("sync", wp[:, k0:k0 + 8, 0:w], Wt[:, k0:k0 + 8, c0:c0 + w])
                for t in range(gn):
                    rows = slice(r0 + 128 * (g0 + t), r0 + 128 * (g0 + t) + 128)
                    pss = []
                    for wi, wp in enumerate(wps):
                        py = K.ps.sub(1024 + 512 * (nev % 4), [128, w])
                        nev += 1
                        pss.append(py)
                        for kc in range(KC):
                            fw.X("pe", "matmul", py[:], XT[:, kc, 128 * t:128 * t + 128], wp[:, kc, 0:w],
                                 start=(kc == 0), stop=(kc == KC - 1))
                    o = ot[nev % 2]
                    py = pss[0]
                    if epi == "none":
                        fw.X("act", "copy", o[:, 0:w], py[:])
                    elif epi == "relu2":
                        fw.X("act", "activation", o[:, 0:w], py[:], AF.Relu)
                        fw.X("pool", "tensor_tensor", o[:, 0:w], o[:, 0:w], o[:, 0:w], ALU.mult)
                    elif epi == "bias":
                        r = rt[nev % 2]
                        fw.dma("sync", r[:, 0:w], bias[0:1, n0:n0 + w].map(bc))
                        fw.X("dve", "tensor_tensor", o[:, 0:w], py[:], r[:, 0:w], ALU.add)
                    elif epi in ("res", "glu"):
                        r = rt[nev % 2]
                        fw.dma("sync", r[:, 0:w], R[rows, n0:n0 + w])
                        sgt = sg[nev % 2]
                        if epi == "glu":
                            fw.X("act", "activation", sgt[:, 0:w], pss[1][:], AF.Sigmoid)
                            fw.X("dve", "tensor_tensor", sgt[:, 0:w], py[:], sgt[:, 0:w], ALU.mult)
                            fw.X("dve", "tensor_tensor", sgt[:, 0:w], sgt[:, 0:w], grow[:, n0:n0 + w], ALU.mult)
                        else:
                            fw.X("dve", "tensor_tensor", sgt[:, 0:w], py[:], grow[:, n0:n0 + w], ALU.mult)
                        fw.X("pool", "tensor_tensor", o[:, 0:w], sgt[:, 0:w], r[:, 0:w], ALU.add)
                    fw.dma("pool", Y[rows, n0:n0 + w], o[:, 0:w])


class _APHk:
    def __init__(self, ap):
        self.ap = ap

    def __getitem__(self, idx):
        return self.ap[idx]


def s5_setup(K):
    fw = K.fw
    Pm = fw.sbuf([128, 3 * 128 + 4 * 2048], name="s5p")
    K.s5R = Pm.sub(0, [128, 128])
    K.s5thn = Pm.sub(128, [128, 128])
    K.s5thn64 = Pm.sub(256, [128, 128])
    K.bbar = Pm.sub(384, [128, 2048])
    K.ibbar = Pm.sub(384 + 2048, [128, 2048])
    K.cstm = Pm.sub(384 + 4096, [128, 2048])
    K.cstim = Pm.sub(384 + 6144, [128, 2048])
    K.s5dt = fw.sbuf([128, 8], name="s5dsb")
    fw.dma("sync", K.s5dt[:], K.s5d[:])
    A = Arena(K)
    g = lambda: A.get([128, 128])
    lre, lim, ldt, th, rho, un, fr, cs, sn, are, aim, xm, t1, t2, cre, cim = [g() for _ in range(16)]
    ki = K.arena.sub(A.off, [128, 128])
    A.off += 128
    kint = Tile(_APHk(ki.h.ap.bitcast(I32)), [128, 128], "kint", base=ki.base, foff=ki.foff)
    fw.dma("sync", lre[:], K.lre[:])
    fw.dma("sync", lim[:], K.lim[:])
    fw.dma("sync", ldt[:], K.ldt[:])
    X = fw.X
    X("act", "activation", ldt[:], ldt[:], AF.Exp)
    X("dve", "tensor_tensor", th[:], lim[:], ldt[:], ALU.mult)
    X("dve", "tensor_tensor", rho[:], lre[:], ldt[:], ALU.mult)
    X("act", "activation", K.s5R[:], rho[:], AF.Exp)

    def frac(dst, src):
        X("dve", "tensor_copy", kint[:], src[:])
        X("dve", "tensor_tensor", dst[:], src[:], kint[:], ALU.subtract)

    X("dve", "tensor_scalar", un[:], th[:], 1.0 / TWO_PI, None, ALU.mult)
    frac(K.s5thn, un)
    X("dve", "tensor_scalar", un[:], K.s5thn[:], 64.0, None, ALU.mult)
    frac(K.s5thn64, un)
    X("dve", "tensor_scalar", un[:], K.s5thn[:], 0.25, None, ALU.add)
    frac(fr, un)
    X("act", "activation", cs[:], fr[:], AF.Sin, scale=TWO_PI)
    X("act", "activation", sn[:], K.s5thn[:], AF.Sin, scale=TWO_PI)
    X("dve", "tensor_tensor", are[:], K.s5R[:], cs[:], ALU.mult)
    X("dve", "tensor_tensor", aim[:], K.s5R[:], sn[:], ALU.mult)
    X("dve", "tensor_scalar", xm[:], are[:], -1.0, None, ALU.add)
    X("dve", "tensor_tensor", t1[:], xm[:], lre[:], ALU.mult)
    X("dve", "tensor_tensor", t2[:], aim[:], lim[:], ALU.mult)
    X("dve", "tensor_tensor", cre[:], t1[:], t2[:], ALU.add)
    X("dve", "tensor_tensor", t1[:], aim[:], lre[:], ALU.mult)
    X("dve", "tensor_tensor", t2[:], xm[:], lim[:], ALU.mult)
    X("dve", "tensor_tensor", cim[:], t1[:], t2[:], ALU.subtract)
    X("dve", "tensor_tensor", t1[:], lre[:], lre[:], ALU.mult)
    X("dve", "tensor_tensor", t2[:], lim[:], lim[:], ALU.mult)
    X("dve", "tensor_tensor", t1[:], t1[:], t2[:], ALU.add)
    X("dve", "reciprocal", t1[:], t1[:])
    X("dve", "tensor_tensor", cre[:], cre[:], t1[:], ALU.mult)
    X("dve", "tensor_tensor", cim[:], cim[:], t1[:], ALU.mult)
    bz, ib, tmp = A.get([128, 128, 16]), A.get([128, 128, 16]), A.get([128, 128, 16])
    fw.dma("sync", bz[:], K.bz[:].map(lambda a: a.rearrange("p (a b) -> p a b", b=16)))
    fw.dma("sync", ib[:], K.biz[:].map(lambda a: a.rearrange("p (a b) -> p a b", b=16)))
    X("dve", "tensor_scalar", ib[:], ib[:], K.sgnA[:], None, ALU.mult)
    b3 = lambda t: t[:].map(lambda a: a.unsqueeze(2).to_broadcast([128, 128, 16]))
    v3 = lambda t: t[:].map(lambda a: a.rearrange("p (a b) -> p a b", b=16))
    X("dve", "tensor_tensor", v3(K.bbar), bz[:], b3(cre), ALU.mult)
    X("dve", "tensor_tensor", tmp[:], ib[:], b3(cim), ALU.mult)
    X("dve", "tensor_tensor", v3(K.bbar), v3(K.bbar), tmp[:], ALU.add)
    X("dve", "tensor_tensor", v3(K.ibbar), ib[:], b3(cre), ALU.mult)
    X("dve", "tensor_tensor", tmp[:], bz[:], b3(cim), ALU.mult)
    X("dve", "tensor_tensor", v3(K.ibbar), v3(K.ibbar), tmp[:], ALU.subtract)
    cz = A.get([128, 2048])
    fw.dma("sync", cz[:], K.cz[:])
    X("dve", "tensor_scalar", K.cstm[:], cz[:], K.sgnB[:], None, ALU.mult)
    cz2 = A.get([128, 2048])
    fw.dma("sync", cz2[:], K.ciz[:])
    X("dve", "tensor_scalar", K.cstim[:], cz2[:], -1.0, None, ALU.mult)


FLAGS = set()
PIECES = [(0, 512), (512, 1024), (1024, 1536), (1536, 2048), (2048, 2304)]


def s5_core(K, H, Y, chunks=range(8), dbg=None):
    fw = K.fw
    X = fw.X
    A = Arena(K)
    big = lambda: A.get([128, SEQ])
    hT = [big(), big()]
    yacc = [big(), big()]
    G0, gs, GS, U, SIN, COS, T1, T0 = [big() for _ in range(8)]
    G0s = [G0, big()]
    GSs = [GS, big()]
    npo = 0
    kraw = big()
    KI = Tile(_APHk(kraw.h.ap.bitcast(I32)), [128, SEQ], "KI", base=kraw.base, foff=kraw.foff)
    LB, LIB, LC, LCI = [A.get([128, 8, 128]) for _ in range(4)]
    tmp = [A.get([128, 512]) for _ in range(2)]
    stg = [A.get([128, 128]) for _ in range(4)]
    X("dve", "memset", LC[:], 0.0)
    X("dve", "memset", LCI[:], 0.0)
    rev = lambda v: v.map(lambda a: a[:, ::-1])
    nst = 0
    npp = 0
    for c in chunks:
        for s in range(2):
            for t in range(18):
                st = stg[nst % 4]
                nst += 1
                fw.dma("sync", st[:], H[s * SEQ + 128 * t: s * SEQ + 128 * t + 128, 128 * c:128 * c + 128])
                pt = K.ps.sub(512 * (nst % 2), [128, 128])
                X("pe", "transpose", pt[:], st[:], K.ident[:])
                X("act", "copy", hT[s][:, 128 * t:128 * t + 128], pt[:])
        first = True
        for d in range(2):
            fw.dma("sync", T1[:], K.ttab[2 * d:2 * d + 1, :].map(bc))
            fw.dma("sync", T0[:], K.ttab[2 * d + 1:2 * d + 2, :].map(bc))
            col0 = (d * 64 + 8 * c) * 16
            for (src, dst) in ((K.bbar, LB), (K.ibbar, LIB)):
                pt = K.ps.sub(1024, [128, 128])
                X("pe", "transpose", pt[:], src[:, col0:col0 + 128], K.ident[:])
                for gl in range(8):
                    X("dve", "tensor_scalar", dst[:, gl, :], pt[:], K.gm[:, gl:gl + 1], None, ALU.mult)
            for gl in range(8):
                X("pool", "tensor_copy", LC[:, gl, 16 * gl:16 * gl + 16], K.cstm[:, col0 + 16 * gl:col0 + 16 * gl + 16])
                X("pool", "tensor_copy", LCI[:, gl, 16 * gl:16 * gl + 16], K.cstim[:, col0 + 16 * gl:col0 + 16 * gl + 16])
            if "stop1" in FLAGS:
                continue
            for gl in range(8):
                dg = d * 64 + 8 * c + gl
                thn = K.s5thn[:, dg:dg + 1]
                thn64 = K.s5thn64[:, dg:dg + 1]
                rbc = lambda n: K.s5R[:, dg:dg + 1].map(lambda a: a.to_broadcast([128, n]))
                if "notab" in FLAGS:
                    X("dve", "memset", SIN[:], 0.0)
                    X("dve", "memset", COS[:], 1.0)
                for _ in ([] if "notab" in FLAGS else [0]):
                  X("dve", "tensor_scalar", U[:], T1[:], thn64, None, ALU.mult)
                  X("dve", "scalar_tensor_tensor", U[:], T0[:], thn, U[:], ALU.mult, ALU.add)
                  X("dve", "tensor_copy", KI[:], U[:])
                  X("dve", "tensor_tensor", SIN[:], U[:], KI[:], ALU.subtract)
                  X("act", "activation", SIN[:], SIN[:], AF.Sin, scale=TWO_PI)
                  X("dve", "tensor_scalar", U[:], U[:], 0.25, None, ALU.add)
                  X("dve", "tensor_copy", KI[:], U[:])
                  X("dve", "tensor_tensor", COS[:], U[:], KI[:], ALU.subtract)
                  X("act", "activation", COS[:], COS[:], AF.Sin, scale=TWO_PI)
                for s in range(2):
                    G0 = G0s[s]
                    for (p0, p1) in PIECES:
                        n = p1 - p0
                        ps1 = K.ps.sub(1536 + 1024 * (npp % 2), [128, n])
                        ps2 = K.ps.sub(2048 + 1024 * (npp % 2), [128, n])
                        tm = tmp[npp % 2]
                        npp += 1
                        X("pe", "matmul", ps1[:], LB[:, gl, :], hT[s][:, p0:p1], start=True, stop=True)
                        X("pe", "matmul", ps2[:], LIB[:, gl, :], hT[s][:, p0:p1], start=True, stop=True)
                        X("dve", "tensor_tensor", tm[:, 0:n], ps2[:], SIN[:, p0:p1], ALU.mult)
                        X("dve", "tensor_tensor", G0[:, p0:p1], ps1[:], COS[:, p0:p1], ALU.mult)
                        X("dve", "tensor_tensor", G0[:, p0:p1], G0[:, p0:p1], tm[:, 0:n], ALU.subtract)
                for s in range(2):
                    G0 = G0s[s]
                    GS = GSs[s]
                    if "noscan" in FLAGS:
                        X("dve", "tensor_copy", gs[:], G0[:])
                    elif d == 0:
                        X("dve", "tensor_tensor_scan", gs[:], rbc(SEQ), G0[:], 0.0, ALU.mult, ALU.add)
                    else:
                        X("dve", "tensor_tensor_scan", rev(gs[:, 0:256]), rbc(256), rev(G0[:, 0:256]), 0.0,
                          ALU.mult, ALU.add)
                        X("dve", "tensor_tensor_scan", rev(gs[:, 256:SEQ]), rbc(SEQ - 256), rev(G0[:, 256:SEQ]),
                          gs[:, 0:1], ALU.mult, ALU.add)
                    X("dve", "tensor_tensor", G0[:], gs[:], COS[:], ALU.mult)
                    X("dve", "tensor_tensor", GS[:], gs[:], SIN[:], ALU.mult)
                    for (p0, p1) in PIECES:
                        n = p1 - p0
                        po = K.ps.sub((3584, 0, 512)[npo % 3], [128, n])
                        npo += 1
                        X("pe", "matmul", po[:], LC[:, gl, :], G0[:, p0:p1], start=True, stop=False)
                        X("pe", "matmul", po[:], LCI[:, gl, :], GS[:, p0:p1], start=False, stop=True)
                        if first:
                            X("act", "copy", yacc[s][:, p0:p1], po[:])
                        else:
                            X("dve", "tensor_tensor", yacc[s][:, p0:p1], yacc[s][:, p0:p1], po[:], ALU.add)
                first = False
        G0 = G0s[0]
        for s in ([] if ("stop1" in FLAGS or "stop2" in FLAGS) else range(2)):
            y = yacc[s]
            X("dve", "scalar_tensor_tensor", y[:], hT[s][:], K.s5dt[:, c:c + 1], y[:], ALU.mult, ALU.add)
            X("dve", "tensor_tensor", G0[:], y[:], y[:], ALU.mult)
            X("dve", "tensor_scalar", G0[:], G0[:], 0.044715, 1.0, ALU.mult, ALU.add)
            X("dve", "tensor_tensor", G0[:], G0[:], y[:], ALU.mult)
            X("act", "activation", G0[:], G0[:], AF.Tanh, scale=math.sqrt(2.0 / math.pi))
            X("dve", "tensor_scalar", G0[:], G0[:], 1.0, 0.5, ALU.add, ALU.mult)
            X("dve", "tensor_tensor", y[:], G0[:], y[:], ALU.mult)
            for t in range(18):
                st = stg[nst % 4]
                nst += 1
                pt = K.ps.sub(512 * (nst % 2), [128, 128])
                X("pe", "transpose", pt[:], y[:, 128 * t:128 * t + 128], K.ident[:])
                X("act", "copy", st[:], pt[:])
                fw.dma("pool", Y[s * SEQ + 128 * t: s * SEQ + 128 * t + 128, 128 * c:128 * c + 128], st[:])


def layer0(K):
    m = K.mods[0]
    normmod(K, K.xin, K.H, K.segs_all, K.n1w[0], m, 0, 1)
    s5_core(K, K.H, K.Y)
    gemm(K, K.Y, 1024, K.wglu, 1024, K.X1, K.segs_all, "glu", R=K.xin, mods=m, kgate=2)
    gemm(K, K.X1, 1024, K.w1[0], 4096, K.HID, K.segs_all, "relu2", pre=(K.n2w[0], m, 3, 4))
    gemm(K, K.HID, 4096, K.w2[0], 1024, K.X2, K.segs_all, "res", R=K.X1, mods=m, kgate=5)


def host_shared(inp):
    f = lambda a: np.ascontiguousarray(np.asarray(a, dtype=np.float32))
    d = {}
    for l in range(2):
        d["ada_w%d" % l] = f(inp["ada_w"][l])
        d["ada_b%d" % l] = f(inp["ada_b"][l][None, :])
        d["n1w%d" % l] = f(inp["norm1_w"][l][None, :])
        d["n2w%d" % l] = f(inp["norm2_w"][l][None, :])
        d["w1_%d" % l] = f(inp["mlp_w1"][l])
        d["w2_%d" % l] = f(inp["mlp_w2"][l])
    d["fnw"] = f(inp["final_norm_w"][None, :])
    d["wglu"] = f(inp["s5_w_glu"][0])
    d["win"] = f(inp["gdn_w_in"][0])
    d["wout"] = f(inp["gdn_w_out"][0])
    lre = np.transpose(inp["s5_lam_re"][0], (2, 0, 1)).reshape(64, 128)
    lim = np.transpose(inp["s5_lam_im"][0], (2, 0, 1)).reshape(64, 128)
    d["lre"] = f(np.concatenate([lre, lre], 0))
    d["lim"] = f(np.concatenate([lim, lim], 0))
    d["ldt"] = f(np.broadcast_to(inp["s5_log_dt"][0].reshape(1, 128), (128, 128)))
    bre = np.transpose(inp["s5_b_re"][0], (2, 0, 1, 3)).reshape(64, 2048)
    bim = np.transpose(inp["s5_b_im"][0], (2, 0, 1, 3)).reshape(64, 2048)
    d["bz"] = f(np.concatenate([bre, bim], 0))
    d["biz"] = f(np.concatenate([bim, bre], 0))
    cre = np.transpose(inp["s5_c_re"][0], (3, 0, 1, 2)).reshape(64, 2048)
    cim = np.transpose(inp["s5_c_im"][0], (3, 0, 1, 2)).reshape(64, 2048)
    d["cz"] = f(np.concatenate([cre, cim], 0))
    d["ciz"] = f(np.concatenate([cim, cre], 0))
    d["s5d"] = f(inp["s5_d"][0].reshape(8, 128).T)
    d["convw"] = f(np.transpose(inp["gdn_conv_w"][0].reshape(5, 32, 128), (2, 1, 0)).reshape(128, 160))
    d["alog"] = f(inp["gdn_a_log"][0].reshape(1, 32))
    d["dtb"] = f(inp["gdn_dt_bias"][0].reshape(1, 32))
    d["onw"] = f(inp["gdn_onorm_w"][0][None, :])
    d["cst"] = host_consts()
    t = np.arange(SEQ)
    trev = np.where(t < 256, 255 - t, 2559 - t)
    d["ttab"] = f(np.stack([t // 64, t % 64, trev // 64, trev % 64]))
    return d


def host_core(inp, core):
    d = {}
    b0 = 2 * core
    x = np.asarray(inp["x"]); ctx = np.asarray(inp["ctx"])
    d["xin"] = np.ascontiguousarray(np.concatenate([ctx[b0], x[b0], ctx[b0 + 1], x[b0 + 1]], 0).astype(np.float32))
    cv = np.zeros((128, 1024), np.float32)
    cv[0] = inp["c"][b0]; cv[1] = inp["c"][b0 + 1]; cv[2] = inp["c_ctx"]
    d["cvec"] = cv
    return d


def gdn_consts(c):
    t = np.arange(64)
    c[:64, 512:576] = (t[:, None] <= t[None, :])
    c[:64, 576:640] = (t[:, None] >= t[None, :])
    c[63, 640:768] = 1.0
    c[0, 768:896] = 1.0
    ji = lambda f: np.concatenate([f(t[:, None], t[None, :])] * 2, 1).astype(np.float32)
    c[:64, 896:1024] = ji(lambda j, i: i >= j)
    c[:64, 1024:1152] = -ji(lambda j, i: i > j)
    c[:64, 1152:1280] = ji(lambda j, i: i <= j)
    c[:64, 1280:1408] = -ji(lambda j, i: i < j)
    c[:64, 1408:1536] = ji(lambda j, i: i == j)
    return c


def interleave(gens):
    gens = list(gens)
    lim = [int(f[2:]) for f in FLAGS if f.startswith("ut")]
    nround = 0
    while gens:
        if lim and nround >= lim[0]:
            return
        nround += 1
        nxt = []
        for g in gens:
            try:
                next(g)
                nxt.append(g)
            except StopIteration:
                pass
        gens = nxt


def gdn_core(K, P, O, seqs=(0, 1), khs=range(8)):
    fw = K.fw
    X = fw.X
    C = K.C
    id64 = C.sub(0, [64, 64])
    ones64 = C.sub(128, [64, 64])
    TRI = [C.sub(512, [64, 64]), C.sub(576, [64, 64])]
    SEL = [C.sub(640, [64, 128]), C.sub(768, [64, 128])]
    MASKI = [C.sub(896, [64, 2, 64]), C.sub(1152, [64, 2, 64])]
    NMASKS = [C.sub(1024, [64, 2, 64]), C.sub(1280, [64, 2, 64])]
    ID2 = C.sub(1408, [64, 2, 64])
    A = Arena(K)
    NG = 36 * 32
    BETA, Gg, GC, EGC, DK, NEGC = [A.get([64, 36, 32]) for _ in range(6)]
    EGL = A.get([128, 36, 32])
    stg, raw, tmp = A.get([128, SEQ]), A.get([128, SEQ]), A.get([128, SEQ])
    Graw = K.arena.sub(stg.foff, [64, 36, 64])
    Zt = K.arena.sub(stg.foff, [64, 32, 128])
    qT, kT = A.get([128, SEQ]), A.get([128, SEQ])
    vT = [A.get([128, SEQ]) for _ in range(2)]
    OACC = [A.get([64, 36, 128]) for _ in range(2)]
    t2 = lambda: A.get([64, 2, 64])
    UT = []
    for u in range(2):
        d = Ctx()
        d.DG, d.E, d.Ei, d.NEs = t2(), t2(), t2(), t2()
        d.Xb = [t2(), t2()]
        d.Yb = [t2(), t2()]
        d.Wb = [t2(), t2()]
        d.TT = [t2(), t2()]
        d.PT = [t2(), t2()]
        d.KTOK = [A.get([64, 128]) for _ in range(2)]
        d.VTOK = [A.get([64, 2, 128]) for _ in range(2)]
        d.bA = 512 * (2 * u)
        d.bB = 512 * (2 * u + 1)
        UT.append(d)
    RHS2 = [A.get([64, 128]) for _ in range(4)]
    VNEW = [A.get([64, 128]) for _ in range(4)]
    VN2 = [A.get([64, 128]) for _ in range(4)]
    S = [A.get([128, 128]) for _ in range(4)]
    rsp = [A.get([128, 512]) for _ in range(2)]
    onw = A.get([64, 128])
    rows32 = A.get([64, 64])
    ssr = A.get([64, 32])
    cw = A.get([128, 160])
    fw.dma("sync", cw[:], K.convw[:])
    fw.dma("sync", onw[:], K.onw[:].map(lambda a: a.partition_broadcast(64)))
    fw.dma("sync", rows32[:, 0:32], K.alog[:].map(lambda a: a.partition_broadcast(64)))
    fw.dma("sync", rows32[:, 32:64], K.dtb[:].map(lambda a: a.partition_broadcast(64)))
    X("act", "activation", rows32[:, 0:32], rows32[:, 0:32], AF.Exp)
    X("dve", "tensor_scalar", rows32[:, 0:32], rows32[:, 0:32], -1.0, None, ALU.mult)
    b36 = lambda v: v.map(lambda a: a.unsqueeze(1).to_broadcast([64, 36, 32]))
    ps = K.ps
    for s in seqs:
        base = s * SEQ
        fw.dma("sync", Graw[:], P[base:base + SEQ, 6144:6208].map(lambda a: a.rearrange("(c p) g -> p c g", p=64)))
        X("act", "activation", BETA[:], Graw[:, :, 0:32], AF.Sigmoid)
        X("dve", "tensor_tensor", Gg[:], Graw[:, :, 32:64], b36(rows32[:, 32:64]), ALU.add)
        X("act", "activation", Gg[:], Gg[:], AF.Exp)
        X("act", "activation", Gg[:], Gg[:], AF.Ln, bias=1.0)
        X("dve", "tensor_tensor", Gg[:], Gg[:], b36(rows32[:, 0:32]), ALU.mult)
        g2 = Gg[:].map(lambda a: a.rearrange("p c g -> p (c g)"))
        if "nogates" in FLAGS:
            X("dve", "memset", GC[:], -0.1)
            X("dve", "memset", EGL[:], -0.1)
        for d in ([] if "nogates" in FLAGS else range(2)):
            for j in range(3):
                pc = ps.sub(512 * j, [64, 384])
                X("pe", "matmul", pc[:], TRI[d][:], g2.map(lambda a: a[:, 384 * j:384 * j + 384]), start=True, stop=True)
                pc3 = pc[:].map(lambda a: a.rearrange("p (c g) -> p c g", g=32)[:, :, 16 * d:16 * d + 16])
                X("act", "copy", GC[:, 12 * j:12 * j + 12, 16 * d:16 * d + 16], pc3)
        gc2 = GC[:].map(lambda a: a.rearrange("p c g -> p (c g)"))
        for j in ([] if "nogates" in FLAGS else range(3)):
            for d in range(2):
                pl = ps.sub(512 * (3 + j), [128, 384])
                X("pe", "matmul", pl[:], SEL[d][:], gc2.map(lambda a: a[:, 384 * j:384 * j + 384]), start=True, stop=True)
                pl3 = pl[:].map(lambda a: a.rearrange("p (c g) -> p c g", g=32)[:, :, 16 * d:16 * d + 16])
                X("dve", "tensor_copy", EGL[:, 12 * j:12 * j + 12, 16 * d:16 * d + 16], pl3)
        X("dve", "tensor_scalar", Gg[:], GC[:], -1.0, None, ALU.mult)
        X("dve", "tensor_tensor", DK[:], EGL[0:64], GC[:], ALU.subtract)
        X("act", "activation", DK[:], DK[:], AF.Exp)
        X("act", "activation", EGC[:], GC[:], AF.Exp)
        X("dve", "tensor_scalar", NEGC[:], EGC[:], -1.0, None, ALU.mult)
        X("act", "activation", EGL[:], EGL[:], AF.Exp)
        if "g_stop" in FLAGS:
            continue
        for kh in khs:
            specs = [(kh * 128, kh, qT, 128 ** -0.5), (3072 + kh * 128, 8 + kh, kT, 1.0),
                     (4096 + (2 * kh) * 128, 16 + 2 * kh, vT[0], None),
                     (4096 + (2 * kh + 1) * 128, 17 + 2 * kh, vT[1], None)]
            for (col0, cc, dst, nrm) in specs:
                fw.dma("sync", stg[:].map(lambda a: a.rearrange("p (t n) -> p t n", n=128)),
                       P[base:base + SEQ, col0:col0 + 128].map(lambda a: a.rearrange("(t p) n -> p t n", p=128)))
                for t0 in range(0, 18, 4):
                    nt_ = min(4, 18 - t0)
                    pt = ps.sub(512 * (4 + (t0 // 4) % 2), [128, 128 * nt_])
                    for t in range(nt_):
                        X("pe", "transpose", pt[:, 128 * t:128 * t + 128], stg[:, 128 * (t0 + t):128 * (t0 + t) + 128],
                          K.ident[:])
                    X("act", "copy", raw[:, 128 * t0:128 * (t0 + nt_)], pt[:])
                wc = lambda k: cw[:, cc * 5 + k:cc * 5 + k + 1]
                X("dve", "tensor_scalar", dst[:], raw[:], wc(2), None, ALU.mult)
                for k in (0, 1, 3, 4):
                    sh = k - 2
                    lo_o, hi_o = (0, 256 - sh) if sh > 0 else (-sh, 256)
                    X("dve", "scalar_tensor_tensor", dst[:, lo_o:hi_o], raw[:, lo_o + sh:hi_o + sh], wc(k),
                      dst[:, lo_o:hi_o], ALU.mult, ALU.add)
                    lo, hi = (0, 64 - sh) if sh > 0 else (-sh, 64)
                    v3 = lambda tl, a0, a1: tl[:, 256:SEQ].map(
                        lambda a: a.rearrange("p (r w) -> p r w", w=64)[:, :, a0:a1])
                    X("dve", "scalar_tensor_tensor", v3(dst, lo, hi), v3(raw, lo + sh, hi + sh), wc(k),
                      v3(dst, lo, hi), ALU.mult, ALU.add)
                X("act", "activation", tmp[:], dst[:], AF.Sigmoid)
                X("dve", "tensor_tensor", dst[:], dst[:], tmp[:], ALU.mult)
                if nrm is not None:
                    X("dve", "tensor_tensor", tmp[:], dst[:], dst[:], ALU.mult)
                    for pi, (p0, p1) in enumerate(PIECES):
                        n = p1 - p0
                        pp = ps.sub(512 * (6 + pi % 2), [128, n])
                        rs = rsp[pi % 2]
                        X("pe", "matmul", pp[:], K.ones[:], tmp[:, p0:p1], start=True, stop=True)
                        X("act", "activation", rs[:, 0:n], pp[:], AF.Sqrt, bias=EPS, scale=1.0)
                        X("dve", "reciprocal", rs[:, 0:n], rs[:, 0:n])
                        X("dve", "scalar_tensor_tensor", dst[:, p0:p1], dst[:, p0:p1], float(nrm), rs[:, 0:n],
                          ALU.mult, ALU.mult)
            if "p_stop" in FLAGS:
                continue
            for x in range(4):
                X("pool", "memset", S[x][:], 0.0)
            for h in range(2):
                X("pool", "memset", OACC[h][:], 0.0)

            def ut_unit(d, c, b):
                U = UT[d]
                cs = slice(64 * c, 64 * c + 64)
                gcol = [d * 16 + 2 * kh + h for h in range(2)]
                pA = lambda o, shp: ps.sub(U.bA + o, shp)
                pB = lambda o, shp: ps.sub(U.bB + o, shp)
                KK, QK, R = pA(0, [64, 64]), pA(64, [64, 64]), pA(128, [64, 2, 64])
                KTp, V0p, V1p = pB(0, [64, 128]), pB(128, [64, 128]), pB(256, [64, 128])
                YTp, Xn, Yn, Wn = pB(0, [64, 2, 64]), pA(0, [64, 2, 64]), pB(128, [64, 2, 64]), pB(256, [64, 2, 64])
                outp = c >= 4
                for h in range(2):
                    X("dve", "tensor_scalar", U.DG[:, h, :], id64[:], GC[:, c, gcol[h]:gcol[h] + 1], None, ALU.mult)
                if "nomm" not in FLAGS:
                    X("pe", "matmul", KK[:], kT[:, cs], kT[:, cs], start=True, stop=True)
                    if outp:
                        X("pe", "matmul", QK[:], kT[:, cs], qT[:, cs], start=True, stop=True)
                if "notr" not in FLAGS:
                    X("pe", "matmul", KTp[:], kT[:, cs], K.ident[:], start=True, stop=True)
                    X("pe", "matmul", V0p[:], vT[0][:, cs], K.ident[:], start=True, stop=True)
                    X("pe", "matmul", V1p[:], vT[1][:, cs], K.ident[:], start=True, stop=True)
                if "nor" not in FLAGS:
                    X("pe", "matmul", R[:].map(lambda a: a.rearrange("p a b -> p (a b)")), ones64[:],
                      U.DG[:].map(lambda a: a.rearrange("p a b -> p (a b)")), start=True, stop=True)
                if "notr" not in FLAGS and "nocp" not in FLAGS:
                    ce = ("dve", "tensor_copy") if "dvecp" in FLAGS else ("act", "copy")
                    if "rdkk" in FLAGS:
                        X(ce[0], ce[1], RHS2[d][:, 0:64], KK[:])
                    elif "dst2" in FLAGS:
                        X(ce[0], ce[1], RHS2[d][:], KTp[:])
                    elif "src2" in FLAGS:
                        X(ce[0], ce[1], U.KTOK[b][:], RHS2[d][:])
                    else:
                        X(ce[0], ce[1], U.KTOK[b][:], KTp[:])
                    if "novt" not in FLAGS:
                        X(ce[0], ce[1], U.VTOK[b][:, 0, :], V0p[:])
                        X(ce[0], ce[1], U.VTOK[b][:, 1, :], V1p[:])
                yield
                for h in range(2):
                    X("dve", "tensor_tensor", U.E[:, h, :], R[:, h, :], MASKI[d][:, h, :], ALU.mult)
                for h in range(2):
                    X("dve", "scalar_tensor_tensor", U.E[:, h, :], MASKI[d][:, h, :], Gg[:, c, gcol[h]:gcol[h] + 1],
                      U.E[:, h, :], ALU.mult, ALU.add)
                X("act", "activation", U.E[:], U.E[:], AF.Exp)
                yield
                X("dve", "tensor_tensor", U.Ei[:], U.E[:], MASKI[d][:], ALU.mult)
                X("dve", "tensor_tensor", U.NEs[:], U.E[:], NMASKS[d][:], ALU.mult)
                yield
                Xc, Yc, Wc = U.Xb[0], U.Yb[0], U.Wb[0]
                for h in range(2):
                    X("dve", "scalar_tensor_tensor", Xc[:, h, :], KK[:], BETA[:, c, gcol[h]:gcol[h] + 1],
                      U.NEs[:, h, :], ALU.mult, ALU.mult)
                    if outp:
                        X("dve", "tensor_tensor", U.PT[b][:, h, :], QK[:], U.Ei[:, h, :], ALU.mult)
                X("dve", "tensor_tensor", Wc[:], Xc[:], ID2[:], ALU.add)
                yield
                for h in range(2):
                    X("pe", "matmul", YTp[:, h, :], Xc[:, h, :], id64[:], start=True, stop=True)
                X("act", "copy", Yc[:], YTp[:])
                yield
                for k in range(1, 6):
                    last = k == 5
                    Xn_, Yn_ = U.Xb[k % 2], U.Yb[k % 2]
                    if not last:
                        for h in range(2):
                            X("pe", "matmul", Xn[:, h, :], Yc[:, h, :], Xc[:, h, :], start=True, stop=True)
                    for h in range(2):
                        X("pe", "matmul", Yn[:, h, :], Xc[:, h, :], Yc[:, h, :], start=True, stop=True)
                    X("act", "copy", Yn_[:], Yn[:])
                    if not last:
                        X("dve", "tensor_copy", Xn_[:], Xn[:])
                    yield
                    for h in range(2):
                        X("pe", "matmul", Wn[:, h, :], Yn_[:, h, :], Wc[:, h, :], start=True, stop=True)
                    Wn_ = U.TT[b] if last else U.Wb[k % 2]
                    X("dve", "tensor_tensor", Wn_[:], Wn[:], Wc[:], ALU.add)
                    Xc, Yc, Wc = Xn_, Yn_, Wn_
                    yield

            def rec(h, d, c, b):
                U = UT[d]
                x = 2 * d + h
                gc = d * 16 + 2 * kh + h
                col = lambda T_: T_[:, c, gc:gc + 1]
                cs = slice(64 * c, 64 * c + 64)
                bk = 2048 + 512 * x
                r0, r1, r2, rS = (ps.sub(bk, [64, 128]), ps.sub(bk + 128, [64, 128]), ps.sub(bk + 256, [64, 128]),
                                  ps.sub(bk + 384, [128, 128]))
                outp = c >= 4
                X("pe", "matmul", r0[:], kT[:, cs], S[x][:], start=True, stop=True)
                yield
                X("dve", "scalar_tensor_tensor", RHS2[x][:], r0[:], col(NEGC), U.VTOK[b][:, h, :], ALU.mult, ALU.add)
                yield
                X("pe", "matmul", r1[:], U.TT[b][:, h, :], RHS2[x][:], start=True, stop=True)
                if outp:
                    X("pe", "matmul", r2[:], qT[:, cs], S[x][:], start=True, stop=True)
                yield
                X("dve", "tensor_scalar", VNEW[x][:], r1[:], col(BETA), None, ALU.mult)
                X("dve", "tensor_scalar", VN2[x][:], VNEW[x][:], col(DK), None, ALU.mult)
                if outp:
                    X("dve", "scalar_tensor_tensor", OACC[h][:, c, :], r2[:], col(EGC), OACC[h][:, c, :],
                      ALU.mult, ALU.add)
                yield
                X("pe", "matmul", rS[:], U.KTOK[b][:], VN2[x][:], start=True, stop=True)
                if outp:
                    X("pe", "matmul", r0[:], U.PT[b][:, h, :], VNEW[x][:], start=True, stop=True)
                yield
                X("dve", "scalar_tensor_tensor", S[x][:], S[x][:], EGL[:, c, gc:gc + 1], rS[:], ALU.mult, ALU.add)
                if outp:
                    X("dve", "tensor_tensor", OACC[h][:, c, :], OACC[h][:, c, :], r0[:], ALU.add)
                yield

            order = [[0, 1, 2, 3] + list(range(4, 36)), [3, 2, 1, 0] + list(range(35, 3, -1))]
            interleave([ut_unit(0, order[0][0], 0), ut_unit(1, order[1][0], 0)])
            if "u_stop" in FLAGS:
                continue
            for n in range(1 if "r_stop" in FLAGS else 36):
                gens = []
                if n + 1 < 36:
                    gens += [ut_unit(0, order[0][n + 1], (n + 1) % 2), ut_unit(1, order[1][n + 1], (n + 1) % 2)]
                gens += [rec(h, d, order[d][n], n % 2) for d in range(2) for h in range(2)]
                interleave(gens)
            for h in range(2):
                hv = 2 * kh + h
                o = OACC[h][:, 4:36, :]
                fw.dma("sync", Zt[:], P[base + 256:base + SEQ, 1024 + hv * 128:1024 + hv * 128 + 128].map(
                    lambda a: a.rearrange("(c p) n -> p c n", p=64)))
                tq = K.arena.sub(tmp.foff, [64, 32, 128])
                tq2 = K.arena.sub(tmp.foff + 4096, [64, 32, 128]) if False else None
                X("dve", "tensor_tensor", tq[:], o, o, ALU.mult)
                X("dve", "reduce_sum", ssr[:], tq[:], AX.X)
                X("act", "activation", ssr[:], ssr[:], AF.Sqrt, bias=EPS, scale=1.0 / 128)
                X("dve", "reciprocal", ssr[:], ssr[:])
                X("dve", "tensor_tensor", o, o, ssr[:].map(lambda a: a.unsqueeze(2).to_broadcast([64, 32, 128])), ALU.mult)
                X("dve", "tensor_tensor", o, o, onw[:].map(lambda a: a.unsqueeze(1).to_broadcast([64, 32, 128])), ALU.mult)
                X("act", "activation", tq[:], Zt[:], AF.Sigmoid)
                X("dve", "tensor_tensor", tq[:], tq[:], Zt[:], ALU.mult)
                X("dve", "tensor_tensor", o, o, tq[:], ALU.mult)
                fw.dma("pool", O[base + 256:base + SEQ, hv * 128:hv * 128 + 128].map(
                    lambda a: a.rearrange("(c p) n -> p c n", p=64)), o)


def layer1(K):
    m = K.mods[1]
    gemm(K, K.X2, 1024, K.win, 6208, K.P, K.segs_all, "none", pre=(K.n1w[1], m, 0, 1))
    gdn_core(K, K.P, K.O)
    gemm(K, K.O, 2048, K.wout, 1024, K.X3, K.segs_lat, "res", R=K.X2, mods=m, kgate=2)
    gemm(K, K.X3, 1024, K.w1[1], 4096, K.HID, K.segs_lat, "relu2", pre=(K.n2w[1], m, 3, 4))
    gemm(K, K.HID, 4096, K.w2[1], 1024, K.X4, K.segs_lat, "res", R=K.X3, mods=m, kgate=5)


def build_program():
    nc = bass.Bass("TRN2", target_bir_lowering=False)
    fw = Fw(nc)
    K = setup(fw, dbg=False)
    mods_phase(K)
    s5_setup(K)
    layer0(K)
    layer1(K)
    normmod(K, K.X4, K.out, K.segs_lat, K.fnw, None, None, None, omap=True)
    fw.finish()
    return nc


def kernel(**inputs):
    inp = {k: np.asarray(v) for k, v in inputs.items()}
    nc = build_program()
    shared = host_shared(inp)
    in_maps = []
    for core in range(8):
        d = dict(shared)
        d.update(host_core(inp, core))
        in_maps.append(d)
    res = run_bass_kernel_spmd(nc, in_maps, core_ids=list(range(8)))
    out = np.zeros((16, 2048, 1024), np.float32)
    for core in range(8):
        o = res.results[core]["out"]
        out[2 * core] = o[:2048]
        out[2 * core + 1] = o[2048:]
    return out
```
